# Optimizing a Trainium2 kernel written in Bass

```python
import jax, jax.numpy as jnp
from jax import lax
import numpy as np


D_MODEL = 1024
BATCH = 16
SEQ = 2048
DEPTH = 2

GRID_W = 64
CTX_LEN = 256
EPS = 1e-6
N_REC = (DEPTH + 1) // 2
N_ATT = DEPTH // 2

A_HEADS = 4
A_KEY = 128
A_VAL = D_MODEL // (2 * A_HEADS)
A_QK = A_HEADS * A_KEY
A_V = A_HEADS * A_VAL
B_HEADS = 4
B_VAL = D_MODEL // (2 * B_HEADS)
B_KEY = B_VAL // 2
B_QK = B_HEADS * B_KEY
B_V = B_HEADS * B_VAL
B_GATE_RANK = 16
GLA_GATE_NORM = 16.0
LA_CHUNK = 32
REC_WIDTHS = (A_QK, A_QK, A_QK, A_V, A_V, B_QK, B_QK, B_V, B_GATE_RANK, B_GATE_RANK, B_V)
REC_IN = sum(REC_WIDTHS)
REC_OUT = A_V + B_V

C_HEAD_DIM = 64
C_HEADS = D_MODEL // C_HEAD_DIM
C_KV_HEADS = 4
C_Q = C_HEADS * C_HEAD_DIM
C_KV = C_KV_HEADS * C_HEAD_DIM
WINDOW = 128
ATT_BLOCK = 128
ROPE_BASE = 10000.0

FFN_HIDDEN = -(-(8 * D_MODEL) // (3 * 256)) * 256

kernel_name = 'hybrid_hgrn2_gla_swa_dit_block'


def rms_norm(x, g):
    xf = x.astype(jnp.float32)
    y = xf * lax.rsqrt(jnp.mean(jnp.square(xf), axis=-1, keepdims=True) + EPS)
    return (y * g.astype(jnp.float32)).astype(x.dtype)


def modulate(x, g, shift, scale):
    return rms_norm(x, g) * (1.0 + scale) + shift


def split_heads(a, n_heads):
    b_, l_, _ = a.shape
    return a.reshape(b_, l_, n_heads, -1).transpose(0, 2, 1, 3)


def merge_heads(a):
    b_, h_, l_, e_ = a.shape
    return a.transpose(0, 2, 1, 3).reshape(b_, l_, h_ * e_)


def swiglu(u, w_in, w_out):
    gt, up = jnp.split(u @ w_in, 2, axis=-1)
    return (jax.nn.silu(gt) * up) @ w_out


def hgrn_lower_bounds(lb_logits):
    p = jax.nn.softmax(lb_logits.astype(jnp.float32), axis=0)
    return jnp.cumsum(p, axis=0)[:-1]


def chunk_gla(q, k, v, g, s0):
    b_, h_, t_, _ = q.shape
    dv = v.shape[-1]
    n = t_ // LA_CHUNK

    def chunks(a):
        return a.reshape(b_, h_, n, LA_CHUNK, a.shape[-1]).astype(jnp.float32)

    qc, kc, vc, gc = chunks(q), chunks(k), chunks(v), chunks(g)
    bc = jnp.cumsum(gc, axis=3)
    b_last = bc[:, :, :, -1:, :]
    q_dec = qc * jnp.exp(bc)
    k_inv = kc * jnp.exp(-bc)
    k_end = kc * jnp.exp(b_last - bc)
    scores = jnp.einsum('bhnik,bhnjk->bhnij', q_dec, k_inv)
    causal_in_chunk = jnp.tril(jnp.ones((LA_CHUNK, LA_CHUNK), dtype=bool))
    scores = jnp.where(causal_in_chunk, scores, 0.0)
    o_intra = jnp.einsum('bhnij,bhnjv->bhniv', scores, vc)

    def step(s, xs):
        q_n, k_n, v_n, d_n = xs
        o_n = jnp.einsum('bhck,bhkv->bhcv', q_n, s)
        s = s * d_n[..., None] + jnp.einsum('bhck,bhcv->bhkv', k_n, v_n)
        return s, o_n

    xs = (jnp.moveaxis(q_dec, 2, 0), jnp.moveaxis(k_end, 2, 0), jnp.moveaxis(vc, 2, 0),
          jnp.moveaxis(jnp.exp(b_last[:, :, :, 0, :]), 2, 0))
    s_fin, o_inter = lax.scan(step, s0.astype(jnp.float32), xs)
    o = o_intra + jnp.moveaxis(o_inter, 0, 2)
    return o.reshape(b_, h_, t_, dv), s_fin


def prefix_scan(lat, ctx, reverse):
    if reverse:
        flip = lambda a: jnp.flip(a, axis=2)
    else:
        flip = lambda a: a
    q_c, k_c, v_c, g_c = (flip(a) for a in ctx)
    s0 = jnp.zeros((q_c.shape[0], q_c.shape[1], q_c.shape[3], v_c.shape[3]), jnp.float32)
    o_c, s_c = chunk_gla(q_c, k_c, v_c, g_c, s0)
    q, k, v, g = (flip(a) for a in lat)
    o, _ = chunk_gla(q, k, v, g, s_c)
    return flip(o), flip(o_c)


def rec_features(u, w_in, lb, w_g2, b_g2):
    offsets = np.cumsum(REC_WIDTHS)[:-1].tolist()
    (a_q, a_zf, a_zb, a_i, a_og, b_q, b_k, b_v, b_lf, b_lb, b_r) = jnp.split(u @ w_in, offsets, axis=-1)
    qa = split_heads(jax.nn.silu(a_q) * A_KEY ** -0.5, A_HEADS)
    va = split_heads(a_i, A_HEADS)
    a_dirs = []
    for d, z in enumerate((a_zf, a_zb)):
        f = lb[d] + (1.0 - lb[d]) * jax.nn.sigmoid(z.astype(jnp.float32))
        a_dirs.append((qa, split_heads(1.0 - f, A_HEADS), va, split_heads(jnp.log(f), A_HEADS)))
    qb = split_heads(b_q * B_KEY ** -0.5, B_HEADS)
    kb = split_heads(b_k, B_HEADS)
    vb = split_heads(b_v, B_HEADS)
    b_dirs = []
    for d, lr in enumerate((b_lf, b_lb)):
        gk = jax.nn.log_sigmoid((lr @ w_g2[d] + b_g2[d]).astype(jnp.float32)) / GLA_GATE_NORM
        b_dirs.append((qb, kb, vb, split_heads(gk, B_HEADS)))
    return a_dirs, b_dirs, a_og, b_r


def bidir_group(dirs_lat, dirs_ctx, gn, gate_lat, gate_ctx, need_ctx):
    o_lat, o_ctx = None, None
    for d in range(2):
        ol, oc = prefix_scan(dirs_lat[d], dirs_ctx[d], reverse=(d == 1))
        o_lat = ol if o_lat is None else o_lat + ol
        o_ctx = oc if o_ctx is None else o_ctx + oc
    y_lat = merge_heads(rms_norm(o_lat, gn)).astype(gate_lat.dtype) * jax.nn.silu(gate_lat)
    y_ctx = None
    if need_ctx:
        y_ctx = merge_heads(rms_norm(o_ctx, gn)).astype(gate_ctx.dtype) * jax.nn.silu(gate_ctx)
    return y_lat, y_ctx


def recurrent_mixer(u, u_c, w_in, w_out, lb, w_g2, b_g2, gn_a, gn_b, need_ctx):
    a_lat, b_lat, ag_lat, bg_lat = rec_features(u, w_in, lb, w_g2, b_g2)
    a_ctx, b_ctx, ag_ctx, bg_ctx = rec_features(u_c, w_in, lb, w_g2, b_g2)
    ya, ya_c = bidir_group(a_lat, a_ctx, gn_a, ag_lat, ag_ctx, need_ctx)
    yb, yb_c = bidir_group(b_lat, b_ctx, gn_b, bg_lat, bg_ctx, need_ctx)
    y = jnp.concatenate([ya, yb], axis=-1) @ w_out
    y_c = jnp.concatenate([ya_c, yb_c], axis=-1) @ w_out if need_ctx else None
    return y, y_c


def axial_rope_tables(t_len):
    n_rows = t_len // GRID_W
    row = jnp.repeat(jnp.arange(n_rows), GRID_W).astype(jnp.float32)
    col = jnp.tile(jnp.arange(GRID_W), n_rows).astype(jnp.float32)
    half = C_HEAD_DIM // 2
    inv = ROPE_BASE ** (-jnp.arange(0, half, 2, dtype=jnp.float32) / half)
    ang = jnp.concatenate([row[:, None] * inv, col[:, None] * inv], axis=-1)
    return jnp.cos(ang), jnp.sin(ang)


def apply_rope(x, cos, sin):
    xf = x.astype(jnp.float32).reshape(x.shape[:-1] + (-1, 2))
    x0, x1 = xf[..., 0], xf[..., 1]
    y = jnp.stack([x0 * cos - x1 * sin, x0 * sin + x1 * cos], axis=-1)
    return y.reshape(x.shape).astype(x.dtype)


def windowed_sink_attention(q, k, v, k_c, v_c, sink):
    b_, hq, t_, e_ = q.shape
    g_ = hq // C_KV_HEADS
    nb = t_ // ATT_BLOCK
    lc = k_c.shape[2]
    scale = e_ ** -0.5
    qb = q.reshape(b_, C_KV_HEADS, g_, nb, ATT_BLOCK, e_)

    def band(a):
        ap = jnp.pad(a, ((0, 0), (0, 0), (ATT_BLOCK, ATT_BLOCK), (0, 0)))
        ap = ap.reshape(b_, C_KV_HEADS, nb + 2, ATT_BLOCK, e_)
        return jnp.concatenate([ap[:, :, :-2], ap[:, :, 1:-1], ap[:, :, 2:]], axis=3)

    kb, vb = band(k), band(v)
    qi = jnp.arange(ATT_BLOCK)[:, None]
    kj = jnp.arange(3 * ATT_BLOCK)[None, :] - ATT_BLOCK
    rel_ok = jnp.abs(kj - qi) <= WINDOW
    key_pos = jnp.arange(nb)[:, None] * ATT_BLOCK + kj
    in_range = (key_pos >= 0) & (key_pos < t_)
    mask = rel_ok[None, :, :] & in_range[:, None, :]
    sink_f = sink.astype(jnp.float32).reshape(1, C_KV_HEADS, g_, 1, 1)
    k_cf = k_c.astype(jnp.float32)
    v_cf = v_c.astype(jnp.float32)

    def block_fn(args):
        q_n, k_n, v_n, m_n = args
        q_n = q_n.astype(jnp.float32) * scale
        s_lat = jnp.einsum('bkgie,bkje->bkgij', q_n, k_n.astype(jnp.float32))
        s_lat = jnp.where(m_n, s_lat, -1e30)
        s_ctx = jnp.einsum('bkgie,bkje->bkgij', q_n, k_cf)
        s_snk = jnp.broadcast_to(sink_f, s_lat.shape[:-1] + (1,))
        p = jax.nn.softmax(jnp.concatenate([s_lat, s_ctx, s_snk], axis=-1), axis=-1)
        p_lat = p[..., :3 * ATT_BLOCK]
        p_ctx = p[..., 3 * ATT_BLOCK:3 * ATT_BLOCK + lc]
        return (jnp.einsum('bkgij,bkje->bkgie', p_lat, v_n.astype(jnp.float32))
                + jnp.einsum('bkgij,bkje->bkgie', p_ctx, v_cf))

    out = lax.map(block_fn, (jnp.moveaxis(qb, 3, 0), jnp.moveaxis(kb, 2, 0), jnp.moveaxis(vb, 2, 0), mask))
    out = out.transpose(1, 2, 3, 0, 4, 5).reshape(b_, hq, t_, e_)
    return out.astype(q.dtype)


def context_sink_attention(q, k, v, sink):
    b_, hq, l_, e_ = q.shape
    g_ = hq // C_KV_HEADS
    qg = q.reshape(b_, C_KV_HEADS, g_, l_, e_).astype(jnp.float32) * e_ ** -0.5
    s = jnp.einsum('bkgie,bkje->bkgij', qg, k.astype(jnp.float32))
    s_snk = jnp.broadcast_to(sink.astype(jnp.float32).reshape(1, C_KV_HEADS, g_, 1, 1), s.shape[:-1] + (1,))
    p = jax.nn.softmax(jnp.concatenate([s, s_snk], axis=-1), axis=-1)[..., :-1]
    o = jnp.einsum('bkgij,bkje->bkgie', p, v.astype(jnp.float32))
    return o.reshape(b_, hq, l_, e_).astype(q.dtype)


def attention_mixer(u, u_c, w_qkv, w_o, sink, cos, sin, need_ctx):
    q, k, v = jnp.split(u @ w_qkv, [C_Q, C_Q + C_KV], axis=-1)
    q = apply_rope(split_heads(q, C_HEADS), cos, sin)
    k = apply_rope(split_heads(k, C_KV_HEADS), cos, sin)
    v = split_heads(v, C_KV_HEADS)
    k_c, v_c = jnp.split(u_c @ w_qkv[:, C_Q:], 2, axis=-1)
    k_c = split_heads(k_c, C_KV_HEADS)
    v_c = split_heads(v_c, C_KV_HEADS)
    y = merge_heads(windowed_sink_attention(q, k, v, k_c, v_c, sink)) @ w_o
    y_c = None
    if need_ctx:
        q_c = split_heads(u_c @ w_qkv[:, :C_Q], C_HEADS)
        y_c = merge_heads(context_sink_attention(q_c, k_c, v_c, sink)) @ w_o
    return y, y_c


def setup_inputs(seed: int = 0) -> dict:
    key = jax.random.key(seed)
    ks = jax.random.split(key, 19)

    def nrm(k, shape, fan_in, gain=1.0):
        return jax.random.normal(k, shape, jnp.float32) * (gain * fan_in ** -0.5)

    return {
        'x': jax.random.normal(ks[0], (BATCH, SEQ, D_MODEL), jnp.float32),
        'c': jax.random.normal(ks[1], (BATCH, D_MODEL), jnp.float32),
        'ctx': jax.random.normal(ks[2], (BATCH, CTX_LEN, D_MODEL), jnp.float32),
        'c_ctx': jax.random.normal(ks[3], (D_MODEL,), jnp.float32),
        'ada_w': nrm(ks[4], (DEPTH, D_MODEL, 6 * D_MODEL), D_MODEL, 0.5),
        'ada_b': 0.02 * jax.random.normal(ks[5], (DEPTH, 6 * D_MODEL), jnp.float32),
        'norm_g': 1.0 + 0.05 * jax.random.normal(ks[6], (DEPTH, 4, D_MODEL), jnp.float32),
        'rec_w_in': nrm(ks[7], (N_REC, D_MODEL, REC_IN), D_MODEL),
        'rec_w_out': nrm(ks[8], (N_REC, REC_OUT, D_MODEL), REC_OUT),
        'rec_lb_logits': 0.5 * jax.random.normal(ks[9], (N_REC + 1, 2, A_QK), jnp.float32),
        'rec_w_g2': nrm(ks[10], (N_REC, 2, B_GATE_RANK, B_QK), B_GATE_RANK),
        'rec_b_g2': 0.1 * jax.random.normal(ks[11], (N_REC, 2, B_QK), jnp.float32),
        'rec_gn_a': 1.0 + 0.05 * jax.random.normal(ks[12], (N_REC, A_VAL), jnp.float32),
        'rec_gn_b': 1.0 + 0.05 * jax.random.normal(ks[13], (N_REC, B_VAL), jnp.float32),
        'att_w_qkv': nrm(ks[14], (N_ATT, D_MODEL, C_Q + 2 * C_KV), D_MODEL),
        'att_w_o': nrm(ks[15], (N_ATT, C_Q, D_MODEL), C_Q),
        'att_sink': 0.5 * jax.random.normal(ks[16], (N_ATT, C_HEADS), jnp.float32),
        'ffn_w_in': nrm(ks[17], (DEPTH, D_MODEL, 2 * FFN_HIDDEN), D_MODEL),
        'ffn_w_out': nrm(ks[18], (DEPTH, FFN_HIDDEN, D_MODEL), FFN_HIDDEN),
    }


def reference(x, c, ctx, c_ctx, ada_w, ada_b, norm_g, rec_w_in, rec_w_out, rec_lb_logits,
              rec_w_g2, rec_b_g2, rec_gn_a, rec_gn_b, att_w_qkv, att_w_o, att_sink,
              ffn_w_in, ffn_w_out):
    x_lat, x_ctx = x, ctx
    lbs = hgrn_lower_bounds(rec_lb_logits)
    cos, sin = axial_rope_tables(x.shape[1])
    s_lat = jax.nn.silu(c)
    s_ctx = jax.nn.silu(c_ctx)
    for l in range(DEPTH):
        need_ctx = l < DEPTH - 1
        ng = norm_g[l]
        ml = [m[:, None, :] for m in jnp.split(s_lat @ ada_w[l] + ada_b[l], 6, axis=-1)]
        mc = jnp.split(s_ctx @ ada_w[l] + ada_b[l], 6, axis=-1)
        u_lat = modulate(x_lat, ng[0], ml[0], ml[1])
        u_ctx = modulate(x_ctx, ng[0], mc[0], mc[1])
        j = l // 2
        if l % 2 == 0:
            y_lat, y_ctx = recurrent_mixer(u_lat, u_ctx, rec_w_in[j], rec_w_out[j], lbs[j],
                                           rec_w_g2[j], rec_b_g2[j], rec_gn_a[j], rec_gn_b[j], need_ctx)
        else:
            y_lat, y_ctx = attention_mixer(u_lat, u_ctx, att_w_qkv[j], att_w_o[j], att_sink[j],
                                           cos, sin, need_ctx)
        x_lat = x_lat + ml[2] * rms_norm(y_lat, ng[1])
        h_lat = swiglu(modulate(x_lat, ng[2], ml[3], ml[4]), ffn_w_in[l], ffn_w_out[l])
        x_lat = x_lat + ml[5] * rms_norm(h_lat, ng[3])
        if need_ctx:
            x_ctx = x_ctx + mc[2] * rms_norm(y_ctx, ng[1])
            h_ctx = swiglu(modulate(x_ctx, ng[2], mc[3], mc[4]), ffn_w_in[l], ffn_w_out[l])
            x_ctx = x_ctx + mc[5] * rms_norm(h_ctx, ng[3])
    return x_lat
```

```python
from contextlib import ExitStack
import numpy as np
import concourse.bass as bass
import concourse.mybir as mybir
from concourse.bass_utils import run_bass_kernel_spmd

F32 = mybir.dt.float32
BF16 = mybir.dt.bfloat16
AF = mybir.ActivationFunctionType
ALU = mybir.AluOpType

D = 1024
T = 2304
NT = 18
TL = 2048
FH = 2816
EPS = 1e-6


class Buf:
    __slots__ = ("w", "rs", "excl")

    def __init__(self, excl=False):
        self.w = None
        self.rs = {}
        self.excl = excl


def bufs(n, excl=False):
    return [Buf(excl) for _ in range(n)]


class Chan:
    __slots__ = ("sem", "cnt")

    def __init__(self, sem):
        self.sem = sem
        self.cnt = 0


class Sched:
    ENGS = ("pe", "act", "dve", "pool", "sp")

    def __init__(self, nc, stack):
        self.nc = nc
        self.stack = stack
        self.eng = {"pe": nc.tensor, "act": nc.scalar, "dve": nc.vector, "pool": nc.gpsimd, "sp": nc.sync}
        self.esem = {e: stack.enter_context(nc.semaphore("es_" + e)) for e in self.ENGS}
        self.ecnt = {e: 0 for e in self.ENGS}
        self.seen = {e: {} for e in self.ENGS}
        self.chans = []

    def chan(self):
        c = Chan(self.stack.enter_context(self.nc.semaphore("ch%d" % len(self.chans))))
        self.chans.append(c)
        return c

    def _deps(self, eng, reads, writes, skip_sem=None):
        deps = {}
        for b in reads:
            if b.w is not None:
                s, v = b.w
                if v > deps.get(s, 0):
                    deps[s] = v
        for b in writes:
            if b.w is not None:
                s, v = b.w
                if v > deps.get(s, 0):
                    deps[s] = v
            for s, v in b.rs.items():
                if v > deps.get(s, 0):
                    deps[s] = v
        own = self.esem[eng]
        seen = self.seen[eng]
        e = self.eng[eng]
        for s, v in deps.items():
            if s is skip_sem:
                continue
            if s is own and eng == "pe":
                continue
            if seen.get(s, 0) >= v:
                continue
            seen[s] = v
            e.wait_ge(s, v)

    def _mark(self, tok, reads, writes):
        s, v = tok
        for b in writes:
            b.w = tok
            b.rs = {}
        for b in reads:
            if b.rs.get(s, 0) < v:
                b.rs[s] = v

    def op(self, eng, fn, reads=(), writes=()):
        if any(b.excl for b in reads):
            writes = list(writes) + [b for b in reads if b.excl]
            reads = [b for b in reads if not b.excl]
        self._deps(eng, reads, writes)
        self.ecnt[eng] += 1
        tok = (self.esem[eng], self.ecnt[eng])
        fn(self.eng[eng]).then_inc(self.esem[eng], 1)
        self._mark(tok, reads, writes)

    def dma(self, eng, chan, out, in_, reads=(), writes=()):
        self._deps(eng, reads, writes, skip_sem=chan.sem)
        chan.cnt += 1
        tok = (chan.sem, 16 * chan.cnt)
        self.eng[eng].dma_start(out=out, in_=in_).then_inc(chan.sem, 16)
        self._mark(tok, reads, writes)

    def barrier(self):
        toks = [(self.esem[f], self.ecnt[f]) for f in self.ENGS if self.ecnt[f] > 0]
        toks += [(c.sem, 16 * c.cnt) for c in self.chans if c.cnt > 0]
        for e in self.ENGS:
            seen = self.seen[e]
            for s, v in toks:
                if s is self.esem[e]:
                    continue
                if seen.get(s, 0) >= v:
                    continue
                seen[s] = v
                self.eng[e].wait_ge(s, v)


class StopBuild(Exception):
    pass


def build(NS=2, dbg=None, stop_after=None):
    import os
    CUT = os.environ.get('CUT', '')

    def cut(name):
        if CUT == name:
            raise StopBuild()
    nc = bass.Bass("TRN2", target_bir_lowering=False)

    def din(name, shape, dt=F32):
        return nc.dram_tensor(name, list(shape), dt, kind="ExternalInput").ap()

    x_d = din("x", [NS, TL, D])
    ctx_d = din("ctx", [NS, 256, D])
    cT_d = din("cT", [128, 8, 3])
    adaw_d = din("ada_w", [2, D, 6 * D])
    adab_d = din("ada_bT", [128, 2, 48])
    ng_d = din("ngT", [128, 2, 4, 8])
    rwin_d = din("rec_w_in", [D, 4128])
    rwout_d = din("rec_w_out", [D, D])
    lb_d = din("lbT", [128, 2, 2, 4])
    wg2_d = din("wg2p", [32, 2, 256])
    bg2_d = din("bg2T", [64, 2, 4])
    gn_d = din("gnT", [128, 2])
    wqkv_d = din("att_w_qkv", [D, 1536])
    wqks_d = din("att_w_qk_sw", [D, 1280])
    wo_d = din("att_w_o", [D, D])
    sink_d = din("sinkB", [128, 16])
    cos_d = din("cosT", [64, TL])
    sin_d = din("sinT", [64, TL])
    fwin_d = din("ffn_w_in", [2, D, 2 * FH])
    fwout_d = din("ffn_w_out", [2, FH, D])
    ident_d = din("ident", [128, 128])
    mintra_d = din("mask_intra", [128, 2, 128])
    mexp_d = din("mask_exp", [128, 2, 4, 128])
    smask_d = din("scanmask", [128, 512])
    band_d = din("band", [128, 384])
    out_d = nc.dram_tensor("out", [NS, TL, D], F32, kind="ExternalOutput").ap()
    dbg_d = None
    if dbg:
        dbg_d = nc.dram_tensor("dbg", [128, 8, T], F32, kind="ExternalOutput").ap()

    with ExitStack() as st:
        S = Sched(nc, st)
        AW = 52000
        big = st.enter_context(nc.sbuf_tensor("big", [128, AW], F32))
        pb = [st.enter_context(nc.psum_tensor("pb%d" % i, [128, 512], F32)) for i in range(6)]
        pq = [st.enter_context(nc.psum_tensor("pq%d" % i, [128, 1024], BF16)) for i in range(2)]
        b_pb = bufs(6, True)
        b_pq = bufs(2, True)

        def view(off, shape, dt, parts=128):
            n = 1
            for s_ in shape:
                n *= s_
            esz = 4 if dt is F32 else 2
            nb = n * esz
            assert off % 4 == 0 and nb % 4 == 0
            assert off + nb <= AW * 4, (off, nb)
            a = big[0:parts, off // 4:(off + nb) // 4]
            if dt is BF16:
                a = a.bitcast(BF16)
            if len(shape) == 2:
                a = a.rearrange("p (a b) -> p a b", b=shape[1])
            elif len(shape) == 3:
                a = a.rearrange("p (a b c) -> p a b c", b=shape[1], c=shape[2])
            return a

        class Bump:
            def __init__(self, lo, hi):
                self.lo, self.hi, self.p = lo, hi, lo

            def alloc(self, shape, dt, parts=128):
                n = 1
                for s_ in shape:
                    n *= s_
                nb = ((n * (4 if dt is F32 else 2)) + 3) // 4 * 4
                off = self.p
                self.p += nb
                assert self.p <= self.hi, ("region overflow", self.lo, self.hi, self.p)
                return view(off, shape, dt, parts)

            def reset(self):
                self.p = self.lo

        KB = 1024
        RC = Bump(0, 11 * KB)
        R0 = Bump(11 * KB, 47 * KB)
        R1 = Bump(47 * KB, 119 * KB)
        R2 = Bump(119 * KB, AW * 4)

        def ACT(out, in_, func, r, w, **kw):
            S.op("act", lambda e: e.activation(out=out, in_=in_, func=func, **kw), r, w)

        def TT(eng, out, a, b, op, r, w):
            S.op(eng, lambda e: e.tensor_tensor(out=out, in0=a, in1=b, op=op), r, w)

        def TS(eng, out, a, s1, s2, op0, op1, r, w):
            S.op(eng, lambda e: e.tensor_scalar(out=out, in0=a, scalar1=s1, scalar2=s2, op0=op0, op1=op1), r, w)

        def STT(eng, out, a, s, b, op0, op1, r, w):
            S.op(eng, lambda e: e.scalar_tensor_tensor(out=out, in0=a, scalar=s, in1=b, op0=op0, op1=op1), r, w)

        def CP(eng, out, in_, r, w):
            if eng == "act":
                ACT(out, in_, AF.Copy, r, w)
            else:
                S.op(eng, lambda e: e.tensor_copy(out=out, in_=in_), r, w)

        def MM(out, lhsT, rhs, start, stop, r, w):
            S.op("pe", lambda e: e.matmul(out, lhsT=lhsT, rhs=rhs, start=start, stop=stop), r, w)

        def TR(out, in_, idn, r, w):
            S.op("pe", lambda e: e.transpose(out, in_, idn), r, w)

        def MS(eng, ap, val, w):
            S.op(eng, lambda e: e.memset(ap, val), (), w)

        cch = S.chan()
        ident = RC.alloc([128], F32); b_c = Buf()
        identb = RC.alloc([128], BF16)
        onesb = RC.alloc([128], BF16)
        mintra = RC.alloc([2, 128], F32)
        mexp = RC.alloc([2, 4, 128], BF16)
        mexp32 = R2.alloc([2, 4, 128], F32)
        smask = RC.alloc([512], F32)
        band = RC.alloc([384], BF16)
        band32 = R2.alloc([384], F32)
        epsc = RC.alloc([1], F32)
        ngT = RC.alloc([2, 4, 8], F32)
        adab = RC.alloc([2, 48], F32)
        lbl = RC.alloc([2, 2, 4], F32)
        lbv = RC.alloc([2, 4], F32)
        omlb = RC.alloc([2, 4], F32)
        wg2 = RC.alloc([2, 256], BF16, parts=32)
        wg2f = R2.alloc([2, 256], F32, parts=32)
        bg2 = RC.alloc([2, 4], F32, parts=64)
        gnv = RC.alloc([2], F32)
        sinkB = RC.alloc([16], F32)
        cT = RC.alloc([8, 3], F32)
        sT = RC.alloc([8, 3], F32)
        V = RC.alloc([2 * 3 * 6, 8], F32)
        modT = R2.alloc([2, 48, 3], F32)
        for dst, src in ((ident, ident_d), (mintra, mintra_d), (mexp32, mexp_d), (smask, smask_d), (band32, band_d),
                         (ngT, ng_d), (adab, adab_d), (lbl, lb_d), (gnv, gn_d), (sinkB, sink_d), (cT, cT_d)):
            S.dma("sp", cch, dst, src, (), [b_c])
        S.dma("sp", cch, wg2f, wg2_d, (), [b_c])
        S.dma("sp", cch, bg2, bg2_d, (), [b_c])
        CP("dve", identb, ident, [b_c], [b_c])
        CP("dve", mexp, mexp32, [b_c], [b_c])
        CP("dve", band, band32, [b_c], [b_c])
        CP("dve", wg2, wg2f, [b_c], [b_c])
        MS("dve", onesb, 1.0, [b_c])
        MS("dve", epsc, EPS, [b_c])
        TT("dve", lbv, lbl[:, 0], lbl[:, 1], ALU.subtract, [b_c], [b_c])
        ACT(lbv, lbv, AF.Sigmoid, [b_c], [b_c])
        TS("dve", omlb, lbv, -1.0, 1.0, ALU.mult, ALU.add, [b_c], [b_c])
        ACT(sT, cT, AF.Silu, [b_c], [b_c])

        wch = [S.chan(), S.chan()]
        awb = [R1.alloc([8, 512], F32), R1.alloc([8, 512], F32)]
        b_aw = bufs(2)
        b_mod = Buf()
        it = 0
        for l in range(2):
            awv = adaw_d[l].rearrange("(j p) n -> p j n", p=128)
            for g in range(12):
                sl = it % 2
                S.dma("sp", wch[sl], awb[sl], awv[:, :, g * 512:(g + 1) * 512], (), [b_aw[sl]])
                pbt = pb[it % 2]
                for mm in range(4):
                    for j in range(8):
                        MM(pbt[:, mm * 3:mm * 3 + 3], awb[sl][:, j, mm * 128:(mm + 1) * 128], sT[:, j, :],
                           j == 0, j == 7, [b_aw[sl], b_c], [b_pb[it % 2]])
                for mm in range(4):
                    m = g * 4 + mm
                    TS("dve", modT[:, l, m, :], pbt[:, mm * 3:mm * 3 + 3], adab[:, l, m:m + 1], 0.0, ALU.add, ALU.add,
                       [b_pb[it % 2], b_c], [b_mod])
                it += 1
        def Vv(l, col, kind):
            i = (l * 3 + col) * 6 + kind
            return V[:, i, :]
        for l in range(2):
            for col in range(3):
                def mk(kind):
                    return modT[:, l, kind * 8:(kind + 1) * 8, col]
                STT("dve", Vv(l, col, 0), mk(1), 1.0, ngT[:, l, 0, :], ALU.add, ALU.mult, [b_mod, b_c], [b_c])
                CP("dve", Vv(l, col, 1), mk(0), [b_mod], [b_c])
                TT("dve", Vv(l, col, 2), mk(2), ngT[:, l, 1, :], ALU.mult, [b_mod, b_c], [b_c])
                STT("dve", Vv(l, col, 3), mk(4), 1.0, ngT[:, l, 2, :], ALU.add, ALU.mult, [b_mod, b_c], [b_c])
                CP("dve", Vv(l, col, 4), mk(3), [b_mod], [b_c])
                TT("dve", Vv(l, col, 5), mk(5), ngT[:, l, 3, :], ALU.mult, [b_mod, b_c], [b_c])
        S.barrier()

        xch = [S.chan(), S.chan()]
        och = S.chan()
        dch = S.chan()
        wq = "pool"

        try:
          cut('p0')
          for s in range(NS):
              R0.reset(); R1.reset(); R2.reset()

              def colof(t):
                  return 2 if t < 2 else s

              def xsrc(t):
                  return ctx_d[s, t * 128:(t + 1) * 128, :] if t < 2 else x_d[s, (t - 2) * 128:(t - 1) * 128, :]

              def rstd_from_ss(ps_ap, n, scale, dst, r, w):
                  ACT(dst, ps_ap, AF.Ln, r + [b_c], w, scale=scale, bias=epsc[:, 0:1])
                  ACT(dst, dst, AF.Exp, w, w, scale=-0.5)

              l = 0
              y_st = R0.alloc([8, T], BF16); b_y = bufs(8)
              u_st = R1.alloc([8, T], BF16); b_u = bufs(NT)
              qd = [R1.alloc([T], BF16) for _ in range(2)]; b_qd = [bufs(NT), bufs(NT)]
              ki = [R1.alloc([T], BF16) for _ in range(2)]; b_ki = [bufs(NT), bufs(NT)]
              keT = [R1.alloc([NT, 128], BF16) for _ in range(2)]; b_ke = [bufs(NT), bufs(NT)]
              vT = R1.alloc([T], BF16); b_vT = bufs(NT)
              gate = R1.alloc([T], BF16); b_gate = bufs(NT)
              markR2 = R2.p
              xin = [R2.alloc([1024], F32) for _ in range(2)]; b_xin = bufs(2)
              xt32 = R2.alloc([8, 128], F32); b_xt32 = Buf()
              sqb = R2.alloc([8, 128], BF16); b_sqb = Buf()
              rs_t = R2.alloc([128], F32); b_rs = Buf()
              def load_xT(t, k):
                  sl = k % 2
                  S.dma("sp", xch[sl], xin[sl], xsrc(t), (), [b_xin[sl]])
                  for j in range(8):
                      TR(pb[j // 4][:, (j % 4) * 128:(j % 4 + 1) * 128], xin[sl][:, j * 128:(j + 1) * 128], ident,
                         [b_xin[sl], b_c], [b_pb[j // 4]])

              for t in range(NT):
                  load_xT(t, t)
                  cut('p1a')
                  col = colof(t)
                  for hh in range(2):
                      pv = pb[hh][:, :].rearrange("p (a b) -> p a b", b=128)
                      ACT(sqb[:, hh * 4:(hh + 1) * 4, :], pv, AF.Square, [b_pb[hh]], [b_sqb])
                      CP("dve", xt32[:, hh * 4:(hh + 1) * 4, :], pv, [b_pb[hh]], [b_xt32])
                  cut('p1b')
                  for j in range(8):
                      MM(pb[2][:, 0:128], onesb, sqb[:, j, :], j == 0, j == 7, [b_sqb, b_c], [b_pb[2]])
                  rstd_from_ss(pb[2][:, 0:128], 128, 1.0 / D, rs_t, [b_pb[2]], [b_rs])
                  cut('p1c')
                  TT("dve", xt32, xt32, rs_t.unsqueeze(1).to_broadcast([128, 8, 128]), ALU.mult, [b_xt32, b_rs], [b_xt32])
                  cut('p1d')
                  TT("pool", xt32, xt32, Vv(l, col, 0).unsqueeze(2).to_broadcast([128, 8, 128]), ALU.mult, [b_xt32, b_c], [b_xt32])
                  TT("pool", u_st[:, :, t * 128:(t + 1) * 128], xt32, Vv(l, col, 1).unsqueeze(2).to_broadcast([128, 8, 128]),
                     ALU.add, [b_xt32, b_c], [b_u[t]])

              S.barrier()
              R2.p = markR2
              wb = [R2.alloc([8, 5, 128], BF16) for _ in range(2)]; b_wb = bufs(2)
              wlr = R2.alloc([8, 32], BF16); b_wlr = Buf()
              o_sb = R2.alloc([T], F32); b_o = bufs(NT)
              lrT = R2.alloc([T], BF16, parts=32); b_lr = bufs(NT)
              dtmp = [R2.alloc([16], F32) for _ in range(2)]; b_dt = bufs(2)
              nt_sq = R2.alloc([512], BF16); b_ntsq = Buf()
              nt_r = R2.alloc([512], F32); b_ntr = Buf()
              dcat = [R2.alloc([NT, 5], F32) for _ in range(2)]; b_dc = [bufs(NT), bufs(NT)]
              for d in range(2):
                  MS("pool", dcat[d], 0.0, b_dc[d])
              qt = [R2.alloc([T], BF16) for _ in range(2)]; b_qt = [bufs(NT), bufs(NT)]
              d4 = [R2.alloc([NT], F32) for _ in range(2)]; b_d4 = [bufs(NT), bufs(NT)]
              Dc = [R2.alloc([16], F32) for _ in range(2)]; b_Dc = bufs(2)
              markU = R2.p
              t_qs = R2.alloc([512], F32); b_tqs = Buf()
              t_s = [R2.alloc([512], F32) for _ in range(2)]; b_ts = bufs(2)
              t_g = [R2.alloc([512], F32) for _ in range(2)]; b_tg = bufs(2)
              t_e = [R2.alloc([512], F32) for _ in range(2)]; b_te = bufs(2)
              t_ki = [R2.alloc([512], F32) for _ in range(2)]; b_tki = bufs(2)
              t_ke = [R2.alloc([512], BF16) for _ in range(2)]; b_tke = bufs(2)
              R2.p = markU
              vxm = [R2.alloc([4, 128], BF16) for _ in range(2)]; b_vxm = bufs(2)
              Vx = [[R2.alloc([5, 128], BF16) for _ in range(2)] for _ in range(2)]; b_Vx = [bufs(2), bufs(2)]
              Am = [[R2.alloc([128], BF16) for _ in range(2)] for _ in range(2)]; b_Am = [bufs(2), bufs(2)]
              U32 = [[R2.alloc([4, 128], F32) for _ in range(2)] for _ in range(2)]; b_U32 = [bufs(2), bufs(2)]
              Lb = [[R2.alloc([3, 128], BF16) for _ in range(2)] for _ in range(2)]; b_Lb = [bufs(2), bufs(2)]
              S32 = [R2.alloc([128], F32) for _ in range(2)]; b_S32 = bufs(2)
              Sbf = [[R2.alloc([128], BF16) for _ in range(2)] for _ in range(2)]; b_Sbf = [bufs(2), bufs(2)]

              cut('p1')
              rwv = rwin_d.rearrange("(j p) n -> p j n", p=128)
              wc = [S.chan(), S.chan()]
              wlc = S.chan()
              S.dma(wq, wlc, wlr, rwv[:, :, 3584:3616], (), [b_wlr])

              def load_head_w(hi):
                  sl = hi % 2
                  if hi < 4:
                      for g in range(5):
                          S.dma(wq, wc[sl], wb[sl][:, :, g, :], rwv[:, :, g * 512 + hi * 128: g * 512 + (hi + 1) * 128], (), [b_wb[sl]])
                  else:
                      h = hi - 4
                      S.dma(wq, wc[sl], wb[sl][:, :, 0, 0:64], rwv[:, :, 2560 + h * 64:2560 + (h + 1) * 64], (), [b_wb[sl]])
                      S.dma(wq, wc[sl], wb[sl][:, :, 1, 0:64], rwv[:, :, 2816 + h * 64:2816 + (h + 1) * 64], (), [b_wb[sl]])
                      S.dma(wq, wc[sl], wb[sl][:, :, 3, :], rwv[:, :, 3072 + h * 128:3072 + (h + 1) * 128], (), [b_wb[sl]])
                      S.dma(wq, wc[sl], wb[sl][:, :, 4, :], rwv[:, :, 3616 + h * 128:3616 + (h + 1) * 128], (), [b_wb[sl]])

              blocks = [(i * 512, min(512, T - i * 512)) for i in range(5)]
              order = [list(range(NT)), [1, 0] + list(range(NT - 1, 1, -1))]
              load_head_w(0)
              for hi in range(8):
                  isA = hi < 4
                  h = hi if isA else hi - 4
                  K = 128 if isA else 64
                  sc = 1.0 if isA else 1.0 / 16.0
                  qscale = (128.0 ** -0.5) if isA else (64.0 ** -0.5)
                  sl = hi % 2
                  if hi + 1 < 8:
                      load_head_w(hi + 1)
                  w = wb[sl]
                  for (c0, n) in blocks:
                      ta, tb = c0 // 128, (c0 + n) // 128
                      nch = n // 32
                      tl = list(range(ta, tb))
                      ub = [b_u[t] for t in tl]

                      def proj(g, M, pbi):
                          for j in range(8):
                              MM(pb[pbi][0:M, 0:n], w[:, j, g, 0:M], u_st[:, j, c0:c0 + n], j == 0, j == 7,
                                 ub + [b_wb[sl]], [b_pb[pbi]])
                      if isA:
                          proj(0, 128, 0); proj(1, 128, 1); proj(2, 128, 2); proj(3, 128, 3); proj(4, 128, 4)
                          ACT(t_qs[:, 0:n], pb[0][:, 0:n], AF.Silu, [b_pb[0]], [b_tqs])
                          qsrc = t_qs; qb = [b_tqs]
                          ksrc = []; kb = []
                          ACT(gate[:, c0:c0 + n], pb[4][:, 0:n], AF.Silu, [b_pb[4]], [b_gate[t] for t in tl])
                          for d in range(2):
                              ACT(t_s[d][:, 0:n], pb[1 + d][:, 0:n], AF.Sigmoid, [b_pb[1 + d]], [b_ts[d]])
                              TS("dve", t_s[d][:, 0:n], t_s[d][:, 0:n], omlb[:, d, h:h + 1], lbv[:, d, h:h + 1], ALU.mult, ALU.add,
                                 [b_ts[d], b_c], [b_ts[d]])
                          for d in range(2):
                              ACT(t_g[d][:, 0:n], t_s[d][:, 0:n], AF.Ln, [b_ts[d]], [b_tg[d]])
                              TS("pool", t_s[d][:, 0:n], t_s[d][:, 0:n], -1.0, 1.0, ALU.mult, ALU.add, [b_ts[d]], [b_ts[d]])
                              ksrc.append(t_s[d]); kb.append([b_ts[d]])
                      else:
                          proj(0, 64, 0); proj(1, 64, 1); proj(3, 128, 3); proj(4, 128, 4)
                          if h == 0:
                              for j in range(8):
                                  MM(pb[5][0:32, 0:n], wlr[:, j, :], u_st[:, j, c0:c0 + n], j == 0, j == 7, ub + [b_wlr], [b_pb[5]])
                              CP("act", lrT[:, c0:c0 + n], pb[5][0:32, 0:n], [b_pb[5]], [b_lr[t] for t in tl])
                          qsrc = pb[0]; qb = [b_pb[0]]
                          ksrc = [pb[1], pb[1]]; kb = [[b_pb[1]], [b_pb[1]]]
                          ACT(gate[:, c0:c0 + n], pb[4][:, 0:n], AF.Silu, [b_pb[4]], [b_gate[t] for t in tl])
                          for d in range(2):
                              MM(pb[5][0:64, 0:n], wg2[:, d, h * 64:(h + 1) * 64], lrT[:, c0:c0 + n], True, True,
                                 [b_lr[t] for t in tl] + [b_c], [b_pb[5]])
                              ACT(t_g[d][0:64, 0:n], pb[5][0:64, 0:n], AF.Sigmoid, [b_pb[5], b_c], [b_tg[d]], bias=bg2[:, d, h:h + 1])
                          for d in range(2):
                              ACT(t_g[d][0:64, 0:n], t_g[d][0:64, 0:n], AF.Ln, [b_tg[d]], [b_tg[d]])
                      CP("act", vT[:, c0:c0 + n], pb[3][:, 0:n], [b_pb[3]], [b_vT[t] for t in tl])
                      for d in range(2):
                          g_ = t_g[d][0:K, 0:n]
                          gv = g_.rearrange("p (a b) -> p a b", b=32)
                          S.op("dve", lambda e, g_=g_, d=d: e.tensor_tensor_scan(out=t_e[d][0:K, 0:n], data0=smask[0:K, 0:n], data1=g_,
                                                                            initial=0.0, op0=ALU.mult, op1=ALU.add),
                               [b_tg[d], b_c], [b_te[d]])
                          pv = t_e[d][0:K, 0:n].rearrange("p (a b) -> p a b", b=32)
                          if d == 0:
                              CP("pool", g_, t_e[d][0:K, 0:n], [b_te[d]], [b_tg[d]])
                              tot = gv[:, :, 31:32]
                          else:
                              TT("dve", g_, g_, t_e[d][0:K, 0:n], ALU.subtract, [b_tg[d], b_te[d]], [b_tg[d]])
                              TT("dve", gv, gv, pv[:, :, 31:32].to_broadcast([K, nch, 32]), ALU.add, [b_tg[d], b_te[d]], [b_tg[d]])
                              tot = gv[:, :, 0:1]
                          dd = dtmp[d][0:K, 0:nch]
                          ACT(dd.unsqueeze(2), tot, AF.Exp, [b_tg[d]], [b_dt[d]], scale=sc)
                          ddv = dd.rearrange("p (t c) -> p t c", c=4)
                          if d == 0:
                              CP("pool", dcat[d][0:K, ta:tb, 1:5], ddv, [b_dt[d]], [b_dc[d][t] for t in tl])
                          else:
                              for c in range(4):
                                  CP("pool", dcat[d][0:K, ta:tb, 4 - c], ddv[:, :, c], [b_dt[d]], [b_dc[d][t] for t in tl])
                          Dcv = Dc[d][0:K, 0:nch].rearrange("p (t c) -> p t c", c=4)
                          po_ = [0, 1, 2, 3] if d == 0 else [3, 2, 1, 0]
                          MS("pool", Dcv[:, :, po_[0]], 1.0, [b_Dc[d]])
                          CP("pool", Dcv[:, :, po_[1]], ddv[:, :, po_[0]], [b_dt[d]], [b_Dc[d]])
                          TT("pool", Dcv[:, :, po_[2]], Dcv[:, :, po_[1]], ddv[:, :, po_[1]], ALU.mult, [b_dt[d], b_Dc[d]], [b_Dc[d]])
                          TT("pool", Dcv[:, :, po_[3]], Dcv[:, :, po_[2]], ddv[:, :, po_[2]], ALU.mult, [b_dt[d], b_Dc[d]], [b_Dc[d]])
                          TT("pool", d4[d][0:K, ta:tb], Dcv[:, :, po_[3]], ddv[:, :, po_[3]], ALU.mult, [b_dt[d], b_Dc[d]], [b_d4[d][t] for t in tl])
                          ACT(t_e[d][0:K, 0:n], g_, AF.Exp, [b_tg[d]], [b_te[d]], scale=sc)
                          STT("dve", qd[d][0:K, c0:c0 + n], qsrc[0:K, 0:n], qscale, t_e[d][0:K, 0:n], ALU.mult, ALU.mult,
                              qb + [b_te[d]], [b_qd[d][t] for t in tl])
                          TT("pool", t_ki[d][0:K, 0:n].rearrange("p (a b) -> p a b", b=32), t_e[d][0:K, 0:n].rearrange("p (a b) -> p a b", b=32),
                             Dc[d][0:K, 0:nch].unsqueeze(2).to_broadcast([K, nch, 32]), ALU.mult, [b_te[d], b_Dc[d]], [b_tki[d]])
                          STT("dve", qt[d][0:K, c0:c0 + n], qsrc[0:K, 0:n], qscale, t_ki[d][0:K, 0:n], ALU.mult, ALU.mult,
                              qb + [b_tki[d]], [b_qt[d][t] for t in tl])
                          ACT(t_e[d][0:K, 0:n], g_, AF.Exp, [b_tg[d]], [b_te[d]], scale=-sc)
                          TT("dve", t_ki[d][0:K, 0:n], ksrc[d][0:K, 0:n], t_e[d][0:K, 0:n], ALU.mult, kb[d] + [b_te[d]], [b_tki[d]])
                          CP("pool", ki[d][0:K, c0:c0 + n], t_ki[d][0:K, 0:n], [b_tki[d]], [b_ki[d][t] for t in tl])
                          TT("pool", t_ke[d][0:K, 0:n].rearrange("p (a b) -> p a b", b=32),
                             t_ki[d][0:K, 0:n].rearrange("p (a b) -> p a b", b=32),
                             dd.unsqueeze(2).to_broadcast([K, nch, 32]), ALU.mult,
                             [b_tki[d], b_dt[d]], [b_tke[d]])
                          for ti, t in enumerate(tl):
                              TR(pq[0][:, (d * 4 + ti) * 128:(d * 4 + ti) * 128 + K], t_ke[d][0:K, ti * 128:(ti + 1) * 128], identb[0:K, 0:K],
                                 [b_tke[d], b_c], [b_pq[0]])
                          nt_ = len(tl)
                          CP("act", keT[d][:, ta:tb, 0:K],
                             pq[0][:, d * 512:d * 512 + nt_ * 128].rearrange("p (a b) -> p a b", b=128)[:, :, 0:K],
                             [b_pq[0]], [b_ke[d][t] for t in tl])
                  cut('h%dp1' % hi)
                  S.barrier()
                  for d in range(2):
                      MS("dve", S32[d], 0.0, [b_S32[d]])
                      MS("dve", Sbf[d][0], 0.0, [b_Sbf[d][0]])
                  visited = set()

                  def prep(k, d):
                      t = order[d][k]
                      bf_ = k % 2
                      cs = slice(t * 128, (t + 1) * 128)
                      TT("pool", vxm[d], mexp[:, d], vT[:, cs].unsqueeze(1).to_broadcast([128, 4, 128]), ALU.mult,
                         [b_vT[t], b_c], [b_vxm[d]])
                      for c in range(4):
                          TR(pq[d][:, c * 128:(c + 1) * 128], vxm[d][:, c, :], identb, [b_vxm[d], b_c], [b_pq[d]])
                      TR(pq[d][:, 512:640], vT[:, cs], identb, [b_vT[t], b_c], [b_pq[d]])
                      CP("act", Vx[d][bf_], pq[d][:, 0:640].rearrange("p (a b) -> p a b", b=128), [b_pq[d]], [b_Vx[d][bf_]])
                      MM(pb[2 + d][:, 0:128], ki[d][0:K, cs], qd[d][0:K, cs], True, True,
                         [b_ki[d][t], b_qd[d][t]], [b_pb[2 + d]])
                      TT("dve", Am[d][bf_], pb[2 + d][:, 0:128], mintra[:, d, :], ALU.mult, [b_pb[2 + d], b_c], [b_Am[d][bf_]])
                      MM(pb[d][0:K, :], keT[d][:, t, 0:K], Vx[d][bf_][:, 0:4, :].rearrange("p a b -> p (a b)"), True, True,
                         [b_ke[d][t], b_Vx[d][bf_]], [b_pb[d]])
                      U_ = U32[d][bf_]
                      CP("act", U_[0:K].rearrange("p a b -> p (a b)"), pb[d][0:K, :], [b_pb[d]], [b_U32[d][bf_]])
                      for s_ in (1, 2, 3):
                          STT("dve", U_[0:K, s_, :], U_[0:K, s_ - 1, :], dcat[d][0:K, t, s_ + 1:s_ + 2], U_[0:K, s_, :], ALU.mult, ALU.add,
                              [b_U32[d][bf_], b_dc[d][t]], [b_U32[d][bf_]])
                      CP("act", Lb[d][bf_][0:K].rearrange("p a b -> p (a b)"), U_[0:K, 0:3, :].rearrange("p a b -> p (a b)"),
                         [b_U32[d][bf_]], [b_Lb[d][bf_]])

                  def chain(k, d):
                      t = order[d][k]
                      bf_ = k % 2
                      cur = k % 2
                      cs = slice(t * 128, (t + 1) * 128)
                      STT("dve", S32[d][0:K], S32[d][0:K], d4[d][0:K, t:t + 1], U32[d][bf_][0:K, 3, :], ALU.mult, ALU.add,
                          [b_S32[d], b_d4[d][t], b_U32[d][bf_]], [b_S32[d]])
                      if k + 1 < NT:
                          CP("act", Sbf[d][1 - cur][0:K], S32[d][0:K], [b_S32[d]], [b_Sbf[d][1 - cur]])
                      po = pb[4 + d][:, 0:128]
                      MM(po, Vx[d][bf_][:, 4, :], Am[d][bf_], True, False, [b_Vx[d][bf_], b_Am[d][bf_]], [b_pb[4 + d]])
                      for s_ in (1, 2, 3):
                          c = s_ if d == 0 else 3 - s_
                          MM(po[:, c * 32:(c + 1) * 32], Lb[d][bf_][0:K, s_ - 1, :], qd[d][0:K, t * 128 + c * 32:t * 128 + (c + 1) * 32],
                             False, False, [b_Lb[d][bf_], b_qd[d][t]], [b_pb[4 + d]])
                      MM(po, Sbf[d][cur][0:K], qt[d][0:K, cs], False, True, [b_Sbf[d][cur], b_qt[d][t]], [b_pb[4 + d]])
                      if t not in visited:
                          CP("dve", o_sb[:, cs], po, [b_pb[4 + d]], [b_o[t]])
                          visited.add(t)
                      else:
                          TT("dve", o_sb[:, cs], o_sb[:, cs], po, ALU.add, [b_o[t], b_pb[4 + d]], [b_o[t]])

                  prep(0, 0); prep(0, 1)
                  for k in range(NT):
                      if k + 1 < NT:
                          prep(k + 1, 0); prep(k + 1, 1)
                      chain(k, 0); chain(k, 1)
                  cut('h%dp2' % hi)
                  S.barrier()
                  for (c0, n) in blocks:
                      tl = list(range(c0 // 128, (c0 + n) // 128))
                      ob = [b_o[t] for t in tl]
                      ACT(nt_sq[:, 0:n], o_sb[:, c0:c0 + n], AF.Square, ob, [b_ntsq])
                      MM(pb[4][:, 0:n], onesb, nt_sq[:, 0:n], True, True, [b_ntsq, b_c], [b_pb[4]])
                      rstd_from_ss(pb[4][:, 0:n], n, 1.0 / 128.0, nt_r[:, 0:n], [b_pb[4]], [b_ntr])
                      TT("dve", nt_r[:, 0:n], nt_r[:, 0:n], o_sb[:, c0:c0 + n], ALU.mult, [b_ntr] + ob, [b_ntr])
                      STT("dve", y_st[:, hi, c0:c0 + n], nt_r[:, 0:n], gnv[:, (0 if isA else 1):(1 if isA else 2)], gate[:, c0:c0 + n],
                          ALU.mult, ALU.mult, [b_ntr, b_c] + [b_gate[t] for t in tl], [b_y[hi]])

              cut('heads')
              S.barrier()
              R1.reset(); R2.reset()
              x_fm = R1.alloc([8, T], F32); b_x = bufs(NT)
              wo_sb = R2.alloc([8, 1024], BF16); b_wo = Buf()
              yo32 = R2.alloc([8, 128], F32); b_yo = Buf()
              sq2 = R2.alloc([8, 128], BF16); b_sq2 = Buf()
              rs2 = R2.alloc([128], F32); b_rs2 = Buf()
              xin = [R2.alloc([1024], F32) for _ in range(2)]; b_xin = bufs(2)
              wch2 = S.chan()
              S.dma(wq, wch2, wo_sb, rwout_d.rearrange("(j p) n -> p j n", p=128), (), [b_wo])
              for t in range(NT):
                  col = colof(t)
                  cs = slice(t * 128, (t + 1) * 128)
                  for f in range(8):
                      pbt = pb[2 + f % 2]
                      for j in range(8):
                          MM(pbt[:, 0:128], wo_sb[:, j, f * 128:(f + 1) * 128], y_st[:, j, cs], j == 0, j == 7,
                             [b_wo, b_y[j]], [b_pb[2 + f % 2]])
                      ACT(sq2[:, f, :], pbt[:, 0:128], AF.Square, [b_pb[2 + f % 2]], [b_sq2])
                      CP("dve", yo32[:, f, :], pbt[:, 0:128], [b_pb[2 + f % 2]], [b_yo])
                  for f in range(8):
                      MM(pb[4][:, 0:128], onesb, sq2[:, f, :], f == 0, f == 7, [b_sq2, b_c], [b_pb[4]])
                  rstd_from_ss(pb[4][:, 0:128], 128, 1.0 / D, rs2, [b_pb[4]], [b_rs2])
                  TT("dve", yo32, yo32, rs2.unsqueeze(1).to_broadcast([128, 8, 128]), ALU.mult, [b_yo, b_rs2], [b_yo])
                  TT("pool", yo32, yo32, Vv(l, col, 2).unsqueeze(2).to_broadcast([128, 8, 128]), ALU.mult, [b_yo, b_c], [b_yo])
                  sl = t % 2
                  S.dma("sp", xch[sl], xin[sl], xsrc(t), (), [b_xin[sl]])
                  for j in range(8):
                      TR(pb[j // 4][:, (j % 4) * 128:(j % 4 + 1) * 128], xin[sl][:, j * 128:(j + 1) * 128], ident,
                         [b_xin[sl], b_c], [b_pb[j // 4]])
                  for hh in range(2):
                      TT("dve", x_fm[:, hh * 4:(hh + 1) * 4, cs], yo32[:, hh * 4:(hh + 1) * 4, :],
                         pb[hh][:, :].rearrange("p (a b) -> p a b", b=128), ALU.add, [b_yo, b_pb[hh]], [b_x[t]])
              S.barrier()
              if dbg == "mix0":
                  S.dma("sp", dch, dbg_d, x_fm, [b for b in b_x], ())
                  S.barrier()
                  break

              def xb_of(c0, n):
                  return [b_x[t] for t in range(c0 // 128, (c0 + n + 127) // 128)]

              def prenorm_block(c0, n, vg, vs, dst, dstb, sqt, b_sqt, tmp, b_tmp, rs, b_rsb, pbi):
                  xs = x_fm[:, :, c0:c0 + n]
                  xb = xb_of(c0, n)
                  ACT(sqt[:, :, 0:n], xs, AF.Square, xb, [b_sqt])
                  for j in range(8):
                      MM(pb[pbi][:, 0:n], onesb, sqt[:, j, 0:n], j == 0, j == 7, [b_sqt, b_c], [b_pb[pbi]])
                  rstd_from_ss(pb[pbi][:, 0:n], n, 1.0 / D, rs[:, 0:n], [b_pb[pbi]], [b_rsb])
                  TT("dve", tmp[:, :, 0:n], xs, rs[:, 0:n].unsqueeze(1).to_broadcast([128, 8, n]), ALU.mult, xb + [b_rsb], [b_tmp])
                  TT("pool", tmp[:, :, 0:n], tmp[:, :, 0:n], vg.unsqueeze(2).to_broadcast([128, 8, n]), ALU.mult, [b_tmp, b_c], [b_tmp])
                  TT("pool", dst, tmp[:, :, 0:n], vs.unsqueeze(2).to_broadcast([128, 8, n]), ALU.add, [b_tmp, b_c], dstb)

              def post_block(c0, n, vgate, yo, b_yo_, sq, b_sq_, rs, b_rsb, pbi):
                  xb = xb_of(c0, n)
                  for f in range(8):
                      MM(pb[pbi][:, 0:n], onesb, sq[:, f, 0:n], f == 0, f == 7, [b_sq_, b_c], [b_pb[pbi]])
                  rstd_from_ss(pb[pbi][:, 0:n], n, 1.0 / D, rs[:, 0:n], [b_pb[pbi]], [b_rsb])
                  TT("dve", yo[:, :, 0:n], yo[:, :, 0:n], rs[:, 0:n].unsqueeze(1).to_broadcast([128, 8, n]), ALU.mult, [b_yo_, b_rsb], [b_yo_])
                  TT("pool", yo[:, :, 0:n], yo[:, :, 0:n], vgate.unsqueeze(2).to_broadcast([128, 8, n]), ALU.mult, [b_yo_, b_c], [b_yo_])
                  TT("dve", x_fm[:, :, c0:c0 + n], x_fm[:, :, c0:c0 + n], yo[:, :, 0:n], ALU.add, xb + [b_yo_], xb)

              def ffn(l, groups):
                  R0.reset(); R2.reset()
                  GT = 768
                  u2 = R0.alloc([8, GT], BF16); b_u2 = Buf()
                  yo = R0.alloc([8, GT], F32); b_yo_ = Buf()
                  hh_ = R2.alloc([22, GT], BF16); b_h = bufs(22)
                  sq = R2.alloc([8, GT], BF16); b_sq_ = Buf()
                  wi = [R2.alloc([8, 256], BF16) for _ in range(2)]; b_wi = bufs(2)
                  wo2 = [R2.alloc([22, 128], BF16) for _ in range(2)]; b_wo2 = bufs(2)
                  sg = [R2.alloc([512], F32) for _ in range(2)]; b_sg = bufs(2)
                  rs = R2.alloc([512], F32); b_rsb = Buf()
                  wic = [S.chan(), S.chan()]
                  woc = [S.chan(), S.chan()]
                  fwv = fwin_d[l].rearrange("(j p) n -> p j n", p=128)
                  fov = fwout_d[l].rearrange("(c p) n -> p c n", p=128)
                  for grp in groups:
                      offs = []
                      o_ = 0
                      for (c0, n, col) in grp:
                          offs.append(o_)
                          o_ += n
                      for (c0, n, col), off in zip(grp, offs):
                          prenorm_block(c0, n, Vv(l, col, 3), Vv(l, col, 4), u2[:, :, off:off + n], [b_u2],
                                        sq, b_sq_, yo, b_yo_, rs, b_rsb, 4)
                      cut('f_pre')
                      k = 0
                      for c in range(22):
                          if c == 1:
                              cut('f_h0')
                          sl = c % 2
                          S.dma(wq, wic[sl], wi[sl][:, :, 0:128], fwv[:, :, c * 128:(c + 1) * 128], (), [b_wi[sl]])
                          S.dma(wq, wic[sl], wi[sl][:, :, 128:256], fwv[:, :, FH + c * 128:FH + (c + 1) * 128], (), [b_wi[sl]])
                          for (c0, n, col), off in zip(grp, offs):
                              pa, pu = (0, 1) if k % 2 == 0 else (2, 3)
                              for j in range(8):
                                  MM(pb[pa][:, 0:n], wi[sl][:, j, 0:128], u2[:, j, off:off + n], j == 0, j == 7, [b_wi[sl], b_u2], [b_pb[pa]])
                              for j in range(8):
                                  MM(pb[pu][:, 0:n], wi[sl][:, j, 128:256], u2[:, j, off:off + n], j == 0, j == 7, [b_wi[sl], b_u2], [b_pb[pu]])
                              ACT(sg[k % 2][:, 0:n], pb[pa][:, 0:n], AF.Silu, [b_pb[pa]], [b_sg[k % 2]])
                              TT("dve", hh_[:, c, off:off + n], sg[k % 2][:, 0:n], pb[pu][:, 0:n], ALU.mult, [b_sg[k % 2], b_pb[pu]], [b_h[c]])
                              k += 1
                      cut('f_hid')
                      k = 0
                      for f in range(8):
                          if f == 1:
                              cut('f_o0')
                          sl = f % 2
                          S.dma(wq, woc[sl], wo2[sl], fov[:, :, f * 128:(f + 1) * 128], (), [b_wo2[sl]])
                          for (c0, n, col), off in zip(grp, offs):
                              pi = k % 2
                              for c in range(22):
                                  MM(pb[pi][:, 0:n], wo2[sl][:, c, :], hh_[:, c, off:off + n], c == 0, c == 21, [b_wo2[sl], b_h[c]], [b_pb[pi]])
                              ACT(sq[:, f, off:off + n], pb[pi][:, 0:n], AF.Square, [b_pb[pi]], [b_sq_])
                              CP("dve", yo[:, f, off:off + n], pb[pi][:, 0:n], [b_pb[pi]], [b_yo_])
                              k += 1
                      cut('f_out')
                      for (c0, n, col), off in zip(grp, offs):
                          post_block(c0, n, Vv(l, col, 5), yo[:, :, off:off + n], b_yo_, sq[:, :, off:off + n], b_sq_, rs, b_rsb, 4)
                          cut('f_post')
                  cut('f_all')
                  S.barrier()

              ffn(0, [[(0, 256, 2), (256, 512, s)], [(768, 512, s), (1280, 256, s)], [(1536, 512, s), (2048, 256, s)]])
              if dbg == "ffn0":
                  S.dma("sp", dch, dbg_d, x_fm, [b for b in b_x], ())
                  S.barrier()
                  break

              l = 1
              R0.reset(); R2.reset()
              u_st = R0.alloc([8, T], BF16); b_u1 = Buf()
              y1 = R2.alloc([8, TL], BF16); b_y1 = bufs(8)
              markA = R2.p
              sqt = R2.alloc([8, 512], BF16); b_sqt = Buf()
              tmpn = R2.alloc([8, 512], F32); b_tmpn = Buf()
              rsn = R2.alloc([512], F32); b_rsn = Buf()
              for (c0, n, col) in [(0, 256, 2)] + [(256 + 512 * i, 512, s) for i in range(4)]:
                  prenorm_block(c0, n, Vv(l, col, 0), Vv(l, col, 1), u_st[:, :, c0:c0 + n], [b_u1], sqt, b_sqt, tmpn, b_tmpn, rsn, b_rsn, 4)
              S.barrier()
              R2.p = markA
              cosT = R2.alloc([TL], F32, parts=64); sinT = R2.alloc([TL], F32, parts=64); b_tab = Buf()
              tch = S.chan()
              S.dma("sp", tch, cosT, cos_d, (), [b_tab])
              S.dma("sp", tch, sinT, sin_d, (), [b_tab])
              kT = R2.alloc([T], BF16, parts=64); b_kT = Buf()
              vv = R2.alloc([NT, 128], BF16); b_vv = Buf()
              qT = R2.alloc([TL], BF16, parts=64); b_qT = Buf()
              wk = R2.alloc([8, 2, 64], BF16); b_wk = Buf()
              wv2 = R2.alloc([8, 128], BF16); b_wv2 = Buf()
              wqb = [R2.alloc([8, 2, 64], BF16) for _ in range(2)]; b_wqb = bufs(2)
              ta1 = R2.alloc([512], F32, parts=64); b_ta1 = Buf()
              ta2 = R2.alloc([512], F32, parts=64); b_ta2 = Buf()
              sqa = R2.alloc([512], BF16, parts=64); b_sqa = Buf()
              Pc = [R2.alloc([512], BF16) for _ in range(2)]; b_Pc = bufs(2)
              rden = R2.alloc([512], F32); b_rden = Buf()
              sm = R2.alloc([16], F32); b_sm = Buf()
              wkc = S.chan(); wqc = [S.chan(), S.chan()]
              wqv = wqkv_d.rearrange("(j p) n -> p j n", p=128)
              wsv = wqks_d.rearrange("(j p) n -> p j n", p=128)
              lat_blocks = [(256 + 512 * i, 512) for i in range(4)]

              def load_q_w(hq):
                  sl = hq % 2
                  S.dma(wq, wqc[sl], wqb[sl][:, :, 0, :], wqv[:, :, hq * 64:(hq + 1) * 64], (), [b_wqb[sl]])
                  S.dma(wq, wqc[sl], wqb[sl][:, :, 1, :], wsv[:, :, hq * 64:(hq + 1) * 64], (), [b_wqb[sl]])

              def rope_proj(wt, b_wt, c0, n, dstT, b_dst, scale):
                  for j in range(8):
                      MM(pb[0][0:64, 0:n], wt[:, j, 0, :], u_st[:, j, c0:c0 + n], j == 0, j == 7, [b_wt, b_u1], [b_pb[0]])
                  for j in range(8):
                      MM(pb[1][0:64, 0:n], wt[:, j, 1, :], u_st[:, j, c0:c0 + n], j == 0, j == 7, [b_wt, b_u1], [b_pb[1]])
                  lc = c0 - 256
                  TT("dve", ta1[:, 0:n], pb[0][0:64, 0:n], cosT[:, lc:lc + n], ALU.mult, [b_pb[0], b_tab], [b_ta1])
                  TT("dve", ta2[:, 0:n], pb[1][0:64, 0:n], sinT[:, lc:lc + n], ALU.mult, [b_pb[1], b_tab], [b_ta2])
                  TT("pool", ta1[:, 0:n], ta1[:, 0:n], ta2[:, 0:n], ALU.add, [b_ta1, b_ta2], [b_ta1])
                  ACT(dstT, ta1[:, 0:n], AF.Copy, [b_ta1], [b_dst], scale=scale)

              def sqmax(srcT, b_src, c0, n, acc_col, first):
                  ACT(sqa[:, 0:n], srcT, AF.Square, [b_src], [b_sqa])
                  MM(pb[2][:, 0:n], onesb[0:64, :], sqa[:, 0:n], True, True, [b_sqa, b_c], [b_pb[2]])
                  if first:
                      S.op("dve", lambda e: e.reduce_max(out=sm[:, acc_col:acc_col + 1], in_=pb[2][:, 0:n], axis=mybir.AxisListType.X), [b_pb[2]], [b_sm])
                  else:
                      S.op("dve", lambda e: e.reduce_max(out=sm[:, 2:3], in_=pb[2][:, 0:n], axis=mybir.AxisListType.X), [b_pb[2]], [b_sm])
                      TT("dve", sm[:, acc_col:acc_col + 1], sm[:, acc_col:acc_col + 1], sm[:, 2:3], ALU.max, [b_sm], [b_sm])

              load_q_w(0)
              for g in range(4):
                  S.dma(wq, wkc, wk[:, :, 0, :], wqv[:, :, 1024 + g * 64:1024 + (g + 1) * 64], (), [b_wk])
                  S.dma(wq, wkc, wk[:, :, 1, :], wsv[:, :, 1024 + g * 64:1024 + (g + 1) * 64], (), [b_wk])
                  S.dma(wq, wkc, wv2[:, :, 0:64], wqv[:, :, 1280 + g * 64:1280 + (g + 1) * 64], (), [b_wv2])
                  S.dma(wq, wkc, wv2[:, :, 64:128], wqv[:, :, 1280 + g * 64:1280 + (g + 1) * 64], (), [b_wv2])
                  for j in range(8):
                      MM(pb[0][0:64, 0:256], wk[:, j, 0, :], u_st[:, j, 0:256], j == 0, j == 7, [b_wk, b_u1], [b_pb[0]])
                  CP("act", kT[:, 0:256], pb[0][0:64, 0:256], [b_pb[0]], [b_kT])
                  sqmax(kT[:, 0:256], b_kT, 0, 256, 0, True)
                  for (c0, n) in lat_blocks:
                      rope_proj(wk, b_wk, c0, n, kT[:, c0:c0 + n], b_kT, 1.0)
                      sqmax(kT[:, c0:c0 + n], b_kT, c0, n, 0, False)
                  for t in range(NT):
                      pi = 3 + t % 2
                      for j in range(8):
                          MM(pb[pi][:, 0:128], u_st[:, j, t * 128:(t + 1) * 128], wv2[:, j, :], j == 0, j == 7, [b_wv2, b_u1], [b_pb[pi]])
                      CP("act", vv[:, t, :], pb[pi][:, 0:128], [b_pb[pi]], [b_vv])
                  for hq in range(4 * g, 4 * g + 4):
                      sl = hq % 2
                      if hq + 1 < 16:
                          load_q_w(hq + 1)
                      for bi, (c0, n) in enumerate(lat_blocks):
                          rope_proj(wqb[sl], b_wqb[sl], c0, n, qT[:, c0 - 256:c0 - 256 + n], b_qT, 0.125)
                          sqmax(qT[:, c0 - 256:c0 - 256 + n], b_qT, c0, n, 1, bi == 0)
                      TT("dve", sm[:, 3:4], sm[:, 0:1], sm[:, 1:2], ALU.mult, [b_sm], [b_sm])
                      ACT(sm[:, 3:4], sm[:, 3:4], AF.Ln, [b_sm], [b_sm])
                      ACT(sm[:, 3:4], sm[:, 3:4], AF.Exp, [b_sm], [b_sm], scale=0.5)
                      TT("dve", sm[:, 3:4], sm[:, 3:4], sinkB[:, hq:hq + 1], ALU.max, [b_sm, b_c], [b_sm])
                      TS("dve", sm[:, 4:5], sm[:, 3:4], -1.0, 0.0, ALU.mult, ALU.add, [b_sm], [b_sm])
                      ACT(sm[:, 5:6], sinkB[:, hq:hq + 1], AF.Exp, [b_sm, b_c], [b_sm], bias=sm[:, 4:5])
                      negM = sm[:, 4:5]
                      pk = 0
                      for Q in range(4):
                          n0 = Q * 4
                          qc = slice(Q * 512, (Q + 1) * 512)
                          first = True
                          for cb in range(2):
                              ps_ = pb[pk % 2]; bps = b_pb[pk % 2]; P_ = Pc[pk % 2]; bP = b_Pc[pk % 2]; pk += 1
                              MM(ps_[:, 0:512], kT[:, cb * 128:(cb + 1) * 128], qT[:, qc], True, True, [b_kT, b_qT], [bps])
                              ACT(P_[:, 0:512], ps_[:, 0:512], AF.Exp, [bps, b_sm], [bP], bias=negM)
                              MM(pb[4][:, 0:512], vv[:, cb, :], P_[:, 0:512], first, False, [b_vv, bP], [b_pb[4]])
                              MM(pb[5][:, 0:512], onesb, P_[:, 0:512], first, False, [b_c, bP], [b_pb[5]])
                              first = False
                          jbs = [jb for jb in range(n0 - 1, n0 + 5) if 0 <= jb < 16]
                          for ji, jb in enumerate(jbs):
                              qa = max(jb - 1, n0); qe = min(jb + 1, n0 + 3)
                              w_ = (qe - qa + 1) * 128
                              moff = (qa - (jb - 1)) * 128
                              ps_ = pb[pk % 2]; bps = b_pb[pk % 2]; P_ = Pc[pk % 2]; bP = b_Pc[pk % 2]; pk += 1
                              MM(ps_[:, 0:w_], kT[:, 256 + jb * 128:256 + (jb + 1) * 128], qT[:, qa * 128:(qe + 1) * 128], True, True, [b_kT, b_qT], [bps])
                              ACT(P_[:, 0:w_], ps_[:, 0:w_], AF.Exp, [bps, b_sm], [bP], bias=negM)
                              TT("dve", P_[:, 0:w_], P_[:, 0:w_], band[:, moff:moff + w_], ALU.mult, [bP, b_c], [bP])
                              last = ji == len(jbs) - 1
                              for nb in range(qa, qe + 1):
                                  oc = slice((nb - n0) * 128, (nb - n0 + 1) * 128)
                                  pc_ = slice((nb - qa) * 128, (nb - qa + 1) * 128)
                                  lst = last and nb == qe
                                  MM(pb[4][:, oc], vv[:, 2 + jb, :], P_[:, pc_], False, lst, [b_vv, bP], [b_pb[4]])
                                  MM(pb[5][:, oc], onesb, P_[:, pc_], False, lst, [b_c, bP], [b_pb[5]])
                          ACT(rden[:, 0:512], pb[5][:, 0:512], AF.Ln, [b_pb[5], b_sm], [b_rden], bias=sm[:, 5:6])
                          ACT(rden[:, 0:512], rden[:, 0:512], AF.Exp, [b_rden], [b_rden], scale=-1.0)
                          hp = (hq % 2) * 64
                          TT("dve", y1[hp:hp + 64, hq // 2, qc], pb[4][hp:hp + 64, 0:512], rden[hp:hp + 64, 0:512], ALU.mult,
                             [b_pb[4], b_rden], [b_y1[hq // 2]])
              S.barrier()
              R0.reset(); R2.p = markA
              wo1 = R0.alloc([8, 1024], BF16); b_wo1 = Buf()
              yo1 = R2.alloc([8, 512], F32); b_yo1 = Buf()
              sq1 = R2.alloc([8, 512], BF16); b_sq1 = Buf()
              rs1 = R2.alloc([512], F32); b_rs1 = Buf()
              woc1 = S.chan()
              S.dma(wq, woc1, wo1, wo_d.rearrange("(j p) n -> p j n", p=128), (), [b_wo1])
              k = 0
              for (c0, n) in lat_blocks:
                  lc = c0 - 256
                  for f in range(8):
                      pi = k % 2; k += 1
                      for j in range(8):
                          MM(pb[pi][:, 0:n], wo1[:, j, f * 128:(f + 1) * 128], y1[:, j, lc:lc + n], j == 0, j == 7, [b_wo1, b_y1[j]], [b_pb[pi]])
                      ACT(sq1[:, f, 0:n], pb[pi][:, 0:n], AF.Square, [b_pb[pi]], [b_sq1])
                      CP("dve", yo1[:, f, 0:n], pb[pi][:, 0:n], [b_pb[pi]], [b_yo1])
                  post_block(c0, n, Vv(l, s, 2), yo1, b_yo1, sq1, b_sq1, rs1, b_rs1, 4)
              S.barrier()
              if dbg == "mix1":
                  S.dma("sp", dch, dbg_d, x_fm, [b for b in b_x], ())
                  S.barrier()
                  break
              ffn(1, [[(256, 512, s), (768, 256, s)], [(1024, 512, s), (1536, 256, s)], [(1792, 512, s)]])
              if dbg == "ffn1":
                  S.dma("sp", dch, dbg_d, x_fm, [b for b in b_x], ())
                  S.barrier()
                  break
              R0.reset(); R2.reset()
              ot = [R2.alloc([1024], F32) for _ in range(2)]; b_ot = bufs(2)
              for t in range(2, NT):
                  sl = t % 2
                  for j in range(8):
                      TR(pb[sl * 2 + j // 4][:, (j % 4) * 128:(j % 4 + 1) * 128], x_fm[:, j, t * 128:(t + 1) * 128], ident,
                         [b_x[t], b_c], [b_pb[sl * 2 + j // 4]])
                  CP("act", ot[sl][:, 0:512], pb[sl * 2][:, :], [b_pb[sl * 2]], [b_ot[sl]])
                  CP("dve", ot[sl][:, 512:1024], pb[sl * 2 + 1][:, :], [b_pb[sl * 2 + 1]], [b_ot[sl]])
                  S.dma("sp", och, out_d[s, (t - 2) * 128:(t - 1) * 128, :], ot[sl], [b_ot[sl]], ())
              S.barrier()

        except StopBuild:
            pass
        S.barrier()
        for e in ("sp",):
            for c in S.chans:
                if c.cnt:
                    S.eng[e].wait_ge(c.sem, 16 * c.cnt)
    return nc


def host_prep(inputs, core, NS=2):
    f = np.float32
    b0 = core * NS
    x = np.ascontiguousarray(inputs["x"][b0:b0 + NS]).astype(f)
    ctx = np.ascontiguousarray(inputs["ctx"][b0:b0 + NS]).astype(f)
    c = inputs["c"][b0:b0 + NS]
    cols = [c[0], c[min(1, NS - 1)], inputs["c_ctx"]]
    cT = np.stack([np.asarray(v, f).reshape(8, 128).T for v in cols], axis=-1)
    ada_bT = np.asarray(inputs["ada_b"], f).reshape(2, 48, 128).transpose(2, 0, 1)
    ngT = np.asarray(inputs["norm_g"], f).reshape(2, 4, 8, 128).transpose(3, 0, 1, 2)
    lbT = np.asarray(inputs["rec_lb_logits"], f).reshape(2, 2, 4, 128).transpose(3, 0, 1, 2)
    wg2 = np.asarray(inputs["rec_w_g2"], f)[0]
    wg2p = np.zeros((32, 2, 256), f)
    wg2p[0:16, 0, :] = wg2[0]
    wg2p[16:32, 1, :] = wg2[1]
    bg2T = np.asarray(inputs["rec_b_g2"], f)[0].reshape(2, 4, 64).transpose(2, 0, 1)
    gnT = np.stack([np.asarray(inputs["rec_gn_a"], f)[0], np.asarray(inputs["rec_gn_b"], f)[0]], axis=-1)
    wqkv = np.asarray(inputs["att_w_qkv"], f)[0]
    qk = wqkv[:, :1280].reshape(1024, 640, 2)[:, :, ::-1].reshape(1024, 1280)
    sinkB = np.broadcast_to(np.asarray(inputs["att_sink"], f)[0][None, :], (128, 16))
    n_rows = TL // 64
    row = np.repeat(np.arange(n_rows), 64).astype(f)
    colp = np.tile(np.arange(64), n_rows).astype(f)
    inv = (np.float32(10000.0) ** (-np.arange(0, 32, 2, dtype=f) / np.float32(32))).astype(f)
    ang = np.concatenate([row[:, None] * inv, colp[:, None] * inv], axis=-1).astype(f)
    cosT = np.repeat(np.cos(ang).astype(f).T, 2, axis=0)
    sinv = np.sin(ang).astype(f).T
    sinT = np.empty((64, TL), f)
    sinT[0::2] = -sinv
    sinT[1::2] = sinv
    jj = np.arange(128)[:, None]; ii = np.arange(128)[None, :]
    same = (jj // 32) == (ii // 32)
    mask_intra = np.stack([(same & (jj <= ii)), (same & (jj >= ii))], axis=1).astype(f)
    mask_exp = np.zeros((128, 2, 4, 128), f)
    for cc in range(4):
        mask_exp[:, 0, cc, cc * 32:(cc + 1) * 32] = 1.0
        mask_exp[:, 1, 3 - cc, cc * 32:(cc + 1) * 32] = 1.0
    scanmask = np.ones((128, 512), f); scanmask[:, 0::32] = 0.0
    il = np.arange(384)[None, :]
    band = (np.abs(il - 128 - jj) <= 128).astype(f)
    return {
        "x": x, "ctx": ctx, "cT": np.ascontiguousarray(cT), "ada_w": np.asarray(inputs["ada_w"], f),
        "ada_bT": np.ascontiguousarray(ada_bT), "ngT": np.ascontiguousarray(ngT),
        "rec_w_in": np.asarray(inputs["rec_w_in"], f)[0], "rec_w_out": np.asarray(inputs["rec_w_out"], f)[0],
        "lbT": np.ascontiguousarray(lbT), "wg2p": wg2p, "bg2T": np.ascontiguousarray(bg2T), "gnT": np.ascontiguousarray(gnT),
        "att_w_qkv": wqkv, "att_w_qk_sw": np.ascontiguousarray(qk), "att_w_o": np.asarray(inputs["att_w_o"], f)[0],
        "sinkB": np.ascontiguousarray(sinkB), "cosT": np.ascontiguousarray(cosT), "sinT": sinT,
        "ffn_w_in": np.asarray(inputs["ffn_w_in"], f), "ffn_w_out": np.asarray(inputs["ffn_w_out"], f),
        "ident": np.eye(128, dtype=f), "mask_intra": mask_intra, "mask_exp": mask_exp, "scanmask": scanmask, "band": band,
    }


def kernel(**inputs):
    NS = 2
    nc = build(NS)
    in_maps = [host_prep(inputs, core, NS) for core in range(8)]
    res = run_bass_kernel_spmd(nc, in_maps, core_ids=list(range(8)))
    return np.concatenate([r["out"] for r in res.results], axis=0).astype(np.float32)
```

```python
from contextlib import ExitStack
import numpy as np
import concourse.bass as bass
import concourse.mybir as mybir
from concourse.bass_utils import run_bass_kernel_spmd

F32 = mybir.dt.float32
BF16 = mybir.dt.bfloat16
AF = mybir.ActivationFunctionType
ALU = mybir.AluOpType

D = 1024
T = 2304
NT = 18
TL = 2048
FH = 2816
EPS = 1e-6


class Buf:
    __slots__ = ("w", "rs", "excl")

    def __init__(self, excl=False):
        self.w = None
        self.rs = {}
        self.excl = excl


def bufs(n, excl=False):
    return [Buf(excl) for _ in range(n)]


class Chan:
    __slots__ = ("sem", "cnt")

    def __init__(self, sem):
        self.sem = sem
        self.cnt = 0


class Sched:
    ENGS = ("pe", "act", "dve", "pool", "sp")

    def __init__(self, nc, stack):
        self.nc = nc
        self.stack = stack
        self.eng = {"pe": nc.tensor, "act": nc.scalar, "dve": nc.vector, "pool": nc.gpsimd, "sp": nc.sync}
        self.esem = {e: stack.enter_context(nc.semaphore("es_" + e)) for e in self.ENGS}
        self.ecnt = {e: 0 for e in self.ENGS}
        self.seen = {e: {} for e in self.ENGS}
        self.chans = []

    def chan(self):
        c = Chan(self.stack.enter_context(self.nc.semaphore("ch%d" % len(self.chans))))
        self.chans.append(c)
        return c

    def _deps(self, eng, reads, writes, skip_sem=None):
        deps = {}
        for b in reads:
            if b.w is not None:
                s, v = b.w
                if v > deps.get(s, 0):
                    deps[s] = v
        for b in writes:
            if b.w is not None:
                s, v = b.w
                if v > deps.get(s, 0):
                    deps[s] = v
            for s, v in b.rs.items():
                if v > deps.get(s, 0):
                    deps[s] = v
        own = self.esem[eng]
        seen = self.seen[eng]
        e = self.eng[eng]
        for s, v in deps.items():
            if s is skip_sem:
                continue
            if s is own and eng == "pe":
                continue
            if seen.get(s, 0) >= v:
                continue
            seen[s] = v
            e.wait_ge(s, v)

    def _mark(self, tok, reads, writes):
        s, v = tok
        for b in writes:
            b.w = tok
            b.rs = {}
        for b in reads:
            if b.rs.get(s, 0) < v:
                b.rs[s] = v

    def op(self, eng, fn, reads=(), writes=()):
        if any(b.excl for b in reads):
            writes = list(writes) + [b for b in reads if b.excl]
            reads = [b for b in reads if not b.excl]
        self._deps(eng, reads, writes)
        self.ecnt[eng] += 1
        tok = (self.esem[eng], self.ecnt[eng])
        fn(self.eng[eng]).then_inc(self.esem[eng], 1)
        self._mark(tok, reads, writes)

    def dma(self, eng, chan, out, in_, reads=(), writes=()):
        self._deps(eng, reads, writes, skip_sem=chan.sem)
        chan.cnt += 1
        tok = (chan.sem, 16 * chan.cnt)
        self.eng[eng].dma_start(out=out, in_=in_).then_inc(chan.sem, 16)
        self._mark(tok, reads, writes)

    def barrier(self):
        toks = [(self.esem[f], self.ecnt[f]) for f in self.ENGS if self.ecnt[f] > 0]
        toks += [(c.sem, 16 * c.cnt) for c in self.chans if c.cnt > 0]
        for e in self.ENGS:
            seen = self.seen[e]
            for s, v in toks:
                if s is self.esem[e]:
                    continue
                if seen.get(s, 0) >= v:
                    continue
                seen[s] = v
                self.eng[e].wait_ge(s, v)


class StopBuild(Exception):
    pass


def build(NS=2, dbg=None, stop_after=None):
    import os
    CUT = os.environ.get('CUT', '')

    def cut(name):
        if CUT == name:
            raise StopBuild()
    nc = bass.Bass("TRN2", target_bir_lowering=False)

    def din(name, shape, dt=F32):
        return nc.dram_tensor(name, list(shape), dt, kind="ExternalInput").ap()

    x_d = din("x", [NS, TL, D])
    ctx_d = din("ctx", [NS, 256, D])
    cT_d = din("cT", [128, 8, 3])
    adaw_d = din("ada_w", [2, D, 6 * D])
    adab_d = din("ada_bT", [128, 2, 48])
    ng_d = din("ngT", [128, 2, 4, 8])
    rwin_d = din("rec_w_in", [D, 4128])
    rwout_d = din("rec_w_out", [D, D])
    lb_d = din("lbT", [128, 2, 2, 4])
    wg2_d = din("wg2p", [32, 2, 256])
    bg2_d = din("bg2T", [64, 2, 4])
    gn_d = din("gnT", [128, 2])
    wqkv_d = din("att_w_qkv", [D, 1536])
    wqks_d = din("att_w_qk_sw", [D, 1280])
    wo_d = din("att_w_o", [D, D])
    sink_d = din("sinkB", [128, 16])
    cos_d = din("cosT", [64, TL])
    sin_d = din("sinT", [64, TL])
    fwin_d = din("ffn_w_in", [2, D, 2 * FH])
    fwout_d = din("ffn_w_out", [2, FH, D])
    ident_d = din("ident", [128, 128])
    mintra_d = din("mask_intra", [128, 2, 128])
    mexp_d = din("mask_exp", [128, 2, 4, 128])
    smask_d = din("scanmask", [128, 512])
    band_d = din("band", [128, 384])
    out_d = nc.dram_tensor("out", [NS, TL, D], F32, kind="ExternalOutput").ap()
    dbg_d = None
    if dbg:
        dbg_d = nc.dram_tensor("dbg", [128, 8, T], F32, kind="ExternalOutput").ap()

    with ExitStack() as st:
        S = Sched(nc, st)
        AW = 52000
        big = st.enter_context(nc.sbuf_tensor("big", [128, AW], F32))
        pb = [st.enter_context(nc.psum_tensor("pb%d" % i, [128, 512], F32)) for i in range(6)]
        pq = [st.enter_context(nc.psum_tensor("pq%d" % i, [128, 1024], BF16)) for i in range(2)]
        b_pb = bufs(6, True)
        b_pq = bufs(2, True)

        def view(off, shape, dt, parts=128):
            n = 1
            for s_ in shape:
                n *= s_
            esz = 4 if dt is F32 else 2
            nb = n * esz
            assert off % 4 == 0 and nb % 4 == 0
            assert off + nb <= AW * 4, (off, nb)
            a = big[0:parts, off // 4:(off + nb) // 4]
            if dt is BF16:
                a = a.bitcast(BF16)
            if len(shape) == 2:
                a = a.rearrange("p (a b) -> p a b", b=shape[1])
            elif len(shape) == 3:
                a = a.rearrange("p (a b c) -> p a b c", b=shape[1], c=shape[2])
            return a

        class Bump:
            def __init__(self, lo, hi):
                self.lo, self.hi, self.p = lo, hi, lo

            def alloc(self, shape, dt, parts=128):
                n = 1
                for s_ in shape:
                    n *= s_
                nb = ((n * (4 if dt is F32 else 2)) + 3) // 4 * 4
                off = self.p
                self.p += nb
                assert self.p <= self.hi, ("region overflow", self.lo, self.hi, self.p)
                return view(off, shape, dt, parts)

            def reset(self):
                self.p = self.lo

        KB = 1024
        RC = Bump(0, 11 * KB)
        R0 = Bump(11 * KB, 47 * KB)
        R1 = Bump(47 * KB, 119 * KB)
        R2 = Bump(119 * KB, AW * 4)

        def ACT(out, in_, func, r, w, **kw):
            S.op("act", lambda e: e.activation(out=out, in_=in_, func=func, **kw), r, w)

        def TT(eng, out, a, b, op, r, w):
            S.op(eng, lambda e: e.tensor_tensor(out=out, in0=a, in1=b, op=op), r, w)

        def TS(eng, out, a, s1, s2, op0, op1, r, w):
            S.op(eng, lambda e: e.tensor_scalar(out=out, in0=a, scalar1=s1, scalar2=s2, op0=op0, op1=op1), r, w)

        def STT(eng, out, a, s, b, op0, op1, r, w):
            S.op(eng, lambda e: e.scalar_tensor_tensor(out=out, in0=a, scalar=s, in1=b, op0=op0, op1=op1), r, w)

        def CP(eng, out, in_, r, w):
            if eng == "act":
                ACT(out, in_, AF.Copy, r, w)
            else:
                S.op(eng, lambda e: e.tensor_copy(out=out, in_=in_), r, w)

        def MM(out, lhsT, rhs, start, stop, r, w):
            S.op("pe", lambda e: e.matmul(out, lhsT=lhsT, rhs=rhs, start=start, stop=stop), r, w)

        def TR(out, in_, idn, r, w):
            S.op("pe", lambda e: e.transpose(out, in_, idn), r, w)

        def MS(eng, ap, val, w):
            S.op(eng, lambda e: e.memset(ap, val), (), w)

        cch = S.chan()
        ident = RC.alloc([128], F32); b_c = Buf()
        identb = RC.alloc([128], BF16)
        onesb = RC.alloc([128], BF16)
        mintra = RC.alloc([2, 128], F32)
        mexp = RC.alloc([2, 4, 128], BF16)
        mexp32 = R2.alloc([2, 4, 128], F32)
        smask = RC.alloc([512], F32)
        band = RC.alloc([384], BF16)
        band32 = R2.alloc([384], F32)
        epsc = RC.alloc([1], F32)
        ngT = RC.alloc([2, 4, 8], F32)
        adab = RC.alloc([2, 48], F32)
        lbl = RC.alloc([2, 2, 4], F32)
        lbv = RC.alloc([2, 4], F32)
        omlb = RC.alloc([2, 4], F32)
        wg2 = RC.alloc([2, 256], BF16, parts=32)
        wg2f = R2.alloc([2, 256], F32, parts=32)
        bg2 = RC.alloc([2, 4], F32, parts=64)
        gnv = RC.alloc([2], F32)
        sinkB = RC.alloc([16], F32)
        cT = RC.alloc([8, 3], F32)
        sT = RC.alloc([8, 3], F32)
        V = RC.alloc([2 * 3 * 6, 8], F32)
        modT = R2.alloc([2, 48, 3], F32)
        for dst, src in ((ident, ident_d), (mintra, mintra_d), (mexp32, mexp_d), (smask, smask_d), (band32, band_d),
                         (ngT, ng_d), (adab, adab_d), (lbl, lb_d), (gnv, gn_d), (sinkB, sink_d), (cT, cT_d)):
            S.dma("sp", cch, dst, src, (), [b_c])
        S.dma("sp", cch, wg2f, wg2_d, (), [b_c])
        S.dma("sp", cch, bg2, bg2_d, (), [b_c])
        CP("dve", identb, ident, [b_c], [b_c])
        CP("dve", mexp, mexp32, [b_c], [b_c])
        CP("dve", band, band32, [b_c], [b_c])
        CP("dve", wg2, wg2f, [b_c], [b_c])
        MS("dve", onesb, 1.0, [b_c])
        MS("dve", epsc, EPS, [b_c])
        TT("dve", lbv, lbl[:, 0], lbl[:, 1], ALU.subtract, [b_c], [b_c])
        ACT(lbv, lbv, AF.Sigmoid, [b_c], [b_c])
        TS("dve", omlb, lbv, -1.0, 1.0, ALU.mult, ALU.add, [b_c], [b_c])
        ACT(sT, cT, AF.Silu, [b_c], [b_c])

        wch = [S.chan(), S.chan()]
        awb = [R1.alloc([8, 512], F32), R1.alloc([8, 512], F32)]
        b_aw = bufs(2)
        b_mod = Buf()
        it = 0
        for l in range(2):
            awv = adaw_d[l].rearrange("(j p) n -> p j n", p=128)
            for g in range(12):
                sl = it % 2
                S.dma("sp", wch[sl], awb[sl], awv[:, :, g * 512:(g + 1) * 512], (), [b_aw[sl]])
                pbt = pb[it % 2]
                for mm in range(4):
                    for j in range(8):
                        MM(pbt[:, mm * 3:mm * 3 + 3], awb[sl][:, j, mm * 128:(mm + 1) * 128], sT[:, j, :],
                           j == 0, j == 7, [b_aw[sl], b_c], [b_pb[it % 2]])
                for mm in range(4):
                    m = g * 4 + mm
                    TS("dve", modT[:, l, m, :], pbt[:, mm * 3:mm * 3 + 3], adab[:, l, m:m + 1], 0.0, ALU.add, ALU.add,
                       [b_pb[it % 2], b_c], [b_mod])
                it += 1
        def Vv(l, col, kind):
            i = (l * 3 + col) * 6 + kind
            return V[:, i, :]
        for l in range(2):
            for col in range(3):
                def mk(kind):
                    return modT[:, l, kind * 8:(kind + 1) * 8, col]
                STT("dve", Vv(l, col, 0), mk(1), 1.0, ngT[:, l, 0, :], ALU.add, ALU.mult, [b_mod, b_c], [b_c])
                CP("dve", Vv(l, col, 1), mk(0), [b_mod], [b_c])
                TT("dve", Vv(l, col, 2), mk(2), ngT[:, l, 1, :], ALU.mult, [b_mod, b_c], [b_c])
                STT("dve", Vv(l, col, 3), mk(4), 1.0, ngT[:, l, 2, :], ALU.add, ALU.mult, [b_mod, b_c], [b_c])
                CP("dve", Vv(l, col, 4), mk(3), [b_mod], [b_c])
                TT("dve", Vv(l, col, 5), mk(5), ngT[:, l, 3, :], ALU.mult, [b_mod, b_c], [b_c])
        S.barrier()

        xch = [S.chan(), S.chan()]
        och = [S.chan(), S.chan()]
        dch = S.chan()
        wq = "pool"

        try:
          cut('p0')
          for s in range(NS):
              R0.reset(); R1.reset(); R2.reset()

              def colof(t):
                  return 2 if t < 2 else s

              def xsrc(t):
                  return ctx_d[s, t * 128:(t + 1) * 128, :] if t < 2 else x_d[s, (t - 2) * 128:(t - 1) * 128, :]

              def rstd_from_ss(ps_ap, n, scale, dst, r, w):
                  ACT(dst, ps_ap, AF.Ln, r + [b_c], w, scale=scale, bias=epsc[:, 0:1])
                  ACT(dst, dst, AF.Exp, w, w, scale=-0.5)

              l = 0
              y_st = R0.alloc([8, T], BF16); b_y = bufs(8)
              u_st = R1.alloc([8, T], BF16); b_u = bufs(NT)
              qd = [R1.alloc([T], BF16) for _ in range(2)]; b_qd = [bufs(NT), bufs(NT)]
              ki = [R1.alloc([T], BF16) for _ in range(2)]; b_ki = [bufs(NT), bufs(NT)]
              keT = [R1.alloc([NT, 128], BF16) for _ in range(2)]; b_ke = [bufs(NT), bufs(NT)]
              vT = R1.alloc([T], BF16); b_vT = bufs(NT)
              gate = R1.alloc([T], BF16); b_gate = bufs(NT)
              markR2 = R2.p
              xin = [R2.alloc([1024], F32) for _ in range(2)]; b_xin = bufs(2)
              xt32 = R2.alloc([8, 128], F32); b_xt32 = Buf()
              sqb = R2.alloc([8, 128], BF16); b_sqb = Buf()
              rs_t = R2.alloc([128], F32); b_rs = Buf()
              def load_xT(t, k):
                  sl = k % 2
                  S.dma("sp", xch[sl], xin[sl], xsrc(t), (), [b_xin[sl]])
                  for j in range(8):
                      TR(pb[j // 4][:, (j % 4) * 128:(j % 4 + 1) * 128], xin[sl][:, j * 128:(j + 1) * 128], ident,
                         [b_xin[sl], b_c], [b_pb[j // 4]])

              for t in range(NT):
                  load_xT(t, t)
                  cut('p1a')
                  col = colof(t)
                  for hh in range(2):
                      pv = pb[hh][:, :].rearrange("p (a b) -> p a b", b=128)
                      ACT(sqb[:, hh * 4:(hh + 1) * 4, :], pv, AF.Square, [b_pb[hh]], [b_sqb])
                      CP("dve", xt32[:, hh * 4:(hh + 1) * 4, :], pv, [b_pb[hh]], [b_xt32])
                  cut('p1b')
                  for j in range(8):
                      MM(pb[2][:, 0:128], onesb, sqb[:, j, :], j == 0, j == 7, [b_sqb, b_c], [b_pb[2]])
                  rstd_from_ss(pb[2][:, 0:128], 128, 1.0 / D, rs_t, [b_pb[2]], [b_rs])
                  cut('p1c')
                  TT("dve", xt32, xt32, rs_t.unsqueeze(1).to_broadcast([128, 8, 128]), ALU.mult, [b_xt32, b_rs], [b_xt32])
                  cut('p1d')
                  TT("pool", xt32, xt32, Vv(l, col, 0).unsqueeze(2).to_broadcast([128, 8, 128]), ALU.mult, [b_xt32, b_c], [b_xt32])
                  TT("pool", u_st[:, :, t * 128:(t + 1) * 128], xt32, Vv(l, col, 1).unsqueeze(2).to_broadcast([128, 8, 128]),
                     ALU.add, [b_xt32, b_c], [b_u[t]])

              S.barrier()
              R2.p = markR2
              wb = [R2.alloc([8, 5, 128], BF16) for _ in range(2)]; b_wb = bufs(2)
              wlr = R2.alloc([8, 32], BF16); b_wlr = Buf()
              o_sb = R2.alloc([T], F32); b_o = bufs(NT)
              lrT = R2.alloc([T], BF16, parts=32); b_lr = bufs(NT)
              dtmp = [R2.alloc([16], F32) for _ in range(2)]; b_dt = bufs(2)
              nt_sq = R2.alloc([512], BF16); b_ntsq = Buf()
              nt_r = R2.alloc([512], F32); b_ntr = Buf()
              dcat = [R2.alloc([NT, 5], F32) for _ in range(2)]; b_dc = [bufs(NT), bufs(NT)]
              for d in range(2):
                  MS("pool", dcat[d], 0.0, b_dc[d])
              markU = R2.p
              t_qs = R2.alloc([512], F32); b_tqs = Buf()
              t_s = [R2.alloc([512], F32) for _ in range(2)]; b_ts = bufs(2)
              t_g = [R2.alloc([512], F32) for _ in range(2)]; b_tg = bufs(2)
              t_e = [R2.alloc([512], F32) for _ in range(2)]; b_te = bufs(2)
              t_ki = [R2.alloc([512], F32) for _ in range(2)]; b_tki = bufs(2)
              t_ke = [R2.alloc([512], BF16) for _ in range(2)]; b_tke = bufs(2)
              R2.p = markU
              vxm = [R2.alloc([4, 128], BF16) for _ in range(2)]; b_vxm = bufs(2)
              Vx = [[R2.alloc([5, 128], BF16) for _ in range(2)] for _ in range(2)]; b_Vx = [bufs(2), bufs(2)]
              Am = [[R2.alloc([128], BF16) for _ in range(2)] for _ in range(2)]; b_Am = [bufs(2), bufs(2)]
              Ucat = [[R2.alloc([128, 5], F32) for _ in range(2)] for _ in range(2)]; b_Uc = [bufs(2), bufs(2)]
              Sout = [[R2.alloc([128, 5], F32) for _ in range(2)] for _ in range(2)]; b_So = [bufs(2), bufs(2)]
              Sb4 = [[R2.alloc([4, 128], BF16) for _ in range(2)] for _ in range(2)]; b_Sb4 = [bufs(2), bufs(2)]
              Dexp = [[R2.alloc([128, 5], F32) for _ in range(2)] for _ in range(2)]; b_Dx = [bufs(2), bufs(2)]

              cut('p1')
              rwv = rwin_d.rearrange("(j p) n -> p j n", p=128)
              wc = [S.chan(), S.chan()]
              wlc = S.chan()
              S.dma(wq, wlc, wlr, rwv[:, :, 3584:3616], (), [b_wlr])

              def load_head_w(hi):
                  sl = hi % 2
                  if hi < 4:
                      for g in range(5):
                          S.dma(wq, wc[sl], wb[sl][:, :, g, :], rwv[:, :, g * 512 + hi * 128: g * 512 + (hi + 1) * 128], (), [b_wb[sl]])
                  else:
                      h = hi - 4
                      S.dma(wq, wc[sl], wb[sl][:, :, 0, 0:64], rwv[:, :, 2560 + h * 64:2560 + (h + 1) * 64], (), [b_wb[sl]])
                      S.dma(wq, wc[sl], wb[sl][:, :, 1, 0:64], rwv[:, :, 2816 + h * 64:2816 + (h + 1) * 64], (), [b_wb[sl]])
                      S.dma(wq, wc[sl], wb[sl][:, :, 3, :], rwv[:, :, 3072 + h * 128:3072 + (h + 1) * 128], (), [b_wb[sl]])
                      S.dma(wq, wc[sl], wb[sl][:, :, 4, :], rwv[:, :, 3616 + h * 128:3616 + (h + 1) * 128], (), [b_wb[sl]])

              blocks = [(i * 512, min(512, T - i * 512)) for i in range(5)]
              order = [list(range(NT)), [1, 0] + list(range(NT - 1, 1, -1))]
              load_head_w(0)
              for hi in range(8):
                  isA = hi < 4
                  h = hi if isA else hi - 4
                  K = 128 if isA else 64
                  sc = 1.0 if isA else 1.0 / 16.0
                  qscale = (128.0 ** -0.5) if isA else (64.0 ** -0.5)
                  sl = hi % 2
                  if hi + 1 < 8:
                      load_head_w(hi + 1)
                  w = wb[sl]
                  for (c0, n) in blocks:
                      ta, tb = c0 // 128, (c0 + n) // 128
                      nch = n // 32
                      tl = list(range(ta, tb))
                      ub = [b_u[t] for t in tl]

                      def proj(g, M, pbi):
                          for j in range(8):
                              MM(pb[pbi][0:M, 0:n], w[:, j, g, 0:M], u_st[:, j, c0:c0 + n], j == 0, j == 7,
                                 ub + [b_wb[sl]], [b_pb[pbi]])
                      if isA:
                          proj(0, 128, 0); proj(1, 128, 1); proj(2, 128, 2); proj(3, 128, 3); proj(4, 128, 4)
                          ACT(t_qs[:, 0:n], pb[0][:, 0:n], AF.Silu, [b_pb[0]], [b_tqs])
                          qsrc = t_qs; qb = [b_tqs]
                          ksrc = []; kb = []
                          ACT(gate[:, c0:c0 + n], pb[4][:, 0:n], AF.Silu, [b_pb[4]], [b_gate[t] for t in tl])
                          for d in range(2):
                              ACT(t_s[d][:, 0:n], pb[1 + d][:, 0:n], AF.Sigmoid, [b_pb[1 + d]], [b_ts[d]])
                              TS("dve", t_s[d][:, 0:n], t_s[d][:, 0:n], omlb[:, d, h:h + 1], lbv[:, d, h:h + 1], ALU.mult, ALU.add,
                                 [b_ts[d], b_c], [b_ts[d]])
                          for d in range(2):
                              ACT(t_g[d][:, 0:n], t_s[d][:, 0:n], AF.Ln, [b_ts[d]], [b_tg[d]])
                              TS("pool", t_s[d][:, 0:n], t_s[d][:, 0:n], -1.0, 1.0, ALU.mult, ALU.add, [b_ts[d]], [b_ts[d]])
                              ksrc.append(t_s[d]); kb.append([b_ts[d]])
                      else:
                          proj(0, 64, 0); proj(1, 64, 1); proj(3, 128, 3); proj(4, 128, 4)
                          if h == 0:
                              for j in range(8):
                                  MM(pb[5][0:32, 0:n], wlr[:, j, :], u_st[:, j, c0:c0 + n], j == 0, j == 7, ub + [b_wlr], [b_pb[5]])
                              CP("act", lrT[:, c0:c0 + n], pb[5][0:32, 0:n], [b_pb[5]], [b_lr[t] for t in tl])
                          qsrc = pb[0]; qb = [b_pb[0]]
                          ksrc = [pb[1], pb[1]]; kb = [[b_pb[1]], [b_pb[1]]]
                          ACT(gate[:, c0:c0 + n], pb[4][:, 0:n], AF.Silu, [b_pb[4]], [b_gate[t] for t in tl])
                          for d in range(2):
                              MM(pb[5][0:64, 0:n], wg2[:, d, h * 64:(h + 1) * 64], lrT[:, c0:c0 + n], True, True,
                                 [b_lr[t] for t in tl] + [b_c], [b_pb[5]])
                              ACT(t_g[d][0:64, 0:n], pb[5][0:64, 0:n], AF.Sigmoid, [b_pb[5], b_c], [b_tg[d]], bias=bg2[:, d, h:h + 1])
                          for d in range(2):
                              ACT(t_g[d][0:64, 0:n], t_g[d][0:64, 0:n], AF.Ln, [b_tg[d]], [b_tg[d]])
                      CP("act", vT[:, c0:c0 + n], pb[3][:, 0:n], [b_pb[3]], [b_vT[t] for t in tl])
                      for d in range(2):
                          g_ = t_g[d][0:K, 0:n]
                          gv = g_.rearrange("p (a b) -> p a b", b=32)
                          S.op("dve", lambda e, g_=g_, d=d: e.tensor_tensor_scan(out=t_e[d][0:K, 0:n], data0=smask[0:K, 0:n], data1=g_,
                                                                            initial=0.0, op0=ALU.mult, op1=ALU.add),
                               [b_tg[d], b_c], [b_te[d]])
                          pv = t_e[d][0:K, 0:n].rearrange("p (a b) -> p a b", b=32)
                          if d == 0:
                              CP("pool", g_, t_e[d][0:K, 0:n], [b_te[d]], [b_tg[d]])
                              tot = gv[:, :, 31:32]
                          else:
                              TT("dve", g_, g_, t_e[d][0:K, 0:n], ALU.subtract, [b_tg[d], b_te[d]], [b_tg[d]])
                              TT("dve", gv, gv, pv[:, :, 31:32].to_broadcast([K, nch, 32]), ALU.add, [b_tg[d], b_te[d]], [b_tg[d]])
                              tot = gv[:, :, 0:1]
                          dd = dtmp[d][0:K, 0:nch]
                          ACT(dd.unsqueeze(2), tot, AF.Exp, [b_tg[d]], [b_dt[d]], scale=sc)
                          ddv = dd.rearrange("p (t c) -> p t c", c=4)
                          if d == 0:
                              CP("pool", dcat[d][0:K, ta:tb, 1:5], ddv, [b_dt[d]], [b_dc[d][t] for t in tl])
                          else:
                              for c in range(4):
                                  CP("pool", dcat[d][0:K, ta:tb, 4 - c], ddv[:, :, c], [b_dt[d]], [b_dc[d][t] for t in tl])
                          ACT(t_e[d][0:K, 0:n], g_, AF.Exp, [b_tg[d]], [b_te[d]], scale=sc)
                          STT("dve", qd[d][0:K, c0:c0 + n], qsrc[0:K, 0:n], qscale, t_e[d][0:K, 0:n], ALU.mult, ALU.mult,
                              qb + [b_te[d]], [b_qd[d][t] for t in tl])
                          ACT(t_e[d][0:K, 0:n], g_, AF.Exp, [b_tg[d]], [b_te[d]], scale=-sc)
                          TT("dve", t_ki[d][0:K, 0:n], ksrc[d][0:K, 0:n], t_e[d][0:K, 0:n], ALU.mult, kb[d] + [b_te[d]], [b_tki[d]])
                          CP("pool", ki[d][0:K, c0:c0 + n], t_ki[d][0:K, 0:n], [b_tki[d]], [b_ki[d][t] for t in tl])
                          TT("pool", t_ke[d][0:K, 0:n].rearrange("p (a b) -> p a b", b=32),
                             t_ki[d][0:K, 0:n].rearrange("p (a b) -> p a b", b=32),
                             dd.unsqueeze(2).to_broadcast([K, nch, 32]), ALU.mult,
                             [b_tki[d], b_dt[d]], [b_tke[d]])
                          for ti, t in enumerate(tl):
                              TR(pq[0][:, (d * 4 + ti) * 128:(d * 4 + ti) * 128 + K], t_ke[d][0:K, ti * 128:(ti + 1) * 128], identb[0:K, 0:K],
                                 [b_tke[d], b_c], [b_pq[0]])
                          nt_ = len(tl)
                          CP("act", keT[d][:, ta:tb, 0:K],
                             pq[0][:, d * 512:d * 512 + nt_ * 128].rearrange("p (a b) -> p a b", b=128)[:, :, 0:K],
                             [b_pq[0]], [b_ke[d][t] for t in tl])
                  cut('h%dp1' % hi)
                  S.barrier()
                  for d in range(2):
                      MS("dve", Ucat[d][0][0:K, :, 0:1], 0.0, [b_Uc[d][0]])
                  visited = set()

                  def prep(k, d):
                      t = order[d][k]
                      bf_ = k % 2
                      cs = slice(t * 128, (t + 1) * 128)
                      TT("pool", vxm[d], mexp[:, d], vT[:, cs].unsqueeze(1).to_broadcast([128, 4, 128]), ALU.mult,
                         [b_vT[t], b_c], [b_vxm[d]])
                      for c in range(4):
                          TR(pq[d][:, c * 128:(c + 1) * 128], vxm[d][:, c, :], identb, [b_vxm[d], b_c], [b_pq[d]])
                      TR(pq[d][:, 512:640], vT[:, cs], identb, [b_vT[t], b_c], [b_pq[d]])
                      CP("act", Vx[d][bf_], pq[d][:, 0:640].rearrange("p (a b) -> p a b", b=128), [b_pq[d]], [b_Vx[d][bf_]])
                      MM(pb[2 + d][:, 0:128], ki[d][0:K, cs], qd[d][0:K, cs], True, True,
                         [b_ki[d][t], b_qd[d][t]], [b_pb[2 + d]])
                      TT("dve", Am[d][bf_], pb[2 + d][:, 0:128], mintra[:, d, :], ALU.mult, [b_pb[2 + d], b_c], [b_Am[d][bf_]])
                      MM(pb[d][0:K, :], keT[d][:, t, 0:K], Vx[d][bf_][:, 0:4, :].rearrange("p a b -> p (a b)"), True, True,
                         [b_ke[d][t], b_Vx[d][bf_]], [b_pb[d]])
                      CP("pool", Dexp[d][bf_][0:K], dcat[d][0:K, t, :].unsqueeze(1).to_broadcast([K, 128, 5]), [b_dc[d][t]], [b_Dx[d][bf_]])
                      CP("act", Ucat[d][bf_][0:K, :, 1:5].rearrange("p v s -> p s v"),
                         pb[d][0:K, :].rearrange("p (s v) -> p s v", v=128), [b_pb[d]], [b_Uc[d][bf_]])

                  def chain(k, d):
                      t = order[d][k]
                      bf_ = k % 2
                      cs = slice(t * 128, (t + 1) * 128)
                      S.op("dve", lambda e: e.tensor_tensor_scan(
                          out=Sout[d][bf_][0:K].rearrange("p v s -> p (v s)"), data0=Dexp[d][bf_][0:K].rearrange("p v s -> p (v s)"),
                          data1=Ucat[d][bf_][0:K].rearrange("p v s -> p (v s)"), initial=0.0, op0=ALU.mult, op1=ALU.add),
                          [b_Uc[d][bf_], b_Dx[d][bf_]], [b_So[d][bf_]])
                      if k + 1 < NT:
                          CP("dve", Ucat[d][1 - bf_][0:K, :, 0:1], Sout[d][bf_][0:K, :, 4:5], [b_So[d][bf_]], [b_Uc[d][1 - bf_]])
                      CP("act", Sb4[d][bf_][0:K], Sout[d][bf_][0:K, :, 0:4].rearrange("p v s -> p s v"), [b_So[d][bf_]], [b_Sb4[d][bf_]])
                      po = pb[4 + d][:, 0:128]
                      MM(po, Vx[d][bf_][:, 4, :], Am[d][bf_], True, False, [b_Vx[d][bf_], b_Am[d][bf_]], [b_pb[4 + d]])
                      for cp in range(4):
                          c = cp if d == 0 else 3 - cp
                          MM(po[:, c * 32:(c + 1) * 32], Sb4[d][bf_][0:K, cp, :], qd[d][0:K, t * 128 + c * 32:t * 128 + (c + 1) * 32],
                             False, cp == 3, [b_Sb4[d][bf_], b_qd[d][t]], [b_pb[4 + d]])
                      if t not in visited:
                          CP("dve", o_sb[:, cs], po, [b_pb[4 + d]], [b_o[t]])
                          visited.add(t)
                      else:
                          TT("dve", o_sb[:, cs], o_sb[:, cs], po, ALU.add, [b_o[t], b_pb[4 + d]], [b_o[t]])

                  prep(0, 0); prep(0, 1)
                  for k in range(NT):
                      if k + 1 < NT:
                          prep(k + 1, 0); prep(k + 1, 1)
                      chain(k, 0); chain(k, 1)
                  cut('h%dp2' % hi)
                  S.barrier()
                  for (c0, n) in blocks:
                      tl = list(range(c0 // 128, (c0 + n) // 128))
                      ob = [b_o[t] for t in tl]
                      ACT(nt_sq[:, 0:n], o_sb[:, c0:c0 + n], AF.Square, ob, [b_ntsq])
                      MM(pb[4][:, 0:n], onesb, nt_sq[:, 0:n], True, True, [b_ntsq, b_c], [b_pb[4]])
                      rstd_from_ss(pb[4][:, 0:n], n, 1.0 / 128.0, nt_r[:, 0:n], [b_pb[4]], [b_ntr])
                      TT("dve", nt_r[:, 0:n], nt_r[:, 0:n], o_sb[:, c0:c0 + n], ALU.mult, [b_ntr] + ob, [b_ntr])
                      STT("dve", y_st[:, hi, c0:c0 + n], nt_r[:, 0:n], gnv[:, (0 if isA else 1):(1 if isA else 2)], gate[:, c0:c0 + n],
                          ALU.mult, ALU.mult, [b_ntr, b_c] + [b_gate[t] for t in tl], [b_y[hi]])

              cut('heads')
              S.barrier()
              R1.reset(); R2.reset()
              x_fm = R1.alloc([8, T], F32); b_x = bufs(NT)
              wo_sb = R2.alloc([8, 1024], BF16); b_wo = Buf()
              yo32 = R2.alloc([8, 128], F32); b_yo = Buf()
              sq2 = R2.alloc([8, 128], BF16); b_sq2 = Buf()
              rs2 = R2.alloc([128], F32); b_rs2 = Buf()
              xin = [R2.alloc([1024], F32) for _ in range(2)]; b_xin = bufs(2)
              wch2 = S.chan()
              S.dma(wq, wch2, wo_sb, rwout_d.rearrange("(j p) n -> p j n", p=128), (), [b_wo])
              for t in range(NT):
                  col = colof(t)
                  cs = slice(t * 128, (t + 1) * 128)
                  for f in range(8):
                      pbt = pb[2 + f % 2]
                      for j in range(8):
                          MM(pbt[:, 0:128], wo_sb[:, j, f * 128:(f + 1) * 128], y_st[:, j, cs], j == 0, j == 7,
                             [b_wo, b_y[j]], [b_pb[2 + f % 2]])
                      ACT(sq2[:, f, :], pbt[:, 0:128], AF.Square, [b_pb[2 + f % 2]], [b_sq2])
                      CP("dve", yo32[:, f, :], pbt[:, 0:128], [b_pb[2 + f % 2]], [b_yo])
                  for f in range(8):
                      MM(pb[4][:, 0:128], onesb, sq2[:, f, :], f == 0, f == 7, [b_sq2, b_c], [b_pb[4]])
                  rstd_from_ss(pb[4][:, 0:128], 128, 1.0 / D, rs2, [b_pb[4]], [b_rs2])
                  TT("dve", yo32, yo32, rs2.unsqueeze(1).to_broadcast([128, 8, 128]), ALU.mult, [b_yo, b_rs2], [b_yo])
                  TT("pool", yo32, yo32, Vv(l, col, 2).unsqueeze(2).to_broadcast([128, 8, 128]), ALU.mult, [b_yo, b_c], [b_yo])
                  sl = t % 2
                  S.dma("sp", xch[sl], xin[sl], xsrc(t), (), [b_xin[sl]])
                  for j in range(8):
                      TR(pb[j // 4][:, (j % 4) * 128:(j % 4 + 1) * 128], xin[sl][:, j * 128:(j + 1) * 128], ident,
                         [b_xin[sl], b_c], [b_pb[j // 4]])
                  for hh in range(2):
                      TT("dve", x_fm[:, hh * 4:(hh + 1) * 4, cs], yo32[:, hh * 4:(hh + 1) * 4, :],
                         pb[hh][:, :].rearrange("p (a b) -> p a b", b=128), ALU.add, [b_yo, b_pb[hh]], [b_x[t]])
              S.barrier()
              if dbg == "mix0":
                  S.dma("sp", dch, dbg_d, x_fm, [b for b in b_x], ())
                  S.barrier()
                  break

              def xb_of(c0, n):
                  return [b_x[t] for t in range(c0 // 128, (c0 + n + 127) // 128)]

              def prenorm_block(c0, n, vg, vs, dst, dstb, sqt, b_sqt, tmp, b_tmp, rs, b_rsb, pbi):
                  xs = x_fm[:, :, c0:c0 + n]
                  xb = xb_of(c0, n)
                  ACT(sqt[:, :, 0:n], xs, AF.Square, xb, [b_sqt])
                  for j in range(8):
                      MM(pb[pbi][:, 0:n], onesb, sqt[:, j, 0:n], j == 0, j == 7, [b_sqt, b_c], [b_pb[pbi]])
                  rstd_from_ss(pb[pbi][:, 0:n], n, 1.0 / D, rs[:, 0:n], [b_pb[pbi]], [b_rsb])
                  TT("dve", tmp[:, :, 0:n], xs, rs[:, 0:n].unsqueeze(1).to_broadcast([128, 8, n]), ALU.mult, xb + [b_rsb], [b_tmp])
                  TT("pool", tmp[:, :, 0:n], tmp[:, :, 0:n], vg.unsqueeze(2).to_broadcast([128, 8, n]), ALU.mult, [b_tmp, b_c], [b_tmp])
                  TT("pool", dst, tmp[:, :, 0:n], vs.unsqueeze(2).to_broadcast([128, 8, n]), ALU.add, [b_tmp, b_c], dstb)

              def post_block(c0, n, vgate, yo, b_yo_, sq, b_sq_, rs, b_rsb, pbi):
                  xb = xb_of(c0, n)
                  for f in range(8):
                      MM(pb[pbi][:, 0:n], onesb, sq[:, f, 0:n], f == 0, f == 7, [b_sq_, b_c], [b_pb[pbi]])
                  rstd_from_ss(pb[pbi][:, 0:n], n, 1.0 / D, rs[:, 0:n], [b_pb[pbi]], [b_rsb])
                  TT("dve", yo[:, :, 0:n], yo[:, :, 0:n], rs[:, 0:n].unsqueeze(1).to_broadcast([128, 8, n]), ALU.mult, [b_yo_, b_rsb], [b_yo_])
                  TT("pool", yo[:, :, 0:n], yo[:, :, 0:n], vgate.unsqueeze(2).to_broadcast([128, 8, n]), ALU.mult, [b_yo_, b_c], [b_yo_])
                  TT("dve", x_fm[:, :, c0:c0 + n], x_fm[:, :, c0:c0 + n], yo[:, :, 0:n], ALU.add, xb + [b_yo_], xb)

              def ffn(l, groups):
                  R0.reset(); R2.reset()
                  GT = 768
                  u2 = R0.alloc([8, GT], BF16); b_u2 = Buf()
                  yo = R0.alloc([8, GT], F32); b_yo_ = Buf()
                  hh_ = R2.alloc([22, GT], BF16); b_h = bufs(22)
                  sq = R2.alloc([8, GT], BF16); b_sq_ = Buf()
                  wi = [R2.alloc([8, 256], BF16) for _ in range(2)]; b_wi = bufs(2)
                  wo2 = [R2.alloc([22, 128], BF16) for _ in range(2)]; b_wo2 = bufs(2)
                  sg = [R2.alloc([512], F32) for _ in range(2)]; b_sg = bufs(2)
                  rs = R2.alloc([512], F32); b_rsb = Buf()
                  wic = [S.chan(), S.chan()]
                  woc = [S.chan(), S.chan()]
                  fwv = fwin_d[l].rearrange("(j p) n -> p j n", p=128)
                  fov = fwout_d[l].rearrange("(c p) n -> p c n", p=128)
                  for grp in groups:
                      offs = []
                      o_ = 0
                      for (c0, n, col) in grp:
                          offs.append(o_)
                          o_ += n
                      for (c0, n, col), off in zip(grp, offs):
                          prenorm_block(c0, n, Vv(l, col, 3), Vv(l, col, 4), u2[:, :, off:off + n], [b_u2],
                                        sq, b_sq_, yo, b_yo_, rs, b_rsb, 4)
                      cut('f_pre')
                      k = 0
                      for c in range(22):
                          if c == 1:
                              cut('f_h0')
                          sl = c % 2
                          S.dma(wq, wic[sl], wi[sl][:, :, 0:128], fwv[:, :, c * 128:(c + 1) * 128], (), [b_wi[sl]])
                          S.dma(wq, wic[sl], wi[sl][:, :, 128:256], fwv[:, :, FH + c * 128:FH + (c + 1) * 128], (), [b_wi[sl]])
                          for (c0, n, col), off in zip(grp, offs):
                              pa, pu = (0, 1) if k % 2 == 0 else (2, 3)
                              for j in range(8):
                                  MM(pb[pa][:, 0:n], wi[sl][:, j, 0:128], u2[:, j, off:off + n], j == 0, j == 7, [b_wi[sl], b_u2], [b_pb[pa]])
                              for j in range(8):
                                  MM(pb[pu][:, 0:n], wi[sl][:, j, 128:256], u2[:, j, off:off + n], j == 0, j == 7, [b_wi[sl], b_u2], [b_pb[pu]])
                              ACT(sg[k % 2][:, 0:n], pb[pa][:, 0:n], AF.Silu, [b_pb[pa]], [b_sg[k % 2]])
                              TT("dve", hh_[:, c, off:off + n], sg[k % 2][:, 0:n], pb[pu][:, 0:n], ALU.mult, [b_sg[k % 2], b_pb[pu]], [b_h[c]])
                              k += 1
                      cut('f_hid')
                      k = 0
                      for f in range(8):
                          if f == 1:
                              cut('f_o0')
                          sl = f % 2
                          S.dma(wq, woc[sl], wo2[sl], fov[:, :, f * 128:(f + 1) * 128], (), [b_wo2[sl]])
                          for (c0, n, col), off in zip(grp, offs):
                              pi = k % 2
                              for c in range(22):
                                  MM(pb[pi][:, 0:n], wo2[sl][:, c, :], hh_[:, c, off:off + n], c == 0, c == 21, [b_wo2[sl], b_h[c]], [b_pb[pi]])
                              ACT(sq[:, f, off:off + n], pb[pi][:, 0:n], AF.Square, [b_pb[pi]], [b_sq_])
                              CP("dve", yo[:, f, off:off + n], pb[pi][:, 0:n], [b_pb[pi]], [b_yo_])
                              k += 1
                      cut('f_out')
                      for (c0, n, col), off in zip(grp, offs):
                          post_block(c0, n, Vv(l, col, 5), yo[:, :, off:off + n], b_yo_, sq[:, :, off:off + n], b_sq_, rs, b_rsb, 4)
                          cut('f_post')
                  cut('f_all')
                  S.barrier()

              ffn(0, [[(0, 256, 2), (256, 512, s)], [(768, 512, s), (1280, 256, s)], [(1536, 512, s), (2048, 256, s)]])
              if dbg == "ffn0":
                  S.dma("sp", dch, dbg_d, x_fm, [b for b in b_x], ())
                  S.barrier()
                  break

              l = 1
              R0.reset(); R2.reset()
              u_st = R0.alloc([8, T], BF16); b_u1 = Buf()
              y1 = R2.alloc([8, TL], BF16); b_y1 = bufs(8)
              markA = R2.p
              sqt = R2.alloc([8, 512], BF16); b_sqt = Buf()
              tmpn = R2.alloc([8, 512], F32); b_tmpn = Buf()
              rsn = R2.alloc([512], F32); b_rsn = Buf()
              for (c0, n, col) in [(0, 256, 2)] + [(256 + 512 * i, 512, s) for i in range(4)]:
                  prenorm_block(c0, n, Vv(l, col, 0), Vv(l, col, 1), u_st[:, :, c0:c0 + n], [b_u1], sqt, b_sqt, tmpn, b_tmpn, rsn, b_rsn, 4)
              S.barrier()
              R2.p = markA
              cosT = R2.alloc([TL], F32, parts=64); sinT = R2.alloc([TL], F32, parts=64); b_tab = Buf()
              tch = S.chan()
              S.dma("sp", tch, cosT, cos_d, (), [b_tab])
              S.dma("sp", tch, sinT, sin_d, (), [b_tab])
              kT = R2.alloc([T], BF16, parts=64); b_kT = Buf()
              vv = R2.alloc([NT, 128], BF16); b_vv = Buf()
              qT = R2.alloc([TL], BF16, parts=64); b_qT = Buf()
              wk = R2.alloc([8, 2, 64], BF16); b_wk = Buf()
              wv2 = R2.alloc([8, 128], BF16); b_wv2 = Buf()
              wqb = [R2.alloc([8, 2, 64], BF16) for _ in range(2)]; b_wqb = bufs(2)
              ta1 = R2.alloc([512], F32, parts=64); b_ta1 = Buf()
              ta2 = R2.alloc([512], F32, parts=64); b_ta2 = Buf()
              sqa = R2.alloc([512], BF16, parts=64); b_sqa = Buf()
              Pc = [R2.alloc([512], BF16) for _ in range(2)]; b_Pc = bufs(2)
              rden = R2.alloc([512], F32); b_rden = Buf()
              sm = R2.alloc([16], F32); b_sm = Buf()
              wkc = S.chan(); wvc = S.chan(); wqc = [S.chan(), S.chan()]
              wqv = wqkv_d.rearrange("(j p) n -> p j n", p=128)
              wsv = wqks_d.rearrange("(j p) n -> p j n", p=128)
              lat_blocks = [(256 + 512 * i, 512) for i in range(4)]

              def load_q_w(hq):
                  sl = hq % 2
                  S.dma(wq, wqc[sl], wqb[sl][:, :, 0, :], wqv[:, :, hq * 64:(hq + 1) * 64], (), [b_wqb[sl]])
                  S.dma(wq, wqc[sl], wqb[sl][:, :, 1, :], wsv[:, :, hq * 64:(hq + 1) * 64], (), [b_wqb[sl]])

              def rope_proj(wt, b_wt, c0, n, dstT, b_dst, scale):
                  for j in range(8):
                      MM(pb[0][0:64, 0:n], wt[:, j, 0, :], u_st[:, j, c0:c0 + n], j == 0, j == 7, [b_wt, b_u1], [b_pb[0]])
                  for j in range(8):
                      MM(pb[1][0:64, 0:n], wt[:, j, 1, :], u_st[:, j, c0:c0 + n], j == 0, j == 7, [b_wt, b_u1], [b_pb[1]])
                  lc = c0 - 256
                  TT("dve", ta1[:, 0:n], pb[0][0:64, 0:n], cosT[:, lc:lc + n], ALU.mult, [b_pb[0], b_tab], [b_ta1])
                  TT("dve", ta2[:, 0:n], pb[1][0:64, 0:n], sinT[:, lc:lc + n], ALU.mult, [b_pb[1], b_tab], [b_ta2])
                  TT("pool", ta1[:, 0:n], ta1[:, 0:n], ta2[:, 0:n], ALU.add, [b_ta1, b_ta2], [b_ta1])
                  ACT(dstT, ta1[:, 0:n], AF.Copy, [b_ta1], [b_dst], scale=scale)

              def sqmax(srcT, b_src, c0, n, acc_col, first):
                  ACT(sqa[:, 0:n], srcT, AF.Square, [b_src], [b_sqa])
                  MM(pb[2][:, 0:n], onesb[0:64, :], sqa[:, 0:n], True, True, [b_sqa, b_c], [b_pb[2]])
                  if first:
                      S.op("dve", lambda e: e.reduce_max(out=sm[:, acc_col:acc_col + 1], in_=pb[2][:, 0:n], axis=mybir.AxisListType.X), [b_pb[2]], [b_sm])
                  else:
                      S.op("dve", lambda e: e.reduce_max(out=sm[:, 2:3], in_=pb[2][:, 0:n], axis=mybir.AxisListType.X), [b_pb[2]], [b_sm])
                      TT("dve", sm[:, acc_col:acc_col + 1], sm[:, acc_col:acc_col + 1], sm[:, 2:3], ALU.max, [b_sm], [b_sm])

              load_q_w(0)
              for g in range(4):
                  S.dma(wq, wkc, wk[:, :, 0, :], wqv[:, :, 1024 + g * 64:1024 + (g + 1) * 64], (), [b_wk])
                  S.dma(wq, wkc, wk[:, :, 1, :], wsv[:, :, 1024 + g * 64:1024 + (g + 1) * 64], (), [b_wk])
                  S.dma(wq, wvc, wv2[:, :, 0:64], wqv[:, :, 1280 + g * 64:1280 + (g + 1) * 64], (), [b_wv2])
                  S.dma(wq, wvc, wv2[:, :, 64:128], wqv[:, :, 1280 + g * 64:1280 + (g + 1) * 64], (), [b_wv2])
                  for j in range(8):
                      MM(pb[0][0:64, 0:256], wk[:, j, 0, :], u_st[:, j, 0:256], j == 0, j == 7, [b_wk, b_u1], [b_pb[0]])
                  CP("act", kT[:, 0:256], pb[0][0:64, 0:256], [b_pb[0]], [b_kT])
                  sqmax(kT[:, 0:256], b_kT, 0, 256, 0, True)
                  for (c0, n) in lat_blocks:
                      rope_proj(wk, b_wk, c0, n, kT[:, c0:c0 + n], b_kT, 1.0)
                      sqmax(kT[:, c0:c0 + n], b_kT, c0, n, 0, False)
                  for t in range(NT):
                      pi = 3 + t % 2
                      for j in range(8):
                          MM(pb[pi][:, 0:128], u_st[:, j, t * 128:(t + 1) * 128], wv2[:, j, :], j == 0, j == 7, [b_wv2, b_u1], [b_pb[pi]])
                      CP("act", vv[:, t, :], pb[pi][:, 0:128], [b_pb[pi]], [b_vv])
                  for hq in range(4 * g, 4 * g + 4):
                      sl = hq % 2
                      if hq + 1 < 16:
                          load_q_w(hq + 1)
                      for bi, (c0, n) in enumerate(lat_blocks):
                          rope_proj(wqb[sl], b_wqb[sl], c0, n, qT[:, c0 - 256:c0 - 256 + n], b_qT, 0.125)
                          sqmax(qT[:, c0 - 256:c0 - 256 + n], b_qT, c0, n, 1, bi == 0)
                      TT("dve", sm[:, 3:4], sm[:, 0:1], sm[:, 1:2], ALU.mult, [b_sm], [b_sm])
                      ACT(sm[:, 3:4], sm[:, 3:4], AF.Ln, [b_sm], [b_sm])
                      ACT(sm[:, 3:4], sm[:, 3:4], AF.Exp, [b_sm], [b_sm], scale=0.5)
                      TT("dve", sm[:, 3:4], sm[:, 3:4], sinkB[:, hq:hq + 1], ALU.max, [b_sm, b_c], [b_sm])
                      TS("dve", sm[:, 4:5], sm[:, 3:4], -1.0, 0.0, ALU.mult, ALU.add, [b_sm], [b_sm])
                      ACT(sm[:, 5:6], sinkB[:, hq:hq + 1], AF.Exp, [b_sm, b_c], [b_sm], bias=sm[:, 4:5])
                      negM = sm[:, 4:5]
                      pk = 0
                      for Q in range(4):
                          n0 = Q * 4
                          qc = slice(Q * 512, (Q + 1) * 512)
                          first = True
                          for cb in range(2):
                              ps_ = pb[pk % 2]; bps = b_pb[pk % 2]; P_ = Pc[pk % 2]; bP = b_Pc[pk % 2]; pk += 1
                              MM(ps_[:, 0:512], kT[:, cb * 128:(cb + 1) * 128], qT[:, qc], True, True, [b_kT, b_qT], [bps])
                              ACT(P_[:, 0:512], ps_[:, 0:512], AF.Exp, [bps, b_sm], [bP], bias=negM)
                              MM(pb[4][:, 0:512], vv[:, cb, :], P_[:, 0:512], first, False, [b_vv, bP], [b_pb[4]])
                              MM(pb[5][:, 0:512], onesb, P_[:, 0:512], first, False, [b_c, bP], [b_pb[5]])
                              first = False
                          jbs = [jb for jb in range(n0 - 1, n0 + 5) if 0 <= jb < 16]
                          for ji, jb in enumerate(jbs):
                              qa = max(jb - 1, n0); qe = min(jb + 1, n0 + 3)
                              w_ = (qe - qa + 1) * 128
                              moff = (qa - (jb - 1)) * 128
                              ps_ = pb[pk % 2]; bps = b_pb[pk % 2]; P_ = Pc[pk % 2]; bP = b_Pc[pk % 2]; pk += 1
                              MM(ps_[:, 0:w_], kT[:, 256 + jb * 128:256 + (jb + 1) * 128], qT[:, qa * 128:(qe + 1) * 128], True, True, [b_kT, b_qT], [bps])
                              ACT(P_[:, 0:w_], ps_[:, 0:w_], AF.Exp, [bps, b_sm], [bP], bias=negM)
                              TT("dve", P_[:, 0:w_], P_[:, 0:w_], band[:, moff:moff + w_], ALU.mult, [bP, b_c], [bP])
                              last = ji == len(jbs) - 1
                              for nb in range(qa, qe + 1):
                                  oc = slice((nb - n0) * 128, (nb - n0 + 1) * 128)
                                  pc_ = slice((nb - qa) * 128, (nb - qa + 1) * 128)
                                  lst = last and nb == qe
                                  MM(pb[4][:, oc], vv[:, 2 + jb, :], P_[:, pc_], False, lst, [b_vv, bP], [b_pb[4]])
                                  MM(pb[5][:, oc], onesb, P_[:, pc_], False, lst, [b_c, bP], [b_pb[5]])
                          ACT(rden[:, 0:512], pb[5][:, 0:512], AF.Ln, [b_pb[5], b_sm], [b_rden], bias=sm[:, 5:6])
                          ACT(rden[:, 0:512], rden[:, 0:512], AF.Exp, [b_rden], [b_rden], scale=-1.0)
                          hp = (hq % 2) * 64
                          TT("dve", y1[hp:hp + 64, hq // 2, qc], pb[4][hp:hp + 64, 0:512], rden[hp:hp + 64, 0:512], ALU.mult,
                             [b_pb[4], b_rden], [b_y1[hq // 2]])
              S.barrier()
              R0.reset(); R2.p = markA
              wo1 = R0.alloc([8, 1024], BF16); b_wo1 = Buf()
              yo1 = R2.alloc([8, 512], F32); b_yo1 = Buf()
              sq1 = R2.alloc([8, 512], BF16); b_sq1 = Buf()
              rs1 = R2.alloc([512], F32); b_rs1 = Buf()
              woc1 = S.chan()
              S.dma(wq, woc1, wo1, wo_d.rearrange("(j p) n -> p j n", p=128), (), [b_wo1])
              k = 0
              for (c0, n) in lat_blocks:
                  lc = c0 - 256
                  for f in range(8):
                      pi = k % 2; k += 1
                      for j in range(8):
                          MM(pb[pi][:, 0:n], wo1[:, j, f * 128:(f + 1) * 128], y1[:, j, lc:lc + n], j == 0, j == 7, [b_wo1, b_y1[j]], [b_pb[pi]])
                      ACT(sq1[:, f, 0:n], pb[pi][:, 0:n], AF.Square, [b_pb[pi]], [b_sq1])
                      CP("dve", yo1[:, f, 0:n], pb[pi][:, 0:n], [b_pb[pi]], [b_yo1])
                  post_block(c0, n, Vv(l, s, 2), yo1, b_yo1, sq1, b_sq1, rs1, b_rs1, 4)
              S.barrier()
              if dbg == "mix1":
                  S.dma("sp", dch, dbg_d, x_fm, [b for b in b_x], ())
                  S.barrier()
                  break
              ffn(1, [[(256, 512, s), (768, 256, s)], [(1024, 512, s), (1536, 256, s)], [(1792, 512, s)]])
              if dbg == "ffn1":
                  S.dma("sp", dch, dbg_d, x_fm, [b for b in b_x], ())
                  S.barrier()
                  break
              R0.reset(); R2.reset()
              ot = [R2.alloc([1024], F32) for _ in range(2)]; b_ot = bufs(2)
              for t in range(2, NT):
                  sl = t % 2
                  for j in range(8):
                      TR(pb[sl * 2 + j // 4][:, (j % 4) * 128:(j % 4 + 1) * 128], x_fm[:, j, t * 128:(t + 1) * 128], ident,
                         [b_x[t], b_c], [b_pb[sl * 2 + j // 4]])
                  CP("act", ot[sl][:, 0:512], pb[sl * 2][:, :], [b_pb[sl * 2]], [b_ot[sl]])
                  CP("dve", ot[sl][:, 512:1024], pb[sl * 2 + 1][:, :], [b_pb[sl * 2 + 1]], [b_ot[sl]])
                  S.dma("sp", och[sl], out_d[s, (t - 2) * 128:(t - 1) * 128, :], ot[sl], [b_ot[sl]], ())
              S.barrier()

        except StopBuild:
            pass
        S.barrier()
        for e in ("sp",):
            for c in S.chans:
                if c.cnt:
                    S.eng[e].wait_ge(c.sem, 16 * c.cnt)
    return nc


def host_prep(inputs, core, NS=2):
    f = np.float32
    b0 = core * NS
    x = np.ascontiguousarray(inputs["x"][b0:b0 + NS]).astype(f)
    ctx = np.ascontiguousarray(inputs["ctx"][b0:b0 + NS]).astype(f)
    c = inputs["c"][b0:b0 + NS]
    cols = [c[0], c[min(1, NS - 1)], inputs["c_ctx"]]
    cT = np.stack([np.asarray(v, f).reshape(8, 128).T for v in cols], axis=-1)
    ada_bT = np.asarray(inputs["ada_b"], f).reshape(2, 48, 128).transpose(2, 0, 1)
    ngT = np.asarray(inputs["norm_g"], f).reshape(2, 4, 8, 128).transpose(3, 0, 1, 2)
    lbT = np.asarray(inputs["rec_lb_logits"], f).reshape(2, 2, 4, 128).transpose(3, 0, 1, 2)
    wg2 = np.asarray(inputs["rec_w_g2"], f)[0]
    wg2p = np.zeros((32, 2, 256), f)
    wg2p[0:16, 0, :] = wg2[0]
    wg2p[16:32, 1, :] = wg2[1]
    bg2T = np.asarray(inputs["rec_b_g2"], f)[0].reshape(2, 4, 64).transpose(2, 0, 1)
    gnT = np.stack([np.asarray(inputs["rec_gn_a"], f)[0], np.asarray(inputs["rec_gn_b"], f)[0]], axis=-1)
    wqkv = np.asarray(inputs["att_w_qkv"], f)[0]
    qk = wqkv[:, :1280].reshape(1024, 640, 2)[:, :, ::-1].reshape(1024, 1280)
    sinkB = np.broadcast_to(np.asarray(inputs["att_sink"], f)[0][None, :], (128, 16))
    n_rows = TL // 64
    row = np.repeat(np.arange(n_rows), 64).astype(f)
    colp = np.tile(np.arange(64), n_rows).astype(f)
    inv = (np.float32(10000.0) ** (-np.arange(0, 32, 2, dtype=f) / np.float32(32))).astype(f)
    ang = np.concatenate([row[:, None] * inv, colp[:, None] * inv], axis=-1).astype(f)
    cosT = np.repeat(np.cos(ang).astype(f).T, 2, axis=0)
    sinv = np.sin(ang).astype(f).T
    sinT = np.empty((64, TL), f)
    sinT[0::2] = -sinv
    sinT[1::2] = sinv
    jj = np.arange(128)[:, None]; ii = np.arange(128)[None, :]
    same = (jj // 32) == (ii // 32)
    mask_intra = np.stack([(same & (jj <= ii)), (same & (jj >= ii))], axis=1).astype(f)
    mask_exp = np.zeros((128, 2, 4, 128), f)
    for cc in range(4):
        mask_exp[:, 0, cc, cc * 32:(cc + 1) * 32] = 1.0
        mask_exp[:, 1, 3 - cc, cc * 32:(cc + 1) * 32] = 1.0
    scanmask = np.ones((128, 512), f); scanmask[:, 0::32] = 0.0
    il = np.arange(384)[None, :]
    band = (np.abs(il - 128 - jj) <= 128).astype(f)
    return {
        "x": x, "ctx": ctx, "cT": np.ascontiguousarray(cT), "ada_w": np.asarray(inputs["ada_w"], f),
        "ada_bT": np.ascontiguousarray(ada_bT), "ngT": np.ascontiguousarray(ngT),
        "rec_w_in": np.asarray(inputs["rec_w_in"], f)[0], "rec_w_out": np.asarray(inputs["rec_w_out"], f)[0],
        "lbT": np.ascontiguousarray(lbT), "wg2p": wg2p, "bg2T": np.ascontiguousarray(bg2T), "gnT": np.ascontiguousarray(gnT),
        "att_w_qkv": wqkv, "att_w_qk_sw": np.ascontiguousarray(qk), "att_w_o": np.asarray(inputs["att_w_o"], f)[0],
        "sinkB": np.ascontiguousarray(sinkB), "cosT": np.ascontiguousarray(cosT), "sinT": sinT,
        "ffn_w_in": np.asarray(inputs["ffn_w_in"], f), "ffn_w_out": np.asarray(inputs["ffn_w_out"], f),
        "ident": np.eye(128, dtype=f), "mask_intra": mask_intra, "mask_exp": mask_exp, "scanmask": scanmask, "band": band,
    }


def kernel(**inputs):
    NS = 2
    nc = build(NS)
    in_maps = [host_prep(inputs, core, NS) for core in range(8)]
    res = run_bass_kernel_spmd(nc, in_maps, core_ids=list(range(8)))
    return np.concatenate([r["out"] for r in res.results], axis=0).astype(np.float32)
```

```python
from contextlib import ExitStack
import numpy as np
import concourse.bass as bass
import concourse.mybir as mybir
from concourse.bass_utils import run_bass_kernel_spmd

F32 = mybir.dt.float32
BF16 = mybir.dt.bfloat16
AF = mybir.ActivationFunctionType
ALU = mybir.AluOpType

D = 1024
T = 2304
NT = 18
TL = 2048
FH = 2816
EPS = 1e-6


class Buf:
    __slots__ = ("w", "rs", "excl")

    def __init__(self, excl=False):
        self.w = None
        self.rs = {}
        self.excl = excl


def bufs(n, excl=False):
    return [Buf(excl) for _ in range(n)]


class Chan:
    __slots__ = ("sem", "cnt")

    def __init__(self, sem):
        self.sem = sem
        self.cnt = 0


class Sched:
    ENGS = ("pe", "act", "dve", "pool", "sp")

    def __init__(self, nc, stack):
        self.nc = nc
        self.stack = stack
        self.eng = {"pe": nc.tensor, "act": nc.scalar, "dve": nc.vector, "pool": nc.gpsimd, "sp": nc.sync}
        self.esem = {e: stack.enter_context(nc.semaphore("es_" + e)) for e in self.ENGS}
        self.ecnt = {e: 0 for e in self.ENGS}
        self.seen = {e: {} for e in self.ENGS}
        self.chans = []

    def chan(self):
        c = Chan(self.stack.enter_context(self.nc.semaphore("ch%d" % len(self.chans))))
        self.chans.append(c)
        return c

    def _deps(self, eng, reads, writes, skip_sem=None):
        deps = {}
        for b in reads:
            if b.w is not None:
                s, v = b.w
                if v > deps.get(s, 0):
                    deps[s] = v
        for b in writes:
            if b.w is not None:
                s, v = b.w
                if v > deps.get(s, 0):
                    deps[s] = v
            for s, v in b.rs.items():
                if v > deps.get(s, 0):
                    deps[s] = v
        own = self.esem[eng]
        seen = self.seen[eng]
        e = self.eng[eng]
        for s, v in deps.items():
            if s is skip_sem:
                continue
            if s is own and eng == "pe":
                continue
            if seen.get(s, 0) >= v:
                continue
            seen[s] = v
            e.wait_ge(s, v)

    def _mark(self, tok, reads, writes):
        s, v = tok
        for b in writes:
            b.w = tok
            b.rs = {}
        for b in reads:
            if b.rs.get(s, 0) < v:
                b.rs[s] = v

    def op(self, eng, fn, reads=(), writes=()):
        if any(b.excl for b in reads):
            writes = list(writes) + [b for b in reads if b.excl]
            reads = [b for b in reads if not b.excl]
        self._deps(eng, reads, writes)
        self.ecnt[eng] += 1
        tok = (self.esem[eng], self.ecnt[eng])
        fn(self.eng[eng]).then_inc(self.esem[eng], 1)
        self._mark(tok, reads, writes)

    def dma(self, eng, chan, out, in_, reads=(), writes=()):
        self._deps(eng, reads, writes, skip_sem=chan.sem)
        chan.cnt += 1
        tok = (chan.sem, 16 * chan.cnt)
        self.eng[eng].dma_start(out=out, in_=in_).then_inc(chan.sem, 16)
        self._mark(tok, reads, writes)

    def barrier(self):
        toks = [(self.esem[f], self.ecnt[f]) for f in self.ENGS if self.ecnt[f] > 0]
        toks += [(c.sem, 16 * c.cnt) for c in self.chans if c.cnt > 0]
        for e in self.ENGS:
            seen = self.seen[e]
            for s, v in toks:
                if s is self.esem[e]:
                    continue
                if seen.get(s, 0) >= v:
                    continue
                seen[s] = v
                self.eng[e].wait_ge(s, v)


class StopBuild(Exception):
    pass


def build(NS=2, dbg=None, stop_after=None):
    import os
    CUT = os.environ.get('CUT', '')

    def cut(name):
        if CUT == name:
            raise StopBuild()
    nc = bass.Bass("TRN2", target_bir_lowering=False)

    def din(name, shape, dt=F32):
        return nc.dram_tensor(name, list(shape), dt, kind="ExternalInput").ap()

    x_d = din("x", [NS, TL, D])
    ctx_d = din("ctx", [NS, 256, D])
    cT_d = din("cT", [128, 8, 3])
    adaw_d = din("ada_w", [2, D, 6 * D])
    adab_d = din("ada_bT", [128, 2, 48])
    ng_d = din("ngT", [128, 2, 4, 8])
    rwin_d = din("rec_w_in", [D, 4128])
    rwout_d = din("rec_w_out", [D, D])
    lb_d = din("lbT", [128, 2, 2, 4])
    wg2_d = din("wg2p", [32, 2, 256])
    bg2_d = din("bg2T", [64, 2, 4])
    gn_d = din("gnT", [128, 2])
    wqkv_d = din("att_w_qkv", [D, 1536])
    wqks_d = din("att_w_qk_sw", [D, 1280])
    wo_d = din("att_w_o", [D, D])
    sink_d = din("sinkB", [128, 16])
    cos_d = din("cosT", [64, TL])
    sin_d = din("sinT", [64, TL])
    fwin_d = din("ffn_w_in", [2, D, 2 * FH])
    fwout_d = din("ffn_w_out", [2, FH, D])
    ident_d = din("ident", [128, 128])
    mintra_d = din("mask_intra", [128, 2, 128])
    mexp_d = din("mask_exp", [128, 2, 4, 128])
    smask_d = din("scanmask", [128, 512])
    band_d = din("band", [128, 384])
    out_d = nc.dram_tensor("out", [NS, TL, D], F32, kind="ExternalOutput").ap()
    dbg_d = None
    if dbg:
        dbg_d = nc.dram_tensor("dbg", [128, 8, T], F32, kind="ExternalOutput").ap()

    with ExitStack() as st:
        S = Sched(nc, st)
        AW = 52000
        big = st.enter_context(nc.sbuf_tensor("big", [128, AW], F32))
        pb = [st.enter_context(nc.psum_tensor("pb%d" % i, [128, 512], F32)) for i in range(6)]
        pq = [st.enter_context(nc.psum_tensor("pq%d" % i, [128, 1024], BF16)) for i in range(2)]
        b_pb = bufs(6, True)
        b_pq = bufs(2, True)

        def view(off, shape, dt, parts=128):
            n = 1
            for s_ in shape:
                n *= s_
            esz = 4 if dt is F32 else 2
            nb = n * esz
            assert off % 4 == 0 and nb % 4 == 0
            assert off + nb <= AW * 4, (off, nb)
            a = big[0:parts, off // 4:(off + nb) // 4]
            if dt is BF16:
                a = a.bitcast(BF16)
            if len(shape) == 2:
                a = a.rearrange("p (a b) -> p a b", b=shape[1])
            elif len(shape) == 3:
                a = a.rearrange("p (a b c) -> p a b c", b=shape[1], c=shape[2])
            return a

        class Bump:
            def __init__(self, lo, hi):
                self.lo, self.hi, self.p = lo, hi, lo

            def alloc(self, shape, dt, parts=128):
                n = 1
                for s_ in shape:
                    n *= s_
                nb = ((n * (4 if dt is F32 else 2)) + 3) // 4 * 4
                off = self.p
                self.p += nb
                assert self.p <= self.hi, ("region overflow", self.lo, self.hi, self.p)
                return view(off, shape, dt, parts)

            def reset(self):
                self.p = self.lo

        KB = 1024
        RC = Bump(0, 11 * KB)
        R0 = Bump(11 * KB, 47 * KB)
        R1 = Bump(47 * KB, 119 * KB)
        R2 = Bump(119 * KB, AW * 4)

        def ACT(out, in_, func, r, w, **kw):
            S.op("act", lambda e: e.activation(out=out, in_=in_, func=func, **kw), r, w)

        def TT(eng, out, a, b, op, r, w):
            S.op(eng, lambda e: e.tensor_tensor(out=out, in0=a, in1=b, op=op), r, w)

        def TS(eng, out, a, s1, s2, op0, op1, r, w):
            S.op(eng, lambda e: e.tensor_scalar(out=out, in0=a, scalar1=s1, scalar2=s2, op0=op0, op1=op1), r, w)

        def STT(eng, out, a, s, b, op0, op1, r, w):
            S.op(eng, lambda e: e.scalar_tensor_tensor(out=out, in0=a, scalar=s, in1=b, op0=op0, op1=op1), r, w)

        def CP(eng, out, in_, r, w):
            if eng == "act":
                ACT(out, in_, AF.Copy, r, w)
            else:
                S.op(eng, lambda e: e.tensor_copy(out=out, in_=in_), r, w)

        def MM(out, lhsT, rhs, start, stop, r, w):
            S.op("pe", lambda e: e.matmul(out, lhsT=lhsT, rhs=rhs, start=start, stop=stop), r, w)

        def TR(out, in_, idn, r, w):
            S.op("pe", lambda e: e.transpose(out, in_, idn), r, w)

        def MS(eng, ap, val, w):
            S.op(eng, lambda e: e.memset(ap, val), (), w)

        cch = S.chan()
        ident = RC.alloc([128], F32); b_c = Buf()
        identb = RC.alloc([128], BF16)
        onesb = RC.alloc([128], BF16)
        mintra = RC.alloc([2, 128], F32)
        mexp = RC.alloc([2, 4, 128], BF16)
        mexp32 = R2.alloc([2, 4, 128], F32)
        smask = RC.alloc([512], F32)
        band = RC.alloc([384], BF16)
        band32 = R2.alloc([384], F32)
        epsc = RC.alloc([1], F32)
        ngT = RC.alloc([2, 4, 8], F32)
        adab = RC.alloc([2, 48], F32)
        lbl = RC.alloc([2, 2, 4], F32)
        lbv = RC.alloc([2, 4], F32)
        omlb = RC.alloc([2, 4], F32)
        wg2 = RC.alloc([2, 256], BF16, parts=32)
        wg2f = R2.alloc([2, 256], F32, parts=32)
        bg2 = RC.alloc([2, 4], F32, parts=64)
        gnv = RC.alloc([2], F32)
        sinkB = RC.alloc([16], F32)
        cT = RC.alloc([8, 3], F32)
        sT = RC.alloc([8, 3], F32)
        V = RC.alloc([2 * 3 * 6, 8], F32)
        modT = R2.alloc([2, 48, 3], F32)
        for dst, src in ((ident, ident_d), (mintra, mintra_d), (mexp32, mexp_d), (smask, smask_d), (band32, band_d),
                         (ngT, ng_d), (adab, adab_d), (lbl, lb_d), (gnv, gn_d), (sinkB, sink_d), (cT, cT_d)):
            S.dma("sp", cch, dst, src, (), [b_c])
        S.dma("sp", cch, wg2f, wg2_d, (), [b_c])
        S.dma("sp", cch, bg2, bg2_d, (), [b_c])
        CP("dve", identb, ident, [b_c], [b_c])
        CP("dve", mexp, mexp32, [b_c], [b_c])
        CP("dve", band, band32, [b_c], [b_c])
        CP("dve", wg2, wg2f, [b_c], [b_c])
        MS("dve", onesb, 1.0, [b_c])
        MS("dve", epsc, EPS, [b_c])
        TT("dve", lbv, lbl[:, 0], lbl[:, 1], ALU.subtract, [b_c], [b_c])
        ACT(lbv, lbv, AF.Sigmoid, [b_c], [b_c])
        TS("dve", omlb, lbv, -1.0, 1.0, ALU.mult, ALU.add, [b_c], [b_c])
        ACT(sT, cT, AF.Silu, [b_c], [b_c])

        wch = [S.chan(), S.chan()]
        awb = [R1.alloc([8, 512], F32), R1.alloc([8, 512], F32)]
        b_aw = bufs(2)
        b_mod = Buf()
        it = 0
        for l in range(2):
            awv = adaw_d[l].rearrange("(j p) n -> p j n", p=128)
            for g in range(12):
                sl = it % 2
                S.dma("sp", wch[sl], awb[sl], awv[:, :, g * 512:(g + 1) * 512], (), [b_aw[sl]])
                pbt = pb[it % 2]
                for mm in range(4):
                    for j in range(8):
                        MM(pbt[:, mm * 3:mm * 3 + 3], awb[sl][:, j, mm * 128:(mm + 1) * 128], sT[:, j, :],
                           j == 0, j == 7, [b_aw[sl], b_c], [b_pb[it % 2]])
                for mm in range(4):
                    m = g * 4 + mm
                    TS("dve", modT[:, l, m, :], pbt[:, mm * 3:mm * 3 + 3], adab[:, l, m:m + 1], 0.0, ALU.add, ALU.add,
                       [b_pb[it % 2], b_c], [b_mod])
                it += 1
        def Vv(l, col, kind):
            i = (l * 3 + col) * 6 + kind
            return V[:, i, :]
        for l in range(2):
            for col in range(3):
                def mk(kind):
                    return modT[:, l, kind * 8:(kind + 1) * 8, col]
                STT("dve", Vv(l, col, 0), mk(1), 1.0, ngT[:, l, 0, :], ALU.add, ALU.mult, [b_mod, b_c], [b_c])
                CP("dve", Vv(l, col, 1), mk(0), [b_mod], [b_c])
                TT("dve", Vv(l, col, 2), mk(2), ngT[:, l, 1, :], ALU.mult, [b_mod, b_c], [b_c])
                STT("dve", Vv(l, col, 3), mk(4), 1.0, ngT[:, l, 2, :], ALU.add, ALU.mult, [b_mod, b_c], [b_c])
                CP("dve", Vv(l, col, 4), mk(3), [b_mod], [b_c])
                TT("dve", Vv(l, col, 5), mk(5), ngT[:, l, 3, :], ALU.mult, [b_mod, b_c], [b_c])
        S.barrier()

        xch = [S.chan(), S.chan()]
        och = [S.chan(), S.chan()]
        dch = S.chan()
        wq = "pool"

        try:
          cut('p0')
          for s in range(NS):
              R0.reset(); R1.reset(); R2.reset()

              def colof(t):
                  return 2 if t < 2 else s

              def xsrc(t):
                  return ctx_d[s, t * 128:(t + 1) * 128, :] if t < 2 else x_d[s, (t - 2) * 128:(t - 1) * 128, :]

              def rstd_from_ss(ps_ap, n, scale, dst, r, w):
                  ACT(dst, ps_ap, AF.Ln, r + [b_c], w, scale=scale, bias=epsc[:, 0:1])
                  ACT(dst, dst, AF.Exp, w, w, scale=-0.5)

              l = 0
              y_st = R0.alloc([8, T], BF16); b_y = bufs(8)
              u_st = R1.alloc([8, T], BF16); b_u = bufs(NT)
              qd = [R1.alloc([T], BF16) for _ in range(2)]; b_qd = [bufs(NT), bufs(NT)]
              ki = [R1.alloc([T], BF16) for _ in range(2)]; b_ki = [bufs(NT), bufs(NT)]
              keT = [R1.alloc([NT, 128], BF16) for _ in range(2)]; b_ke = [bufs(NT), bufs(NT)]
              vT = R1.alloc([T], BF16); b_vT = bufs(NT)
              gate = R1.alloc([T], BF16); b_gate = bufs(NT)
              markR2 = R2.p
              xin = [R2.alloc([1024], F32) for _ in range(2)]; b_xin = bufs(2)
              xt32 = R2.alloc([8, 128], F32); b_xt32 = Buf()
              sqb = R2.alloc([8, 128], BF16); b_sqb = Buf()
              rs_t = R2.alloc([128], F32); b_rs = Buf()
              def load_xT(t, k):
                  sl = k % 2
                  S.dma("sp", xch[sl], xin[sl], xsrc(t), (), [b_xin[sl]])
                  for j in range(8):
                      TR(pb[j // 4][:, (j % 4) * 128:(j % 4 + 1) * 128], xin[sl][:, j * 128:(j + 1) * 128], ident,
                         [b_xin[sl], b_c], [b_pb[j // 4]])

              for t in range(NT):
                  load_xT(t, t)
                  cut('p1a')
                  col = colof(t)
                  for hh in range(2):
                      pv = pb[hh][:, :].rearrange("p (a b) -> p a b", b=128)
                      ACT(sqb[:, hh * 4:(hh + 1) * 4, :], pv, AF.Square, [b_pb[hh]], [b_sqb])
                      CP("dve", xt32[:, hh * 4:(hh + 1) * 4, :], pv, [b_pb[hh]], [b_xt32])
                  cut('p1b')
                  for j in range(8):
                      MM(pb[2][:, 0:128], onesb, sqb[:, j, :], j == 0, j == 7, [b_sqb, b_c], [b_pb[2]])
                  rstd_from_ss(pb[2][:, 0:128], 128, 1.0 / D, rs_t, [b_pb[2]], [b_rs])
                  cut('p1c')
                  TT("dve", xt32, xt32, rs_t.unsqueeze(1).to_broadcast([128, 8, 128]), ALU.mult, [b_xt32, b_rs], [b_xt32])
                  cut('p1d')
                  TT("pool", xt32, xt32, Vv(l, col, 0).unsqueeze(2).to_broadcast([128, 8, 128]), ALU.mult, [b_xt32, b_c], [b_xt32])
                  TT("pool", u_st[:, :, t * 128:(t + 1) * 128], xt32, Vv(l, col, 1).unsqueeze(2).to_broadcast([128, 8, 128]),
                     ALU.add, [b_xt32, b_c], [b_u[t]])

              S.barrier()
              R2.p = markR2
              wb = [R2.alloc([8, 5, 128], BF16) for _ in range(2)]; b_wb = bufs(2)
              wlr = R2.alloc([8, 32], BF16); b_wlr = Buf()
              o_sb = R2.alloc([T], F32); b_o = bufs(NT)
              lrT = R2.alloc([T], BF16, parts=32); b_lr = bufs(NT)
              dtmp = [R2.alloc([16], F32) for _ in range(2)]; b_dt = bufs(2)
              nt_sq = R2.alloc([512], BF16); b_ntsq = Buf()
              nt_r = R2.alloc([512], F32); b_ntr = Buf()
              dcat = [R2.alloc([NT, 5], F32) for _ in range(2)]; b_dc = [bufs(NT), bufs(NT)]
              for d in range(2):
                  MS("pool", dcat[d], 0.0, b_dc[d])
              qt = [R2.alloc([T], BF16) for _ in range(2)]; b_qt = [bufs(NT), bufs(NT)]
              d4 = [R2.alloc([NT], F32) for _ in range(2)]; b_d4 = [bufs(NT), bufs(NT)]
              Dc = [R2.alloc([16], F32) for _ in range(2)]; b_Dc = bufs(2)
              markU = R2.p
              t_qs = R2.alloc([512], F32); b_tqs = Buf()
              t_s = [R2.alloc([512], F32) for _ in range(2)]; b_ts = bufs(2)
              t_g = [R2.alloc([512], F32) for _ in range(2)]; b_tg = bufs(2)
              t_e = [R2.alloc([512], F32) for _ in range(2)]; b_te = bufs(2)
              t_ki = [R2.alloc([512], F32) for _ in range(2)]; b_tki = bufs(2)
              t_ke = [R2.alloc([512], BF16) for _ in range(2)]; b_tke = bufs(2)
              R2.p = markU
              vxm = [R2.alloc([4, 128], BF16) for _ in range(2)]; b_vxm = bufs(2)
              Vx = [[R2.alloc([5, 128], BF16) for _ in range(3)] for _ in range(2)]; b_Vx = [bufs(3), bufs(3)]
              Am = [[R2.alloc([128], BF16) for _ in range(3)] for _ in range(2)]; b_Am = [bufs(3), bufs(3)]
              U32 = [[R2.alloc([4, 128], F32) for _ in range(2)] for _ in range(2)]; b_U32 = [bufs(2), bufs(2)]
              Lb = [[R2.alloc([3, 128], BF16) for _ in range(2)] for _ in range(2)]; b_Lb = [bufs(2), bufs(2)]
              S32 = [R2.alloc([128], F32) for _ in range(2)]; b_S32 = bufs(2)
              Sbf = [[R2.alloc([128], BF16) for _ in range(2)] for _ in range(2)]; b_Sbf = [bufs(2), bufs(2)]

              cut('p1')
              rwv = rwin_d.rearrange("(j p) n -> p j n", p=128)
              wc = [S.chan(), S.chan()]
              wlc = S.chan()
              S.dma(wq, wlc, wlr, rwv[:, :, 3584:3616], (), [b_wlr])

              def load_head_w(hi):
                  sl = hi % 2
                  if hi < 4:
                      for g in range(5):
                          S.dma(wq, wc[sl], wb[sl][:, :, g, :], rwv[:, :, g * 512 + hi * 128: g * 512 + (hi + 1) * 128], (), [b_wb[sl]])
                  else:
                      h = hi - 4
                      S.dma(wq, wc[sl], wb[sl][:, :, 0, 0:64], rwv[:, :, 2560 + h * 64:2560 + (h + 1) * 64], (), [b_wb[sl]])
                      S.dma(wq, wc[sl], wb[sl][:, :, 1, 0:64], rwv[:, :, 2816 + h * 64:2816 + (h + 1) * 64], (), [b_wb[sl]])
                      S.dma(wq, wc[sl], wb[sl][:, :, 3, :], rwv[:, :, 3072 + h * 128:3072 + (h + 1) * 128], (), [b_wb[sl]])
                      S.dma(wq, wc[sl], wb[sl][:, :, 4, :], rwv[:, :, 3616 + h * 128:3616 + (h + 1) * 128], (), [b_wb[sl]])

              blocks = [(i * 512, min(512, T - i * 512)) for i in range(5)]
              order = [list(range(NT)), [1, 0] + list(range(NT - 1, 1, -1))]
              load_head_w(0)
              for hi in range(8):
                  isA = hi < 4
                  h = hi if isA else hi - 4
                  K = 128 if isA else 64
                  sc = 1.0 if isA else 1.0 / 16.0
                  qscale = (128.0 ** -0.5) if isA else (64.0 ** -0.5)
                  sl = hi % 2
                  if hi + 1 < 8:
                      load_head_w(hi + 1)
                  w = wb[sl]
                  for (c0, n) in blocks:
                      ta, tb = c0 // 128, (c0 + n) // 128
                      nch = n // 32
                      tl = list(range(ta, tb))
                      ub = [b_u[t] for t in tl]

                      def proj(g, M, pbi):
                          for j in range(8):
                              MM(pb[pbi][0:M, 0:n], w[:, j, g, 0:M], u_st[:, j, c0:c0 + n], j == 0, j == 7,
                                 ub + [b_wb[sl]], [b_pb[pbi]])
                      if isA:
                          proj(0, 128, 0); proj(1, 128, 1); proj(2, 128, 2); proj(3, 128, 3); proj(4, 128, 4)
                          ACT(t_qs[:, 0:n], pb[0][:, 0:n], AF.Silu, [b_pb[0]], [b_tqs])
                          qsrc = t_qs; qb = [b_tqs]
                          ksrc = []; kb = []
                          ACT(gate[:, c0:c0 + n], pb[4][:, 0:n], AF.Silu, [b_pb[4]], [b_gate[t] for t in tl])
                          for d in range(2):
                              ACT(t_s[d][:, 0:n], pb[1 + d][:, 0:n], AF.Sigmoid, [b_pb[1 + d]], [b_ts[d]])
                              TS("dve", t_s[d][:, 0:n], t_s[d][:, 0:n], omlb[:, d, h:h + 1], lbv[:, d, h:h + 1], ALU.mult, ALU.add,
                                 [b_ts[d], b_c], [b_ts[d]])
                          for d in range(2):
                              ACT(t_g[d][:, 0:n], t_s[d][:, 0:n], AF.Ln, [b_ts[d]], [b_tg[d]])
                              TS("pool", t_s[d][:, 0:n], t_s[d][:, 0:n], -1.0, 1.0, ALU.mult, ALU.add, [b_ts[d]], [b_ts[d]])
                              ksrc.append(t_s[d]); kb.append([b_ts[d]])
                      else:
                          proj(0, 64, 0); proj(1, 64, 1); proj(3, 128, 3); proj(4, 128, 4)
                          if h == 0:
                              for j in range(8):
                                  MM(pb[5][0:32, 0:n], wlr[:, j, :], u_st[:, j, c0:c0 + n], j == 0, j == 7, ub + [b_wlr], [b_pb[5]])
                              CP("act", lrT[:, c0:c0 + n], pb[5][0:32, 0:n], [b_pb[5]], [b_lr[t] for t in tl])
                          qsrc = pb[0]; qb = [b_pb[0]]
                          ksrc = [pb[1], pb[1]]; kb = [[b_pb[1]], [b_pb[1]]]
                          ACT(gate[:, c0:c0 + n], pb[4][:, 0:n], AF.Silu, [b_pb[4]], [b_gate[t] for t in tl])
                          for d in range(2):
                              MM(pb[5][0:64, 0:n], wg2[:, d, h * 64:(h + 1) * 64], lrT[:, c0:c0 + n], True, True,
                                 [b_lr[t] for t in tl] + [b_c], [b_pb[5]])
                              ACT(t_g[d][0:64, 0:n], pb[5][0:64, 0:n], AF.Sigmoid, [b_pb[5], b_c], [b_tg[d]], bias=bg2[:, d, h:h + 1])
                          for d in range(2):
                              ACT(t_g[d][0:64, 0:n], t_g[d][0:64, 0:n], AF.Ln, [b_tg[d]], [b_tg[d]])
                      CP("act", vT[:, c0:c0 + n], pb[3][:, 0:n], [b_pb[3]], [b_vT[t] for t in tl])
                      for d in range(2):
                          g_ = t_g[d][0:K, 0:n]
                          gv = g_.rearrange("p (a b) -> p a b", b=32)
                          S.op("dve", lambda e, g_=g_, d=d: e.tensor_tensor_scan(out=t_e[d][0:K, 0:n], data0=smask[0:K, 0:n], data1=g_,
                                                                            initial=0.0, op0=ALU.mult, op1=ALU.add),
                               [b_tg[d], b_c], [b_te[d]])
                          pv = t_e[d][0:K, 0:n].rearrange("p (a b) -> p a b", b=32)
                          if d == 0:
                              CP("pool", g_, t_e[d][0:K, 0:n], [b_te[d]], [b_tg[d]])
                              tot = gv[:, :, 31:32]
                          else:
                              TT("dve", g_, g_, t_e[d][0:K, 0:n], ALU.subtract, [b_tg[d], b_te[d]], [b_tg[d]])
                              TT("dve", gv, gv, pv[:, :, 31:32].to_broadcast([K, nch, 32]), ALU.add, [b_tg[d], b_te[d]], [b_tg[d]])
                              tot = gv[:, :, 0:1]
                          dd = dtmp[d][0:K, 0:nch]
                          ACT(dd.unsqueeze(2), tot, AF.Exp, [b_tg[d]], [b_dt[d]], scale=sc)
                          ddv = dd.rearrange("p (t c) -> p t c", c=4)
                          if d == 0:
                              CP("pool", dcat[d][0:K, ta:tb, 1:5], ddv, [b_dt[d]], [b_dc[d][t] for t in tl])
                          else:
                              for c in range(4):
                                  CP("pool", dcat[d][0:K, ta:tb, 4 - c], ddv[:, :, c], [b_dt[d]], [b_dc[d][t] for t in tl])
                          Dcv = Dc[d][0:K, 0:nch].rearrange("p (t c) -> p t c", c=4)
                          po_ = [0, 1, 2, 3] if d == 0 else [3, 2, 1, 0]
                          MS("dve", Dcv[:, :, po_[0]], 1.0, [b_Dc[d]])
                          CP("dve", Dcv[:, :, po_[1]], ddv[:, :, po_[0]], [b_dt[d]], [b_Dc[d]])
                          TT("dve", Dcv[:, :, po_[2]], Dcv[:, :, po_[1]], ddv[:, :, po_[1]], ALU.mult, [b_dt[d], b_Dc[d]], [b_Dc[d]])
                          TT("dve", Dcv[:, :, po_[3]], Dcv[:, :, po_[2]], ddv[:, :, po_[2]], ALU.mult, [b_dt[d], b_Dc[d]], [b_Dc[d]])
                          TT("dve", d4[d][0:K, ta:tb], Dcv[:, :, po_[3]], ddv[:, :, po_[3]], ALU.mult, [b_dt[d], b_Dc[d]], [b_d4[d][t] for t in tl])
                          ACT(t_e[d][0:K, 0:n], g_, AF.Exp, [b_tg[d]], [b_te[d]], scale=sc)
                          STT("dve", qd[d][0:K, c0:c0 + n], qsrc[0:K, 0:n], qscale, t_e[d][0:K, 0:n], ALU.mult, ALU.mult,
                              qb + [b_te[d]], [b_qd[d][t] for t in tl])
                          TT("dve", t_ki[d][0:K, 0:n].rearrange("p (a b) -> p a b", b=32), t_e[d][0:K, 0:n].rearrange("p (a b) -> p a b", b=32),
                             Dc[d][0:K, 0:nch].unsqueeze(2).to_broadcast([K, nch, 32]), ALU.mult, [b_te[d], b_Dc[d]], [b_tki[d]])
                          STT("dve", qt[d][0:K, c0:c0 + n], qsrc[0:K, 0:n], qscale, t_ki[d][0:K, 0:n], ALU.mult, ALU.mult,
                              qb + [b_tki[d]], [b_qt[d][t] for t in tl])
                          ACT(t_e[d][0:K, 0:n], g_, AF.Exp, [b_tg[d]], [b_te[d]], scale=-sc)
                          TT("dve", t_ki[d][0:K, 0:n], ksrc[d][0:K, 0:n], t_e[d][0:K, 0:n], ALU.mult, kb[d] + [b_te[d]], [b_tki[d]])
                          CP("pool", ki[d][0:K, c0:c0 + n], t_ki[d][0:K, 0:n], [b_tki[d]], [b_ki[d][t] for t in tl])
                          TT("pool", t_ke[d][0:K, 0:n].rearrange("p (a b) -> p a b", b=32),
                             t_ki[d][0:K, 0:n].rearrange("p (a b) -> p a b", b=32),
                             dd.unsqueeze(2).to_broadcast([K, nch, 32]), ALU.mult,
                             [b_tki[d], b_dt[d]], [b_tke[d]])
                          for ti, t in enumerate(tl):
                              TR(pq[0][:, (d * 4 + ti) * 128:(d * 4 + ti) * 128 + K], t_ke[d][0:K, ti * 128:(ti + 1) * 128], identb[0:K, 0:K],
                                 [b_tke[d], b_c], [b_pq[0]])
                          nt_ = len(tl)
                          CP("act", keT[d][:, ta:tb, 0:K],
                             pq[0][:, d * 512:d * 512 + nt_ * 128].rearrange("p (a b) -> p a b", b=128)[:, :, 0:K],
                             [b_pq[0]], [b_ke[d][t] for t in tl])
                  cut('h%dp1' % hi)
                  S.barrier()
                  for d in range(2):
                      MS("dve", S32[d], 0.0, [b_S32[d]])
                      MS("dve", Sbf[d][0], 0.0, [b_Sbf[d][0]])
                  visited = set()

                  def prepA(k):
                      ts_ = [order[d][k] for d in range(2)]
                      b3 = k % 3
                      for d in range(2):
                          t = ts_[d]
                          TT("pool", vxm[d], mexp[:, d], vT[:, t * 128:(t + 1) * 128].unsqueeze(1).to_broadcast([128, 4, 128]), ALU.mult,
                             [b_vT[t], b_c], [b_vxm[d]])
                      for d in range(2):
                          t = ts_[d]
                          cs = slice(t * 128, (t + 1) * 128)
                          for c in range(4):
                              TR(pq[d][:, c * 128:(c + 1) * 128], vxm[d][:, c, :], identb, [b_vxm[d], b_c], [b_pq[d]])
                          TR(pq[d][:, 512:640], vT[:, cs], identb, [b_vT[t], b_c], [b_pq[d]])
                          MM(pb[2 + d][:, 0:128], ki[d][0:K, cs], qd[d][0:K, cs], True, True,
                             [b_ki[d][t], b_qd[d][t]], [b_pb[2 + d]])
                      for d in range(2):
                          CP("act", Vx[d][b3], pq[d][:, 0:640].rearrange("p (a b) -> p a b", b=128), [b_pq[d]], [b_Vx[d][b3]])
                          TT("dve", Am[d][b3], pb[2 + d][:, 0:128], mintra[:, d, :], ALU.mult, [b_pb[2 + d], b_c], [b_Am[d][b3]])

                  def prepB(k):
                      ts_ = [order[d][k] for d in range(2)]
                      b3 = k % 3
                      bf_ = k % 2
                      for d in range(2):
                          t = ts_[d]
                          MM(pb[d][0:K, :], keT[d][:, t, 0:K], Vx[d][b3][:, 0:4, :].rearrange("p a b -> p (a b)"), True, True,
                             [b_ke[d][t], b_Vx[d][b3]], [b_pb[d]])
                      for d in range(2):
                          CP("act", U32[d][bf_][0:K].rearrange("p a b -> p (a b)"), pb[d][0:K, :], [b_pb[d]], [b_U32[d][bf_]])
                      for s_ in (1, 2, 3):
                          for d in range(2):
                              t = ts_[d]
                              U_ = U32[d][bf_]
                              STT("dve", U_[0:K, s_, :], U_[0:K, s_ - 1, :], dcat[d][0:K, t, s_ + 1:s_ + 2], U_[0:K, s_, :], ALU.mult, ALU.add,
                                  [b_U32[d][bf_], b_dc[d][t]], [b_U32[d][bf_]])
                      for d in range(2):
                          CP("act", Lb[d][bf_][0:K].rearrange("p a b -> p (a b)"), U32[d][bf_][0:K, 0:3, :].rearrange("p a b -> p (a b)"),
                             [b_U32[d][bf_]], [b_Lb[d][bf_]])

                  def chain(k):
                      ts_ = [order[d][k] for d in range(2)]
                      b3 = k % 3
                      bf_ = k % 2
                      cur = k % 2
                      for d in range(2):
                          t = ts_[d]
                          STT("dve", S32[d][0:K], S32[d][0:K], d4[d][0:K, t:t + 1], U32[d][bf_][0:K, 3, :], ALU.mult, ALU.add,
                              [b_S32[d], b_d4[d][t], b_U32[d][bf_]], [b_S32[d]])
                      if k + 1 < NT:
                          for d in range(2):
                              CP("act", Sbf[d][1 - cur][0:K], S32[d][0:K], [b_S32[d]], [b_Sbf[d][1 - cur]])
                      for d in range(2):
                          t = ts_[d]
                          cs = slice(t * 128, (t + 1) * 128)
                          po = pb[4 + d][:, 0:128]
                          MM(po, Vx[d][b3][:, 4, :], Am[d][b3], True, False, [b_Vx[d][b3], b_Am[d][b3]], [b_pb[4 + d]])
                          for s_ in (1, 2, 3):
                              c = s_ if d == 0 else 3 - s_
                              MM(po[:, c * 32:(c + 1) * 32], Lb[d][bf_][0:K, s_ - 1, :], qd[d][0:K, t * 128 + c * 32:t * 128 + (c + 1) * 32],
                                 False, False, [b_Lb[d][bf_], b_qd[d][t]], [b_pb[4 + d]])
                          MM(po, Sbf[d][cur][0:K], qt[d][0:K, cs], False, True, [b_Sbf[d][cur], b_qt[d][t]], [b_pb[4 + d]])
                      for d in range(2):
                          t = ts_[d]
                          cs = slice(t * 128, (t + 1) * 128)
                          po = pb[4 + d][:, 0:128]
                          if t not in visited:
                              CP("dve", o_sb[:, cs], po, [b_pb[4 + d]], [b_o[t]])
                              visited.add(t)
                          else:
                              TT("dve", o_sb[:, cs], o_sb[:, cs], po, ALU.add, [b_o[t], b_pb[4 + d]], [b_o[t]])

                  prepA(0); prepA(1); prepB(0)
                  for k in range(NT):
                      if k + 2 < NT:
                          prepA(k + 2)
                      if k + 1 < NT:
                          prepB(k + 1)
                      chain(k)
                  cut('h%dp2' % hi)
                  S.barrier()
                  for (c0, n) in blocks:
                      tl = list(range(c0 // 128, (c0 + n) // 128))
                      ob = [b_o[t] for t in tl]
                      ACT(nt_sq[:, 0:n], o_sb[:, c0:c0 + n], AF.Square, ob, [b_ntsq])
                      MM(pb[4][:, 0:n], onesb, nt_sq[:, 0:n], True, True, [b_ntsq, b_c], [b_pb[4]])
                      rstd_from_ss(pb[4][:, 0:n], n, 1.0 / 128.0, nt_r[:, 0:n], [b_pb[4]], [b_ntr])
                      TT("dve", nt_r[:, 0:n], nt_r[:, 0:n], o_sb[:, c0:c0 + n], ALU.mult, [b_ntr] + ob, [b_ntr])
                      STT("dve", y_st[:, hi, c0:c0 + n], nt_r[:, 0:n], gnv[:, (0 if isA else 1):(1 if isA else 2)], gate[:, c0:c0 + n],
                          ALU.mult, ALU.mult, [b_ntr, b_c] + [b_gate[t] for t in tl], [b_y[hi]])

              cut('heads')
              S.barrier()
              R1.reset(); R2.reset()
              x_fm = R1.alloc([8, T], F32); b_x = bufs(NT)
              wo_sb = R2.alloc([8, 1024], BF16); b_wo = Buf()
              yo32 = R2.alloc([8, 128], F32); b_yo = Buf()
              sq2 = R2.alloc([8, 128], BF16); b_sq2 = Buf()
              rs2 = R2.alloc([128], F32); b_rs2 = Buf()
              xin = [R2.alloc([1024], F32) for _ in range(2)]; b_xin = bufs(2)
              wch2 = S.chan()
              S.dma(wq, wch2, wo_sb, rwout_d.rearrange("(j p) n -> p j n", p=128), (), [b_wo])
              for t in range(NT):
                  col = colof(t)
                  cs = slice(t * 128, (t + 1) * 128)
                  for f in range(8):
                      pbt = pb[2 + f % 2]
                      for j in range(8):
                          MM(pbt[:, 0:128], wo_sb[:, j, f * 128:(f + 1) * 128], y_st[:, j, cs], j == 0, j == 7,
                             [b_wo, b_y[j]], [b_pb[2 + f % 2]])
                      ACT(sq2[:, f, :], pbt[:, 0:128], AF.Square, [b_pb[2 + f % 2]], [b_sq2])
                      CP("dve", yo32[:, f, :], pbt[:, 0:128], [b_pb[2 + f % 2]], [b_yo])
                  for f in range(8):
                      MM(pb[4][:, 0:128], onesb, sq2[:, f, :], f == 0, f == 7, [b_sq2, b_c], [b_pb[4]])
                  rstd_from_ss(pb[4][:, 0:128], 128, 1.0 / D, rs2, [b_pb[4]], [b_rs2])
                  TT("dve", yo32, yo32, rs2.unsqueeze(1).to_broadcast([128, 8, 128]), ALU.mult, [b_yo, b_rs2], [b_yo])
                  TT("pool", yo32, yo32, Vv(l, col, 2).unsqueeze(2).to_broadcast([128, 8, 128]), ALU.mult, [b_yo, b_c], [b_yo])
                  sl = t % 2
                  S.dma("sp", xch[sl], xin[sl], xsrc(t), (), [b_xin[sl]])
                  for j in range(8):
                      TR(pb[j // 4][:, (j % 4) * 128:(j % 4 + 1) * 128], xin[sl][:, j * 128:(j + 1) * 128], ident,
                         [b_xin[sl], b_c], [b_pb[j // 4]])
                  for hh in range(2):
                      TT("dve", x_fm[:, hh * 4:(hh + 1) * 4, cs], yo32[:, hh * 4:(hh + 1) * 4, :],
                         pb[hh][:, :].rearrange("p (a b) -> p a b", b=128), ALU.add, [b_yo, b_pb[hh]], [b_x[t]])
              S.barrier()
              if dbg == "mix0":
                  S.dma("sp", dch, dbg_d, x_fm, [b for b in b_x], ())
                  S.barrier()
                  break

              def xb_of(c0, n):
                  return [b_x[t] for t in range(c0 // 128, (c0 + n + 127) // 128)]

              def prenorm_block(c0, n, vg, vs, dst, dstb, sqt, b_sqt, tmp, b_tmp, rs, b_rsb, pbi):
                  xs = x_fm[:, :, c0:c0 + n]
                  xb = xb_of(c0, n)
                  ACT(sqt[:, :, 0:n], xs, AF.Square, xb, [b_sqt])
                  for j in range(8):
                      MM(pb[pbi][:, 0:n], onesb, sqt[:, j, 0:n], j == 0, j == 7, [b_sqt, b_c], [b_pb[pbi]])
                  rstd_from_ss(pb[pbi][:, 0:n], n, 1.0 / D, rs[:, 0:n], [b_pb[pbi]], [b_rsb])
                  TT("dve", tmp[:, :, 0:n], xs, rs[:, 0:n].unsqueeze(1).to_broadcast([128, 8, n]), ALU.mult, xb + [b_rsb], [b_tmp])
                  TT("pool", tmp[:, :, 0:n], tmp[:, :, 0:n], vg.unsqueeze(2).to_broadcast([128, 8, n]), ALU.mult, [b_tmp, b_c], [b_tmp])
                  TT("pool", dst, tmp[:, :, 0:n], vs.unsqueeze(2).to_broadcast([128, 8, n]), ALU.add, [b_tmp, b_c], dstb)

              def post_block(c0, n, vgate, yo, b_yo_, sq, b_sq_, rs, b_rsb, pbi):
                  xb = xb_of(c0, n)
                  for f in range(8):
                      MM(pb[pbi][:, 0:n], onesb, sq[:, f, 0:n], f == 0, f == 7, [b_sq_, b_c], [b_pb[pbi]])
                  rstd_from_ss(pb[pbi][:, 0:n], n, 1.0 / D, rs[:, 0:n], [b_pb[pbi]], [b_rsb])
                  TT("dve", yo[:, :, 0:n], yo[:, :, 0:n], rs[:, 0:n].unsqueeze(1).to_broadcast([128, 8, n]), ALU.mult, [b_yo_, b_rsb], [b_yo_])
                  TT("pool", yo[:, :, 0:n], yo[:, :, 0:n], vgate.unsqueeze(2).to_broadcast([128, 8, n]), ALU.mult, [b_yo_, b_c], [b_yo_])
                  TT("dve", x_fm[:, :, c0:c0 + n], x_fm[:, :, c0:c0 + n], yo[:, :, 0:n], ALU.add, xb + [b_yo_], xb)

              def ffn(l, groups):
                  R0.reset(); R2.reset()
                  GT = 768
                  u2 = R0.alloc([8, GT], BF16); b_u2 = Buf()
                  yo = R0.alloc([8, GT], F32); b_yo_ = Buf()
                  hh_ = R2.alloc([22, GT], BF16); b_h = bufs(22)
                  sq = R2.alloc([8, GT], BF16); b_sq_ = Buf()
                  wi = [R2.alloc([8, 256], BF16) for _ in range(2)]; b_wi = bufs(2)
                  wo2 = [R2.alloc([22, 128], BF16) for _ in range(2)]; b_wo2 = bufs(2)
                  sg = [R2.alloc([512], F32) for _ in range(2)]; b_sg = bufs(2)
                  rs = R2.alloc([512], F32); b_rsb = Buf()
                  wic = [S.chan(), S.chan()]
                  woc = [S.chan(), S.chan()]
                  fwv = fwin_d[l].rearrange("(j p) n -> p j n", p=128)
                  fov = fwout_d[l].rearrange("(c p) n -> p c n", p=128)
                  for grp in groups:
                      offs = []
                      o_ = 0
                      for (c0, n, col) in grp:
                          offs.append(o_)
                          o_ += n
                      for (c0, n, col), off in zip(grp, offs):
                          prenorm_block(c0, n, Vv(l, col, 3), Vv(l, col, 4), u2[:, :, off:off + n], [b_u2],
                                        sq, b_sq_, yo, b_yo_, rs, b_rsb, 4)
                      cut('f_pre')
                      k = 0
                      for c in range(22):
                          if c == 1:
                              cut('f_h0')
                          sl = c % 2
                          S.dma(wq, wic[sl], wi[sl][:, :, 0:128], fwv[:, :, c * 128:(c + 1) * 128], (), [b_wi[sl]])
                          S.dma(wq, wic[sl], wi[sl][:, :, 128:256], fwv[:, :, FH + c * 128:FH + (c + 1) * 128], (), [b_wi[sl]])
                          for (c0, n, col), off in zip(grp, offs):
                              pa, pu = (0, 1) if k % 2 == 0 else (2, 3)
                              for j in range(8):
                                  MM(pb[pa][:, 0:n], wi[sl][:, j, 0:128], u2[:, j, off:off + n], j == 0, j == 7, [b_wi[sl], b_u2], [b_pb[pa]])
                              for j in range(8):
                                  MM(pb[pu][:, 0:n], wi[sl][:, j, 128:256], u2[:, j, off:off + n], j == 0, j == 7, [b_wi[sl], b_u2], [b_pb[pu]])
                              ACT(sg[k % 2][:, 0:n], pb[pa][:, 0:n], AF.Silu, [b_pb[pa]], [b_sg[k % 2]])
                              TT("dve", hh_[:, c, off:off + n], sg[k % 2][:, 0:n], pb[pu][:, 0:n], ALU.mult, [b_sg[k % 2], b_pb[pu]], [b_h[c]])
                              k += 1
                      cut('f_hid')
                      k = 0
                      for f in range(8):
                          if f == 1:
                              cut('f_o0')
                          sl = f % 2
                          S.dma(wq, woc[sl], wo2[sl], fov[:, :, f * 128:(f + 1) * 128], (), [b_wo2[sl]])
                          for (c0, n, col), off in zip(grp, offs):
                              pi = k % 2
                              for c in range(22):
                                  MM(pb[pi][:, 0:n], wo2[sl][:, c, :], hh_[:, c, off:off + n], c == 0, c == 21, [b_wo2[sl], b_h[c]], [b_pb[pi]])
                              ACT(sq[:, f, off:off + n], pb[pi][:, 0:n], AF.Square, [b_pb[pi]], [b_sq_])
                              CP("dve", yo[:, f, off:off + n], pb[pi][:, 0:n], [b_pb[pi]], [b_yo_])
                              k += 1
                      cut('f_out')
                      for (c0, n, col), off in zip(grp, offs):
                          post_block(c0, n, Vv(l, col, 5), yo[:, :, off:off + n], b_yo_, sq[:, :, off:off + n], b_sq_, rs, b_rsb, 4)
                          cut('f_post')
                  cut('f_all')
                  S.barrier()

              ffn(0, [[(0, 256, 2), (256, 512, s)], [(768, 512, s), (1280, 256, s)], [(1536, 512, s), (2048, 256, s)]])
              if dbg == "ffn0":
                  S.dma("sp", dch, dbg_d, x_fm, [b for b in b_x], ())
                  S.barrier()
                  break

              l = 1
              R0.reset(); R2.reset()
              u_st = R0.alloc([8, T], BF16); b_u1 = Buf()
              y1 = R2.alloc([8, TL], BF16); b_y1 = bufs(8)
              markA = R2.p
              sqt = R2.alloc([8, 512], BF16); b_sqt = Buf()
              tmpn = R2.alloc([8, 512], F32); b_tmpn = Buf()
              rsn = R2.alloc([512], F32); b_rsn = Buf()
              for (c0, n, col) in [(0, 256, 2)] + [(256 + 512 * i, 512, s) for i in range(4)]:
                  prenorm_block(c0, n, Vv(l, col, 0), Vv(l, col, 1), u_st[:, :, c0:c0 + n], [b_u1], sqt, b_sqt, tmpn, b_tmpn, rsn, b_rsn, 4)
              S.barrier()
              R2.p = markA
              cosT = R2.alloc([TL], F32, parts=64); sinT = R2.alloc([TL], F32, parts=64); b_tab = Buf()
              tch = S.chan()
              S.dma("sp", tch, cosT, cos_d, (), [b_tab])
              S.dma("sp", tch, sinT, sin_d, (), [b_tab])
              kT = R2.alloc([T], BF16, parts=64); b_kT = Buf()
              vv = R2.alloc([NT, 128], BF16); b_vv = Buf()
              qT = R2.alloc([TL], BF16, parts=64); b_qT = Buf()
              wk = R2.alloc([8, 2, 64], BF16); b_wk = Buf()
              wv2 = R2.alloc([8, 128], BF16); b_wv2 = Buf()
              wqb = [R2.alloc([8, 2, 64], BF16) for _ in range(2)]; b_wqb = bufs(2)
              ta1 = R2.alloc([512], F32, parts=64); b_ta1 = Buf()
              ta2 = R2.alloc([512], F32, parts=64); b_ta2 = Buf()
              sqa = R2.alloc([512], BF16, parts=64); b_sqa = Buf()
              Pc = [R2.alloc([512], BF16) for _ in range(2)]; b_Pc = bufs(2)
              rden = R2.alloc([512], F32); b_rden = Buf()
              sm = R2.alloc([16], F32); b_sm = Buf()
              wkc = S.chan(); wvc = S.chan(); wqc = [S.chan(), S.chan()]
              wqv = wqkv_d.rearrange("(j p) n -> p j n", p=128)
              wsv = wqks_d.rearrange("(j p) n -> p j n", p=128)
              lat_blocks = [(256 + 512 * i, 512) for i in range(4)]

              def load_q_w(hq):
                  sl = hq % 2
                  S.dma(wq, wqc[sl], wqb[sl][:, :, 0, :], wqv[:, :, hq * 64:(hq + 1) * 64], (), [b_wqb[sl]])
                  S.dma(wq, wqc[sl], wqb[sl][:, :, 1, :], wsv[:, :, hq * 64:(hq + 1) * 64], (), [b_wqb[sl]])

              def rope_proj(wt, b_wt, c0, n, dstT, b_dst, scale):
                  for j in range(8):
                      MM(pb[0][0:64, 0:n], wt[:, j, 0, :], u_st[:, j, c0:c0 + n], j == 0, j == 7, [b_wt, b_u1], [b_pb[0]])
                  for j in range(8):
                      MM(pb[1][0:64, 0:n], wt[:, j, 1, :], u_st[:, j, c0:c0 + n], j == 0, j == 7, [b_wt, b_u1], [b_pb[1]])
                  lc = c0 - 256
                  TT("dve", ta1[:, 0:n], pb[0][0:64, 0:n], cosT[:, lc:lc + n], ALU.mult, [b_pb[0], b_tab], [b_ta1])
                  TT("dve", ta2[:, 0:n], pb[1][0:64, 0:n], sinT[:, lc:lc + n], ALU.mult, [b_pb[1], b_tab], [b_ta2])
                  TT("pool", ta1[:, 0:n], ta1[:, 0:n], ta2[:, 0:n], ALU.add, [b_ta1, b_ta2], [b_ta1])
                  ACT(dstT, ta1[:, 0:n], AF.Copy, [b_ta1], [b_dst], scale=scale)

              def sqmax(srcT, b_src, c0, n, acc_col, first):
                  ACT(sqa[:, 0:n], srcT, AF.Square, [b_src], [b_sqa])
                  MM(pb[2][:, 0:n], onesb[0:64, :], sqa[:, 0:n], True, True, [b_sqa, b_c], [b_pb[2]])
                  if first:
                      S.op("dve", lambda e: e.reduce_max(out=sm[:, acc_col:acc_col + 1], in_=pb[2][:, 0:n], axis=mybir.AxisListType.X), [b_pb[2]], [b_sm])
                  else:
                      S.op("dve", lambda e: e.reduce_max(out=sm[:, 2:3], in_=pb[2][:, 0:n], axis=mybir.AxisListType.X), [b_pb[2]], [b_sm])
                      TT("dve", sm[:, acc_col:acc_col + 1], sm[:, acc_col:acc_col + 1], sm[:, 2:3], ALU.max, [b_sm], [b_sm])

              load_q_w(0)
              for g in range(4):
                  S.dma(wq, wkc, wk[:, :, 0, :], wqv[:, :, 1024 + g * 64:1024 + (g + 1) * 64], (), [b_wk])
                  S.dma(wq, wkc, wk[:, :, 1, :], wsv[:, :, 1024 + g * 64:1024 + (g + 1) * 64], (), [b_wk])
                  S.dma(wq, wvc, wv2[:, :, 0:64], wqv[:, :, 1280 + g * 64:1280 + (g + 1) * 64], (), [b_wv2])
                  S.dma(wq, wvc, wv2[:, :, 64:128], wqv[:, :, 1280 + g * 64:1280 + (g + 1) * 64], (), [b_wv2])
                  for j in range(8):
                      MM(pb[0][0:64, 0:256], wk[:, j, 0, :], u_st[:, j, 0:256], j == 0, j == 7, [b_wk, b_u1], [b_pb[0]])
                  CP("act", kT[:, 0:256], pb[0][0:64, 0:256], [b_pb[0]], [b_kT])
                  sqmax(kT[:, 0:256], b_kT, 0, 256, 0, True)
                  for (c0, n) in lat_blocks:
                      rope_proj(wk, b_wk, c0, n, kT[:, c0:c0 + n], b_kT, 1.0)
                      sqmax(kT[:, c0:c0 + n], b_kT, c0, n, 0, False)
                  for t in range(NT):
                      pi = 3 + t % 2
                      for j in range(8):
                          MM(pb[pi][:, 0:128], u_st[:, j, t * 128:(t + 1) * 128], wv2[:, j, :], j == 0, j == 7, [b_wv2, b_u1], [b_pb[pi]])
                      CP("act", vv[:, t, :], pb[pi][:, 0:128], [b_pb[pi]], [b_vv])
                  for hq in range(4 * g, 4 * g + 4):
                      sl = hq % 2
                      if hq + 1 < 16:
                          load_q_w(hq + 1)
                      for bi, (c0, n) in enumerate(lat_blocks):
                          rope_proj(wqb[sl], b_wqb[sl], c0, n, qT[:, c0 - 256:c0 - 256 + n], b_qT, 0.125)
                          sqmax(qT[:, c0 - 256:c0 - 256 + n], b_qT, c0, n, 1, bi == 0)
                      TT("dve", sm[:, 3:4], sm[:, 0:1], sm[:, 1:2], ALU.mult, [b_sm], [b_sm])
                      ACT(sm[:, 3:4], sm[:, 3:4], AF.Ln, [b_sm], [b_sm])
                      ACT(sm[:, 3:4], sm[:, 3:4], AF.Exp, [b_sm], [b_sm], scale=0.5)
                      TT("dve", sm[:, 3:4], sm[:, 3:4], sinkB[:, hq:hq + 1], ALU.max, [b_sm, b_c], [b_sm])
                      TS("dve", sm[:, 4:5], sm[:, 3:4], -1.0, 0.0, ALU.mult, ALU.add, [b_sm], [b_sm])
                      ACT(sm[:, 5:6], sinkB[:, hq:hq + 1], AF.Exp, [b_sm, b_c], [b_sm], bias=sm[:, 4:5])
                      negM = sm[:, 4:5]
                      pk = 0
                      for Q in range(4):
                          n0 = Q * 4
                          qc = slice(Q * 512, (Q + 1) * 512)
                          first = True
                          for cb in range(2):
                              ps_ = pb[pk % 2]; bps = b_pb[pk % 2]; P_ = Pc[pk % 2]; bP = b_Pc[pk % 2]; pk += 1
                              MM(ps_[:, 0:512], kT[:, cb * 128:(cb + 1) * 128], qT[:, qc], True, True, [b_kT, b_qT], [bps])
                              ACT(P_[:, 0:512], ps_[:, 0:512], AF.Exp, [bps, b_sm], [bP], bias=negM)
                              MM(pb[4][:, 0:512], vv[:, cb, :], P_[:, 0:512], first, False, [b_vv, bP], [b_pb[4]])
                              MM(pb[5][:, 0:512], onesb, P_[:, 0:512], first, False, [b_c, bP], [b_pb[5]])
                              first = False
                          jbs = [jb for jb in range(n0 - 1, n0 + 5) if 0 <= jb < 16]
                          for ji, jb in enumerate(jbs):
                              qa = max(jb - 1, n0); qe = min(jb + 1, n0 + 3)
                              w_ = (qe - qa + 1) * 128
                              moff = (qa - (jb - 1)) * 128
                              ps_ = pb[pk % 2]; bps = b_pb[pk % 2]; P_ = Pc[pk % 2]; bP = b_Pc[pk % 2]; pk += 1
                              MM(ps_[:, 0:w_], kT[:, 256 + jb * 128:256 + (jb + 1) * 128], qT[:, qa * 128:(qe + 1) * 128], True, True, [b_kT, b_qT], [bps])
                              ACT(P_[:, 0:w_], ps_[:, 0:w_], AF.Exp, [bps, b_sm], [bP], bias=negM)
                              TT("dve", P_[:, 0:w_], P_[:, 0:w_], band[:, moff:moff + w_], ALU.mult, [bP, b_c], [bP])
                              last = ji == len(jbs) - 1
                              for nb in range(qa, qe + 1):
                                  oc = slice((nb - n0) * 128, (nb - n0 + 1) * 128)
                                  pc_ = slice((nb - qa) * 128, (nb - qa + 1) * 128)
                                  lst = last and nb == qe
                                  MM(pb[4][:, oc], vv[:, 2 + jb, :], P_[:, pc_], False, lst, [b_vv, bP], [b_pb[4]])
                                  MM(pb[5][:, oc], onesb, P_[:, pc_], False, lst, [b_c, bP], [b_pb[5]])
                          ACT(rden[:, 0:512], pb[5][:, 0:512], AF.Ln, [b_pb[5], b_sm], [b_rden], bias=sm[:, 5:6])
                          ACT(rden[:, 0:512], rden[:, 0:512], AF.Exp, [b_rden], [b_rden], scale=-1.0)
                          hp = (hq % 2) * 64
                          TT("dve", y1[hp:hp + 64, hq // 2, qc], pb[4][hp:hp + 64, 0:512], rden[hp:hp + 64, 0:512], ALU.mult,
                             [b_pb[4], b_rden], [b_y1[hq // 2]])
              S.barrier()
              R0.reset(); R2.p = markA
              wo1 = R0.alloc([8, 1024], BF16); b_wo1 = Buf()
              yo1 = R2.alloc([8, 512], F32); b_yo1 = Buf()
              sq1 = R2.alloc([8, 512], BF16); b_sq1 = Buf()
              rs1 = R2.alloc([512], F32); b_rs1 = Buf()
              woc1 = S.chan()
              S.dma(wq, woc1, wo1, wo_d.rearrange("(j p) n -> p j n", p=128), (), [b_wo1])
              k = 0
              for (c0, n) in lat_blocks:
                  lc = c0 - 256
                  for f in range(8):
                      pi = k % 2; k += 1
                      for j in range(8):
                          MM(pb[pi][:, 0:n], wo1[:, j, f * 128:(f + 1) * 128], y1[:, j, lc:lc + n], j == 0, j == 7, [b_wo1, b_y1[j]], [b_pb[pi]])
                      ACT(sq1[:, f, 0:n], pb[pi][:, 0:n], AF.Square, [b_pb[pi]], [b_sq1])
                      CP("dve", yo1[:, f, 0:n], pb[pi][:, 0:n], [b_pb[pi]], [b_yo1])
                  post_block(c0, n, Vv(l, s, 2), yo1, b_yo1, sq1, b_sq1, rs1, b_rs1, 4)
              S.barrier()
              if dbg == "mix1":
                  S.dma("sp", dch, dbg_d, x_fm, [b for b in b_x], ())
                  S.barrier()
                  break
              ffn(1, [[(256, 512, s), (768, 256, s)], [(1024, 512, s), (1536, 256, s)], [(1792, 512, s)]])
              if dbg == "ffn1":
                  S.dma("sp", dch, dbg_d, x_fm, [b for b in b_x], ())
                  S.barrier()
                  break
              R0.reset(); R2.reset()
              ot = [R2.alloc([1024], F32) for _ in range(2)]; b_ot = bufs(2)
              for t in range(2, NT):
                  sl = t % 2
                  for j in range(8):
                      TR(pb[sl * 2 + j // 4][:, (j % 4) * 128:(j % 4 + 1) * 128], x_fm[:, j, t * 128:(t + 1) * 128], ident,
                         [b_x[t], b_c], [b_pb[sl * 2 + j // 4]])
                  CP("act", ot[sl][:, 0:512], pb[sl * 2][:, :], [b_pb[sl * 2]], [b_ot[sl]])
                  CP("dve", ot[sl][:, 512:1024], pb[sl * 2 + 1][:, :], [b_pb[sl * 2 + 1]], [b_ot[sl]])
                  S.dma("sp", och[sl], out_d[s, (t - 2) * 128:(t - 1) * 128, :], ot[sl], [b_ot[sl]], ())
              S.barrier()

        except StopBuild:
            pass
        S.barrier()
        for e in ("sp",):
            for c in S.chans:
                if c.cnt:
                    S.eng[e].wait_ge(c.sem, 16 * c.cnt)
    return nc


def host_prep(inputs, core, NS=2):
    f = np.float32
    b0 = core * NS
    x = np.ascontiguousarray(inputs["x"][b0:b0 + NS]).astype(f)
    ctx = np.ascontiguousarray(inputs["ctx"][b0:b0 + NS]).astype(f)
    c = inputs["c"][b0:b0 + NS]
    cols = [c[0], c[min(1, NS - 1)], inputs["c_ctx"]]
    cT = np.stack([np.asarray(v, f).reshape(8, 128).T for v in cols], axis=-1)
    ada_bT = np.asarray(inputs["ada_b"], f).reshape(2, 48, 128).transpose(2, 0, 1)
    ngT = np.asarray(inputs["norm_g"], f).reshape(2, 4, 8, 128).transpose(3, 0, 1, 2)
    lbT = np.asarray(inputs["rec_lb_logits"], f).reshape(2, 2, 4, 128).transpose(3, 0, 1, 2)
    wg2 = np.asarray(inputs["rec_w_g2"], f)[0]
    wg2p = np.zeros((32, 2, 256), f)
    wg2p[0:16, 0, :] = wg2[0]
    wg2p[16:32, 1, :] = wg2[1]
    bg2T = np.asarray(inputs["rec_b_g2"], f)[0].reshape(2, 4, 64).transpose(2, 0, 1)
    gnT = np.stack([np.asarray(inputs["rec_gn_a"], f)[0], np.asarray(inputs["rec_gn_b"], f)[0]], axis=-1)
    wqkv = np.asarray(inputs["att_w_qkv"], f)[0]
    qk = wqkv[:, :1280].reshape(1024, 640, 2)[:, :, ::-1].reshape(1024, 1280)
    sinkB = np.broadcast_to(np.asarray(inputs["att_sink"], f)[0][None, :], (128, 16))
    n_rows = TL // 64
    row = np.repeat(np.arange(n_rows), 64).astype(f)
    colp = np.tile(np.arange(64), n_rows).astype(f)
    inv = (np.float32(10000.0) ** (-np.arange(0, 32, 2, dtype=f) / np.float32(32))).astype(f)
    ang = np.concatenate([row[:, None] * inv, colp[:, None] * inv], axis=-1).astype(f)
    cosT = np.repeat(np.cos(ang).astype(f).T, 2, axis=0)
    sinv = np.sin(ang).astype(f).T
    sinT = np.empty((64, TL), f)
    sinT[0::2] = -sinv
    sinT[1::2] = sinv
    jj = np.arange(128)[:, None]; ii = np.arange(128)[None, :]
    same = (jj // 32) == (ii // 32)
    mask_intra = np.stack([(same & (jj <= ii)), (same & (jj >= ii))], axis=1).astype(f)
    mask_exp = np.zeros((128, 2, 4, 128), f)
    for cc in range(4):
        mask_exp[:, 0, cc, cc * 32:(cc + 1) * 32] = 1.0
        mask_exp[:, 1, 3 - cc, cc * 32:(cc + 1) * 32] = 1.0
    scanmask = np.ones((128, 512), f); scanmask[:, 0::32] = 0.0
    il = np.arange(384)[None, :]
    band = (np.abs(il - 128 - jj) <= 128).astype(f)
    return {
        "x": x, "ctx": ctx, "cT": np.ascontiguousarray(cT), "ada_w": np.asarray(inputs["ada_w"], f),
        "ada_bT": np.ascontiguousarray(ada_bT), "ngT": np.ascontiguousarray(ngT),
        "rec_w_in": np.asarray(inputs["rec_w_in"], f)[0], "rec_w_out": np.asarray(inputs["rec_w_out"], f)[0],
        "lbT": np.ascontiguousarray(lbT), "wg2p": wg2p, "bg2T": np.ascontiguousarray(bg2T), "gnT": np.ascontiguousarray(gnT),
        "att_w_qkv": wqkv, "att_w_qk_sw": np.ascontiguousarray(qk), "att_w_o": np.asarray(inputs["att_w_o"], f)[0],
        "sinkB": np.ascontiguousarray(sinkB), "cosT": np.ascontiguousarray(cosT), "sinT": sinT,
        "ffn_w_in": np.asarray(inputs["ffn_w_in"], f), "ffn_w_out": np.asarray(inputs["ffn_w_out"], f),
        "ident": np.eye(128, dtype=f), "mask_intra": mask_intra, "mask_exp": mask_exp, "scanmask": scanmask, "band": band,
    }


def kernel(**inputs):
    NS = 2
    nc = build(NS)
    in_maps = [host_prep(inputs, core, NS) for core in range(8)]
    res = run_bass_kernel_spmd(nc, in_maps, core_ids=list(range(8)))
    return np.concatenate([r["out"] for r in res.results], axis=0).astype(np.float32)
```

```python
from contextlib import ExitStack
import numpy as np
import concourse.bass as bass
import concourse.mybir as mybir
from concourse.bass_utils import run_bass_kernel_spmd

F32 = mybir.dt.float32
BF16 = mybir.dt.bfloat16
AF = mybir.ActivationFunctionType
ALU = mybir.AluOpType

D = 1024
T = 2304
NT = 18
TL = 2048
FH = 2816
EPS = 1e-6


class Buf:
    __slots__ = ("w", "rs", "excl")

    def __init__(self, excl=False):
        self.w = None
        self.rs = {}
        self.excl = excl


def bufs(n, excl=False):
    return [Buf(excl) for _ in range(n)]


class Chan:
    __slots__ = ("sem", "cnt")

    def __init__(self, sem):
        self.sem = sem
        self.cnt = 0


class Sched:
    ENGS = ("pe", "act", "dve", "pool", "sp")

    def __init__(self, nc, stack):
        self.nc = nc
        self.stack = stack
        self.eng = {"pe": nc.tensor, "act": nc.scalar, "dve": nc.vector, "pool": nc.gpsimd, "sp": nc.sync}
        self.esem = {e: stack.enter_context(nc.semaphore("es_" + e)) for e in self.ENGS}
        self.ecnt = {e: 0 for e in self.ENGS}
        self.seen = {e: {} for e in self.ENGS}
        self.chans = []

    def chan(self):
        c = Chan(self.stack.enter_context(self.nc.semaphore("ch%d" % len(self.chans))))
        self.chans.append(c)
        return c

    def _deps(self, eng, reads, writes, skip_sem=None):
        deps = {}
        for b in reads:
            if b.w is not None:
                s, v = b.w
                if v > deps.get(s, 0):
                    deps[s] = v
        for b in writes:
            if b.w is not None:
                s, v = b.w
                if v > deps.get(s, 0):
                    deps[s] = v
            for s, v in b.rs.items():
                if v > deps.get(s, 0):
                    deps[s] = v
        own = self.esem[eng]
        seen = self.seen[eng]
        e = self.eng[eng]
        for s, v in deps.items():
            if s is skip_sem:
                continue
            if s is own and eng == "pe":
                continue
            if seen.get(s, 0) >= v:
                continue
            seen[s] = v
            e.wait_ge(s, v)

    def _mark(self, tok, reads, writes):
        s, v = tok
        for b in writes:
            b.w = tok
            b.rs = {}
        for b in reads:
            if b.rs.get(s, 0) < v:
                b.rs[s] = v

    def op(self, eng, fn, reads=(), writes=()):
        if any(b.excl for b in reads):
            writes = list(writes) + [b for b in reads if b.excl]
            reads = [b for b in reads if not b.excl]
        self._deps(eng, reads, writes)
        self.ecnt[eng] += 1
        tok = (self.esem[eng], self.ecnt[eng])
        fn(self.eng[eng]).then_inc(self.esem[eng], 1)
        self._mark(tok, reads, writes)

    def dma(self, eng, chan, out, in_, reads=(), writes=()):
        self._deps(eng, reads, writes, skip_sem=chan.sem)
        chan.cnt += 1
        tok = (chan.sem, 16 * chan.cnt)
        self.eng[eng].dma_start(out=out, in_=in_).then_inc(chan.sem, 16)
        self._mark(tok, reads, writes)

    def barrier(self):
        toks = [(self.esem[f], self.ecnt[f]) for f in self.ENGS if self.ecnt[f] > 0]
        toks += [(c.sem, 16 * c.cnt) for c in self.chans if c.cnt > 0]
        for e in self.ENGS:
            seen = self.seen[e]
            for s, v in toks:
                if s is self.esem[e]:
                    continue
                if seen.get(s, 0) >= v:
                    continue
                seen[s] = v
                self.eng[e].wait_ge(s, v)


class StopBuild(Exception):
    pass


def build(NS=2, dbg=None, stop_after=None):
    import os
    CUT = os.environ.get('CUT', '')

    def cut(name):
        if CUT == name:
            raise StopBuild()
    nc = bass.Bass("TRN2", target_bir_lowering=False)

    def din(name, shape, dt=F32):
        return nc.dram_tensor(name, list(shape), dt, kind="ExternalInput").ap()

    x_d = din("x", [NS, TL, D])
    ctx_d = din("ctx", [NS, 256, D])
    cT_d = din("cT", [128, 8, 3])
    adaw_d = din("ada_w", [2, D, 6 * D])
    adab_d = din("ada_bT", [128, 2, 48])
    ng_d = din("ngT", [128, 2, 4, 8])
    rwin_d = din("rec_w_in", [D, 4128])
    rwout_d = din("rec_w_out", [D, D])
    lb_d = din("lbT", [128, 2, 2, 4])
    wg2_d = din("wg2p", [32, 2, 256])
    bg2_d = din("bg2T", [64, 2, 4])
    gn_d = din("gnT", [128, 2])
    wqkv_d = din("att_w_qkv", [D, 1536])
    wqks_d = din("att_w_qk_sw", [D, 1280])
    wo_d = din("att_w_o", [D, D])
    sink_d = din("sinkB", [128, 16])
    cos_d = din("cosT", [64, TL])
    sin_d = din("sinT", [64, TL])
    fwin_d = din("ffn_w_in", [2, D, 2 * FH])
    fwout_d = din("ffn_w_out", [2, FH, D])
    ident_d = din("ident", [128, 128])
    mintra_d = din("mask_intra", [128, 2, 128])
    mexp_d = din("mask_exp", [128, 2, 4, 128])
    smask_d = din("scanmask", [128, 512])
    band_d = din("band", [128, 384])
    out_d = nc.dram_tensor("out", [NS, TL, D], F32, kind="ExternalOutput").ap()
    dbg_d = None
    if dbg:
        dbg_d = nc.dram_tensor("dbg", [128, 8, T], F32, kind="ExternalOutput").ap()

    with ExitStack() as st:
        S = Sched(nc, st)
        AW = 52000
        big = st.enter_context(nc.sbuf_tensor("big", [128, AW], F32))
        pb = [st.enter_context(nc.psum_tensor("pb%d" % i, [128, 512], F32)) for i in range(6)]
        pq = [st.enter_context(nc.psum_tensor("pq%d" % i, [128, 1024], BF16)) for i in range(2)]
        b_pb = bufs(6, True)
        b_pq = bufs(2, True)

        def view(off, shape, dt, parts=128):
            n = 1
            for s_ in shape:
                n *= s_
            esz = 4 if dt is F32 else 2
            nb = n * esz
            assert off % 4 == 0 and nb % 4 == 0
            assert off + nb <= AW * 4, (off, nb)
            a = big[0:parts, off // 4:(off + nb) // 4]
            if dt is BF16:
                a = a.bitcast(BF16)
            if len(shape) == 2:
                a = a.rearrange("p (a b) -> p a b", b=shape[1])
            elif len(shape) == 3:
                a = a.rearrange("p (a b c) -> p a b c", b=shape[1], c=shape[2])
            return a

        class Bump:
            def __init__(self, lo, hi):
                self.lo, self.hi, self.p = lo, hi, lo

            def alloc(self, shape, dt, parts=128):
                n = 1
                for s_ in shape:
                    n *= s_
                nb = ((n * (4 if dt is F32 else 2)) + 3) // 4 * 4
                off = self.p
                self.p += nb
                assert self.p <= self.hi, ("region overflow", self.lo, self.hi, self.p)
                return view(off, shape, dt, parts)

            def reset(self):
                self.p = self.lo

        KB = 1024
        RC = Bump(0, 11 * KB)
        R0 = Bump(11 * KB, 47 * KB)
        R1 = Bump(47 * KB, 119 * KB)
        R2 = Bump(119 * KB, AW * 4)

        def ACT(out, in_, func, r, w, **kw):
            S.op("act", lambda e: e.activation(out=out, in_=in_, func=func, **kw), r, w)

        def TT(eng, out, a, b, op, r, w):
            S.op(eng, lambda e: e.tensor_tensor(out=out, in0=a, in1=b, op=op), r, w)

        def TS(eng, out, a, s1, s2, op0, op1, r, w):
            S.op(eng, lambda e: e.tensor_scalar(out=out, in0=a, scalar1=s1, scalar2=s2, op0=op0, op1=op1), r, w)

        def STT(eng, out, a, s, b, op0, op1, r, w):
            S.op(eng, lambda e: e.scalar_tensor_tensor(out=out, in0=a, scalar=s, in1=b, op0=op0, op1=op1), r, w)

        def CP(eng, out, in_, r, w):
            if eng == "act":
                ACT(out, in_, AF.Copy, r, w)
            else:
                S.op(eng, lambda e: e.tensor_copy(out=out, in_=in_), r, w)

        def MM(out, lhsT, rhs, start, stop, r, w):
            S.op("pe", lambda e: e.matmul(out, lhsT=lhsT, rhs=rhs, start=start, stop=stop), r, w)

        def TR(out, in_, idn, r, w):
            S.op("pe", lambda e: e.transpose(out, in_, idn), r, w)

        def MS(eng, ap, val, w):
            S.op(eng, lambda e: e.memset(ap, val), (), w)

        cch = S.chan()
        ident = RC.alloc([128], F32); b_c = Buf()
        identb = RC.alloc([128], BF16)
        onesb = RC.alloc([128], BF16)
        mintra = RC.alloc([2, 128], F32)
        mexp = RC.alloc([2, 4, 128], BF16)
        mexp32 = R2.alloc([2, 4, 128], F32)
        smask = RC.alloc([512], F32)
        band = RC.alloc([384], BF16)
        band32 = R2.alloc([384], F32)
        epsc = RC.alloc([1], F32)
        ngT = RC.alloc([2, 4, 8], F32)
        adab = RC.alloc([2, 48], F32)
        lbl = RC.alloc([2, 2, 4], F32)
        lbv = RC.alloc([2, 4], F32)
        omlb = RC.alloc([2, 4], F32)
        wg2 = RC.alloc([2, 256], BF16, parts=32)
        wg2f = R2.alloc([2, 256], F32, parts=32)
        bg2 = RC.alloc([2, 4], F32, parts=64)
        gnv = RC.alloc([2], F32)
        sinkB = RC.alloc([16], F32)
        cT = RC.alloc([8, 3], F32)
        sT = RC.alloc([8, 3], F32)
        V = RC.alloc([2 * 3 * 6, 8], F32)
        modT = R2.alloc([2, 48, 3], F32)
        for dst, src in ((ident, ident_d), (mintra, mintra_d), (mexp32, mexp_d), (smask, smask_d), (band32, band_d),
                         (ngT, ng_d), (adab, adab_d), (lbl, lb_d), (gnv, gn_d), (sinkB, sink_d), (cT, cT_d)):
            S.dma("sp", cch, dst, src, (), [b_c])
        S.dma("sp", cch, wg2f, wg2_d, (), [b_c])
        S.dma("sp", cch, bg2, bg2_d, (), [b_c])
        CP("dve", identb, ident, [b_c], [b_c])
        CP("dve", mexp, mexp32, [b_c], [b_c])
        CP("dve", band, band32, [b_c], [b_c])
        CP("dve", wg2, wg2f, [b_c], [b_c])
        MS("dve", onesb, 1.0, [b_c])
        MS("dve", epsc, EPS, [b_c])
        TT("dve", lbv, lbl[:, 0], lbl[:, 1], ALU.subtract, [b_c], [b_c])
        ACT(lbv, lbv, AF.Sigmoid, [b_c], [b_c])
        TS("dve", omlb, lbv, -1.0, 1.0, ALU.mult, ALU.add, [b_c], [b_c])
        ACT(sT, cT, AF.Silu, [b_c], [b_c])

        wch = [S.chan(), S.chan()]
        awb = [R1.alloc([8, 512], F32), R1.alloc([8, 512], F32)]
        b_aw = bufs(2)
        b_mod = Buf()
        it = 0
        for l in range(2):
            awv = adaw_d[l].rearrange("(j p) n -> p j n", p=128)
            for g in range(12):
                sl = it % 2
                S.dma("sp", wch[sl], awb[sl], awv[:, :, g * 512:(g + 1) * 512], (), [b_aw[sl]])
                pbt = pb[it % 2]
                for mm in range(4):
                    for j in range(8):
                        MM(pbt[:, mm * 3:mm * 3 + 3], awb[sl][:, j, mm * 128:(mm + 1) * 128], sT[:, j, :],
                           j == 0, j == 7, [b_aw[sl], b_c], [b_pb[it % 2]])
                for mm in range(4):
                    m = g * 4 + mm
                    TS("dve", modT[:, l, m, :], pbt[:, mm * 3:mm * 3 + 3], adab[:, l, m:m + 1], 0.0, ALU.add, ALU.add,
                       [b_pb[it % 2], b_c], [b_mod])
                it += 1
        def Vv(l, col, kind):
            i = (l * 3 + col) * 6 + kind
            return V[:, i, :]
        for l in range(2):
            for col in range(3):
                def mk(kind):
                    return modT[:, l, kind * 8:(kind + 1) * 8, col]
                STT("dve", Vv(l, col, 0), mk(1), 1.0, ngT[:, l, 0, :], ALU.add, ALU.mult, [b_mod, b_c], [b_c])
                CP("dve", Vv(l, col, 1), mk(0), [b_mod], [b_c])
                TT("dve", Vv(l, col, 2), mk(2), ngT[:, l, 1, :], ALU.mult, [b_mod, b_c], [b_c])
                STT("dve", Vv(l, col, 3), mk(4), 1.0, ngT[:, l, 2, :], ALU.add, ALU.mult, [b_mod, b_c], [b_c])
                CP("dve", Vv(l, col, 4), mk(3), [b_mod], [b_c])
                TT("dve", Vv(l, col, 5), mk(5), ngT[:, l, 3, :], ALU.mult, [b_mod, b_c], [b_c])
        S.barrier()

        xch = [S.chan(), S.chan()]
        och = [S.chan(), S.chan()]
        dch = S.chan()
        wq = "pool"

        try:
          cut('p0')
          for s in range(NS):
              R0.reset(); R1.reset(); R2.reset()

              def colof(t):
                  return 2 if t < 2 else s

              def xsrc(t):
                  return ctx_d[s, t * 128:(t + 1) * 128, :] if t < 2 else x_d[s, (t - 2) * 128:(t - 1) * 128, :]

              def rstd_from_ss(ps_ap, n, scale, dst, r, w):
                  ACT(dst, ps_ap, AF.Ln, r + [b_c], w, scale=scale, bias=epsc[:, 0:1])
                  ACT(dst, dst, AF.Exp, w, w, scale=-0.5)

              l = 0
              y_st = R0.alloc([8, T], BF16); b_y = bufs(8)
              u_st = R1.alloc([8, T], BF16); b_u = bufs(NT)
              qd = [R1.alloc([T], BF16) for _ in range(2)]; b_qd = [bufs(NT), bufs(NT)]
              ki = [R1.alloc([T], BF16) for _ in range(2)]; b_ki = [bufs(NT), bufs(NT)]
              keT = [R1.alloc([NT, 128], BF16) for _ in range(2)]; b_ke = [bufs(NT), bufs(NT)]
              vT = R1.alloc([T], BF16); b_vT = bufs(NT)
              gate = R1.alloc([T], BF16); b_gate = bufs(NT)
              markR2 = R2.p
              xin = [R2.alloc([1024], F32) for _ in range(2)]; b_xin = bufs(2)
              xt32 = R2.alloc([8, 128], F32); b_xt32 = Buf()
              sqb = R2.alloc([8, 128], BF16); b_sqb = Buf()
              rs_t = R2.alloc([128], F32); b_rs = Buf()
              def load_xT(t, k):
                  sl = k % 2
                  S.dma("sp", xch[sl], xin[sl], xsrc(t), (), [b_xin[sl]])
                  for j in range(8):
                      TR(pb[j // 4][:, (j % 4) * 128:(j % 4 + 1) * 128], xin[sl][:, j * 128:(j + 1) * 128], ident,
                         [b_xin[sl], b_c], [b_pb[j // 4]])

              for t in range(NT):
                  load_xT(t, t)
                  cut('p1a')
                  col = colof(t)
                  for hh in range(2):
                      pv = pb[hh][:, :].rearrange("p (a b) -> p a b", b=128)
                      ACT(sqb[:, hh * 4:(hh + 1) * 4, :], pv, AF.Square, [b_pb[hh]], [b_sqb])
                      CP("dve", xt32[:, hh * 4:(hh + 1) * 4, :], pv, [b_pb[hh]], [b_xt32])
                  cut('p1b')
                  for j in range(8):
                      MM(pb[2][:, 0:128], onesb, sqb[:, j, :], j == 0, j == 7, [b_sqb, b_c], [b_pb[2]])
                  rstd_from_ss(pb[2][:, 0:128], 128, 1.0 / D, rs_t, [b_pb[2]], [b_rs])
                  cut('p1c')
                  TT("dve", xt32, xt32, rs_t.unsqueeze(1).to_broadcast([128, 8, 128]), ALU.mult, [b_xt32, b_rs], [b_xt32])
                  cut('p1d')
                  TT("pool", xt32, xt32, Vv(l, col, 0).unsqueeze(2).to_broadcast([128, 8, 128]), ALU.mult, [b_xt32, b_c], [b_xt32])
                  TT("pool", u_st[:, :, t * 128:(t + 1) * 128], xt32, Vv(l, col, 1).unsqueeze(2).to_broadcast([128, 8, 128]),
                     ALU.add, [b_xt32, b_c], [b_u[t]])

              S.barrier()
              R2.p = markR2
              wb = [R2.alloc([8, 5, 128], BF16) for _ in range(2)]; b_wb = bufs(2)
              wlr = R2.alloc([8, 32], BF16); b_wlr = Buf()
              o_sb = R2.alloc([T], F32); b_o = bufs(NT)
              lrT = R2.alloc([T], BF16, parts=32); b_lr = bufs(NT)
              dtmp = [R2.alloc([16], F32) for _ in range(2)]; b_dt = bufs(2)
              nt_sq = R2.alloc([512], BF16); b_ntsq = Buf()
              nt_r = R2.alloc([512], F32); b_ntr = Buf()
              dcat = [R2.alloc([NT, 5], F32) for _ in range(2)]; b_dc = [bufs(NT), bufs(NT)]
              for d in range(2):
                  MS("pool", dcat[d], 0.0, b_dc[d])
              qt = [R2.alloc([T], BF16) for _ in range(2)]; b_qt = [bufs(NT), bufs(NT)]
              d4 = [R2.alloc([NT], F32) for _ in range(2)]; b_d4 = [bufs(NT), bufs(NT)]
              Dc = [R2.alloc([16], F32) for _ in range(2)]; b_Dc = bufs(2)
              markU = R2.p
              t_qs = R2.alloc([512], F32); b_tqs = Buf()
              t_s = [R2.alloc([512], F32) for _ in range(2)]; b_ts = bufs(2)
              t_g = [R2.alloc([512], F32) for _ in range(2)]; b_tg = bufs(2)
              t_e = [R2.alloc([512], F32) for _ in range(2)]; b_te = bufs(2)
              t_ki = [R2.alloc([512], F32) for _ in range(2)]; b_tki = bufs(2)
              t_ke = [R2.alloc([512], BF16) for _ in range(2)]; b_tke = bufs(2)
              R2.p = markU
              vxm = [R2.alloc([4, 128], BF16) for _ in range(2)]; b_vxm = bufs(2)
              Vx = [[R2.alloc([5, 128], BF16) for _ in range(3)] for _ in range(2)]; b_Vx = [bufs(3), bufs(3)]
              Am = [[R2.alloc([128], BF16) for _ in range(3)] for _ in range(2)]; b_Am = [bufs(3), bufs(3)]
              U32 = [[R2.alloc([4, 128], F32) for _ in range(2)] for _ in range(2)]; b_U32 = [bufs(2), bufs(2)]
              Lb = [[R2.alloc([3, 128], BF16) for _ in range(2)] for _ in range(2)]; b_Lb = [bufs(2), bufs(2)]
              S32 = [R2.alloc([128], F32) for _ in range(2)]; b_S32 = bufs(2)
              Sbf = [[R2.alloc([128], BF16) for _ in range(2)] for _ in range(2)]; b_Sbf = [bufs(2), bufs(2)]

              cut('p1')
              rwv = rwin_d.rearrange("(j p) n -> p j n", p=128)
              wc = [S.chan(), S.chan()]
              wlc = S.chan()
              S.dma(wq, wlc, wlr, rwv[:, :, 3584:3616], (), [b_wlr])

              def load_head_w(hi):
                  sl = hi % 2
                  if hi < 4:
                      for g in range(5):
                          S.dma(wq, wc[sl], wb[sl][:, :, g, :], rwv[:, :, g * 512 + hi * 128: g * 512 + (hi + 1) * 128], (), [b_wb[sl]])
                  else:
                      h = hi - 4
                      S.dma(wq, wc[sl], wb[sl][:, :, 0, 0:64], rwv[:, :, 2560 + h * 64:2560 + (h + 1) * 64], (), [b_wb[sl]])
                      S.dma(wq, wc[sl], wb[sl][:, :, 1, 0:64], rwv[:, :, 2816 + h * 64:2816 + (h + 1) * 64], (), [b_wb[sl]])
                      S.dma(wq, wc[sl], wb[sl][:, :, 3, :], rwv[:, :, 3072 + h * 128:3072 + (h + 1) * 128], (), [b_wb[sl]])
                      S.dma(wq, wc[sl], wb[sl][:, :, 4, :], rwv[:, :, 3616 + h * 128:3616 + (h + 1) * 128], (), [b_wb[sl]])

              blocks = [(i * 512, min(512, T - i * 512)) for i in range(5)]
              order = [list(range(NT)), [1, 0] + list(range(NT - 1, 1, -1))]
              load_head_w(0)
              for hi in range(8):
                  isA = hi < 4
                  h = hi if isA else hi - 4
                  K = 128 if isA else 64
                  sc = 1.0 if isA else 1.0 / 16.0
                  qscale = (128.0 ** -0.5) if isA else (64.0 ** -0.5)
                  sl = hi % 2
                  if hi + 1 < 8:
                      load_head_w(hi + 1)
                  w = wb[sl]
                  for (c0, n) in blocks:
                      ta, tb = c0 // 128, (c0 + n) // 128
                      nch = n // 32
                      tl = list(range(ta, tb))
                      ub = [b_u[t] for t in tl]

                      def proj(g, M, pbi):
                          for j in range(8):
                              MM(pb[pbi][0:M, 0:n], w[:, j, g, 0:M], u_st[:, j, c0:c0 + n], j == 0, j == 7,
                                 ub + [b_wb[sl]], [b_pb[pbi]])
                      if isA:
                          proj(0, 128, 0); proj(1, 128, 1); proj(2, 128, 2); proj(3, 128, 3); proj(4, 128, 4)
                          ACT(t_qs[:, 0:n], pb[0][:, 0:n], AF.Silu, [b_pb[0]], [b_tqs])
                          qsrc = t_qs; qb = [b_tqs]
                          ksrc = []; kb = []
                          ACT(gate[:, c0:c0 + n], pb[4][:, 0:n], AF.Silu, [b_pb[4]], [b_gate[t] for t in tl])
                          for d in range(2):
                              ACT(t_s[d][:, 0:n], pb[1 + d][:, 0:n], AF.Sigmoid, [b_pb[1 + d]], [b_ts[d]])
                              TS("dve", t_s[d][:, 0:n], t_s[d][:, 0:n], omlb[:, d, h:h + 1], lbv[:, d, h:h + 1], ALU.mult, ALU.add,
                                 [b_ts[d], b_c], [b_ts[d]])
                          for d in range(2):
                              ACT(t_g[d][:, 0:n], t_s[d][:, 0:n], AF.Ln, [b_ts[d]], [b_tg[d]])
                              TS("pool", t_s[d][:, 0:n], t_s[d][:, 0:n], -1.0, 1.0, ALU.mult, ALU.add, [b_ts[d]], [b_ts[d]])
                              ksrc.append(t_s[d]); kb.append([b_ts[d]])
                      else:
                          proj(0, 64, 0); proj(1, 64, 1); proj(3, 128, 3); proj(4, 128, 4)
                          if h == 0:
                              for j in range(8):
                                  MM(pb[5][0:32, 0:n], wlr[:, j, :], u_st[:, j, c0:c0 + n], j == 0, j == 7, ub + [b_wlr], [b_pb[5]])
                              CP("act", lrT[:, c0:c0 + n], pb[5][0:32, 0:n], [b_pb[5]], [b_lr[t] for t in tl])
                          qsrc = pb[0]; qb = [b_pb[0]]
                          ksrc = [pb[1], pb[1]]; kb = [[b_pb[1]], [b_pb[1]]]
                          ACT(gate[:, c0:c0 + n], pb[4][:, 0:n], AF.Silu, [b_pb[4]], [b_gate[t] for t in tl])
                          for d in range(2):
                              MM(pb[5][0:64, 0:n], wg2[:, d, h * 64:(h + 1) * 64], lrT[:, c0:c0 + n], True, True,
                                 [b_lr[t] for t in tl] + [b_c], [b_pb[5]])
                              ACT(t_g[d][0:64, 0:n], pb[5][0:64, 0:n], AF.Sigmoid, [b_pb[5], b_c], [b_tg[d]], bias=bg2[:, d, h:h + 1])
                          for d in range(2):
                              ACT(t_g[d][0:64, 0:n], t_g[d][0:64, 0:n], AF.Ln, [b_tg[d]], [b_tg[d]])
                      CP("act", vT[:, c0:c0 + n], pb[3][:, 0:n], [b_pb[3]], [b_vT[t] for t in tl])
                      def dir_ops(d):
                          g_ = t_g[d][0:K, 0:n]
                          gv = g_.rearrange("p (a b) -> p a b", b=32)
                          S.op("dve", lambda e, g_=g_, d=d: e.tensor_tensor_scan(out=t_e[d][0:K, 0:n], data0=smask[0:K, 0:n], data1=g_,
                                                                            initial=0.0, op0=ALU.mult, op1=ALU.add),
                               [b_tg[d], b_c], [b_te[d]])
                          yield
                          pv = t_e[d][0:K, 0:n].rearrange("p (a b) -> p a b", b=32)
                          if d == 0:
                              CP("pool", g_, t_e[d][0:K, 0:n], [b_te[d]], [b_tg[d]])
                              yield
                              tot = gv[:, :, 31:32]
                          else:
                              TT("dve", g_, g_, t_e[d][0:K, 0:n], ALU.subtract, [b_tg[d], b_te[d]], [b_tg[d]])
                              yield
                              TT("dve", gv, gv, pv[:, :, 31:32].to_broadcast([K, nch, 32]), ALU.add, [b_tg[d], b_te[d]], [b_tg[d]])
                              yield
                              tot = gv[:, :, 0:1]
                          dd = dtmp[d][0:K, 0:nch]
                          ACT(dd.unsqueeze(2), tot, AF.Exp, [b_tg[d]], [b_dt[d]], scale=sc)
                          yield
                          ddv = dd.rearrange("p (t c) -> p t c", c=4)
                          if d == 0:
                              CP("pool", dcat[d][0:K, ta:tb, 1:5], ddv, [b_dt[d]], [b_dc[d][t] for t in tl])
                              yield
                          else:
                              for c in range(4):
                                  CP("pool", dcat[d][0:K, ta:tb, 4 - c], ddv[:, :, c], [b_dt[d]], [b_dc[d][t] for t in tl])
                                  yield
                          Dcv = Dc[d][0:K, 0:nch].rearrange("p (t c) -> p t c", c=4)
                          po_ = [0, 1, 2, 3] if d == 0 else [3, 2, 1, 0]
                          MS("dve", Dcv[:, :, po_[0]], 1.0, [b_Dc[d]])
                          yield
                          CP("dve", Dcv[:, :, po_[1]], ddv[:, :, po_[0]], [b_dt[d]], [b_Dc[d]])
                          yield
                          TT("dve", Dcv[:, :, po_[2]], Dcv[:, :, po_[1]], ddv[:, :, po_[1]], ALU.mult, [b_dt[d], b_Dc[d]], [b_Dc[d]])
                          yield
                          TT("dve", Dcv[:, :, po_[3]], Dcv[:, :, po_[2]], ddv[:, :, po_[2]], ALU.mult, [b_dt[d], b_Dc[d]], [b_Dc[d]])
                          yield
                          TT("dve", d4[d][0:K, ta:tb], Dcv[:, :, po_[3]], ddv[:, :, po_[3]], ALU.mult, [b_dt[d], b_Dc[d]], [b_d4[d][t] for t in tl])
                          yield
                          ACT(t_e[d][0:K, 0:n], g_, AF.Exp, [b_tg[d]], [b_te[d]], scale=sc)
                          yield
                          STT("dve", qd[d][0:K, c0:c0 + n], qsrc[0:K, 0:n], qscale, t_e[d][0:K, 0:n], ALU.mult, ALU.mult,
                              qb + [b_te[d]], [b_qd[d][t] for t in tl])
                          yield
                          TT("dve", t_ki[d][0:K, 0:n].rearrange("p (a b) -> p a b", b=32), t_e[d][0:K, 0:n].rearrange("p (a b) -> p a b", b=32),
                             Dc[d][0:K, 0:nch].unsqueeze(2).to_broadcast([K, nch, 32]), ALU.mult, [b_te[d], b_Dc[d]], [b_tki[d]])
                          yield
                          STT("dve", qt[d][0:K, c0:c0 + n], qsrc[0:K, 0:n], qscale, t_ki[d][0:K, 0:n], ALU.mult, ALU.mult,
                              qb + [b_tki[d]], [b_qt[d][t] for t in tl])
                          yield
                          ACT(t_e[d][0:K, 0:n], g_, AF.Exp, [b_tg[d]], [b_te[d]], scale=-sc)
                          yield
                          TT("dve", t_ki[d][0:K, 0:n], ksrc[d][0:K, 0:n], t_e[d][0:K, 0:n], ALU.mult, kb[d] + [b_te[d]], [b_tki[d]])
                          yield
                          CP("pool", ki[d][0:K, c0:c0 + n], t_ki[d][0:K, 0:n], [b_tki[d]], [b_ki[d][t] for t in tl])
                          yield
                          TT("pool", t_ke[d][0:K, 0:n].rearrange("p (a b) -> p a b", b=32),
                             t_ki[d][0:K, 0:n].rearrange("p (a b) -> p a b", b=32),
                             dd.unsqueeze(2).to_broadcast([K, nch, 32]), ALU.mult,
                             [b_tki[d], b_dt[d]], [b_tke[d]])
                          yield
                          for ti, t in enumerate(tl):
                              TR(pq[0][:, (d * 4 + ti) * 128:(d * 4 + ti) * 128 + K], t_ke[d][0:K, ti * 128:(ti + 1) * 128], identb[0:K, 0:K],
                                 [b_tke[d], b_c], [b_pq[0]])
                              yield
                          nt_ = len(tl)
                          CP("act", keT[d][:, ta:tb, 0:K],
                             pq[0][:, d * 512:d * 512 + nt_ * 128].rearrange("p (a b) -> p a b", b=128)[:, :, 0:K],
                             [b_pq[0]], [b_ke[d][t] for t in tl])
                          yield
                      gens_ = [dir_ops(0), dir_ops(1)]
                      while gens_:
                          for g__ in list(gens_):
                              try:
                                  next(g__)
                              except StopIteration:
                                  gens_.remove(g__)
                  cut('h%dp1' % hi)
                  S.barrier()
                  for d in range(2):
                      MS("dve", S32[d], 0.0, [b_S32[d]])
                      MS("dve", Sbf[d][0], 0.0, [b_Sbf[d][0]])
                  visited = set()

                  def prepA(k):
                      ts_ = [order[d][k] for d in range(2)]
                      b3 = k % 3
                      for d in range(2):
                          t = ts_[d]
                          TT("pool", vxm[d], mexp[:, d], vT[:, t * 128:(t + 1) * 128].unsqueeze(1).to_broadcast([128, 4, 128]), ALU.mult,
                             [b_vT[t], b_c], [b_vxm[d]])
                      for d in range(2):
                          t = ts_[d]
                          cs = slice(t * 128, (t + 1) * 128)
                          for c in range(4):
                              TR(pq[d][:, c * 128:(c + 1) * 128], vxm[d][:, c, :], identb, [b_vxm[d], b_c], [b_pq[d]])
                          TR(pq[d][:, 512:640], vT[:, cs], identb, [b_vT[t], b_c], [b_pq[d]])
                          MM(pb[2 + d][:, 0:128], ki[d][0:K, cs], qd[d][0:K, cs], True, True,
                             [b_ki[d][t], b_qd[d][t]], [b_pb[2 + d]])
                      for d in range(2):
                          CP("act", Vx[d][b3], pq[d][:, 0:640].rearrange("p (a b) -> p a b", b=128), [b_pq[d]], [b_Vx[d][b3]])
                          TT("dve", Am[d][b3], pb[2 + d][:, 0:128], mintra[:, d, :], ALU.mult, [b_pb[2 + d], b_c], [b_Am[d][b3]])

                  def prepB(k):
                      ts_ = [order[d][k] for d in range(2)]
                      b3 = k % 3
                      bf_ = k % 2
                      for d in range(2):
                          t = ts_[d]
                          MM(pb[d][0:K, :], keT[d][:, t, 0:K], Vx[d][b3][:, 0:4, :].rearrange("p a b -> p (a b)"), True, True,
                             [b_ke[d][t], b_Vx[d][b3]], [b_pb[d]])
                      for d in range(2):
                          CP("act", U32[d][bf_][0:K].rearrange("p a b -> p (a b)"), pb[d][0:K, :], [b_pb[d]], [b_U32[d][bf_]])
                      for s_ in (1, 2, 3):
                          for d in range(2):
                              t = ts_[d]
                              U_ = U32[d][bf_]
                              STT("dve", U_[0:K, s_, :], U_[0:K, s_ - 1, :], dcat[d][0:K, t, s_ + 1:s_ + 2], U_[0:K, s_, :], ALU.mult, ALU.add,
                                  [b_U32[d][bf_], b_dc[d][t]], [b_U32[d][bf_]])
                      for d in range(2):
                          CP("act", Lb[d][bf_][0:K].rearrange("p a b -> p (a b)"), U32[d][bf_][0:K, 0:3, :].rearrange("p a b -> p (a b)"),
                             [b_U32[d][bf_]], [b_Lb[d][bf_]])

                  def chain(k):
                      ts_ = [order[d][k] for d in range(2)]
                      b3 = k % 3
                      bf_ = k % 2
                      cur = k % 2
                      for d in range(2):
                          t = ts_[d]
                          STT("dve", S32[d][0:K], S32[d][0:K], d4[d][0:K, t:t + 1], U32[d][bf_][0:K, 3, :], ALU.mult, ALU.add,
                              [b_S32[d], b_d4[d][t], b_U32[d][bf_]], [b_S32[d]])
                      if k + 1 < NT:
                          for d in range(2):
                              CP("act", Sbf[d][1 - cur][0:K], S32[d][0:K], [b_S32[d]], [b_Sbf[d][1 - cur]])
                      for d in range(2):
                          t = ts_[d]
                          cs = slice(t * 128, (t + 1) * 128)
                          po = pb[4 + d][:, 0:128]
                          MM(po, Vx[d][b3][:, 4, :], Am[d][b3], True, False, [b_Vx[d][b3], b_Am[d][b3]], [b_pb[4 + d]])
                          for s_ in (1, 2, 3):
                              c = s_ if d == 0 else 3 - s_
                              MM(po[:, c * 32:(c + 1) * 32], Lb[d][bf_][0:K, s_ - 1, :], qd[d][0:K, t * 128 + c * 32:t * 128 + (c + 1) * 32],
                                 False, False, [b_Lb[d][bf_], b_qd[d][t]], [b_pb[4 + d]])
                          MM(po, Sbf[d][cur][0:K], qt[d][0:K, cs], False, True, [b_Sbf[d][cur], b_qt[d][t]], [b_pb[4 + d]])
                      for d in range(2):
                          t = ts_[d]
                          cs = slice(t * 128, (t + 1) * 128)
                          po = pb[4 + d][:, 0:128]
                          if t not in visited:
                              CP("dve", o_sb[:, cs], po, [b_pb[4 + d]], [b_o[t]])
                              visited.add(t)
                          else:
                              TT("dve", o_sb[:, cs], o_sb[:, cs], po, ALU.add, [b_o[t], b_pb[4 + d]], [b_o[t]])

                  prepA(0); prepA(1); prepB(0)
                  for k in range(NT):
                      if k + 2 < NT:
                          prepA(k + 2)
                      if k + 1 < NT:
                          prepB(k + 1)
                      chain(k)
                  cut('h%dp2' % hi)
                  S.barrier()
                  for (c0, n) in blocks:
                      tl = list(range(c0 // 128, (c0 + n) // 128))
                      ob = [b_o[t] for t in tl]
                      ACT(nt_sq[:, 0:n], o_sb[:, c0:c0 + n], AF.Square, ob, [b_ntsq])
                      MM(pb[4][:, 0:n], onesb, nt_sq[:, 0:n], True, True, [b_ntsq, b_c], [b_pb[4]])
                      rstd_from_ss(pb[4][:, 0:n], n, 1.0 / 128.0, nt_r[:, 0:n], [b_pb[4]], [b_ntr])
                      TT("dve", nt_r[:, 0:n], nt_r[:, 0:n], o_sb[:, c0:c0 + n], ALU.mult, [b_ntr] + ob, [b_ntr])
                      STT("dve", y_st[:, hi, c0:c0 + n], nt_r[:, 0:n], gnv[:, (0 if isA else 1):(1 if isA else 2)], gate[:, c0:c0 + n],
                          ALU.mult, ALU.mult, [b_ntr, b_c] + [b_gate[t] for t in tl], [b_y[hi]])

              cut('heads')
              S.barrier()
              R1.reset(); R2.reset()
              x_fm = R1.alloc([8, T], F32); b_x = bufs(NT)
              wo_sb = R2.alloc([8, 1024], BF16); b_wo = Buf()
              yo32 = R2.alloc([8, 128], F32); b_yo = Buf()
              sq2 = R2.alloc([8, 128], BF16); b_sq2 = Buf()
              rs2 = R2.alloc([128], F32); b_rs2 = Buf()
              xin = [R2.alloc([1024], F32) for _ in range(2)]; b_xin = bufs(2)
              wch2 = S.chan()
              S.dma(wq, wch2, wo_sb, rwout_d.rearrange("(j p) n -> p j n", p=128), (), [b_wo])
              for t in range(NT):
                  col = colof(t)
                  cs = slice(t * 128, (t + 1) * 128)
                  for f in range(8):
                      pbt = pb[2 + f % 2]
                      for j in range(8):
                          MM(pbt[:, 0:128], wo_sb[:, j, f * 128:(f + 1) * 128], y_st[:, j, cs], j == 0, j == 7,
                             [b_wo, b_y[j]], [b_pb[2 + f % 2]])
                      ACT(sq2[:, f, :], pbt[:, 0:128], AF.Square, [b_pb[2 + f % 2]], [b_sq2])
                      CP("dve", yo32[:, f, :], pbt[:, 0:128], [b_pb[2 + f % 2]], [b_yo])
                  for f in range(8):
                      MM(pb[4][:, 0:128], onesb, sq2[:, f, :], f == 0, f == 7, [b_sq2, b_c], [b_pb[4]])
                  rstd_from_ss(pb[4][:, 0:128], 128, 1.0 / D, rs2, [b_pb[4]], [b_rs2])
                  TT("dve", yo32, yo32, rs2.unsqueeze(1).to_broadcast([128, 8, 128]), ALU.mult, [b_yo, b_rs2], [b_yo])
                  TT("pool", yo32, yo32, Vv(l, col, 2).unsqueeze(2).to_broadcast([128, 8, 128]), ALU.mult, [b_yo, b_c], [b_yo])
                  sl = t % 2
                  S.dma("sp", xch[sl], xin[sl], xsrc(t), (), [b_xin[sl]])
                  for j in range(8):
                      TR(pb[j // 4][:, (j % 4) * 128:(j % 4 + 1) * 128], xin[sl][:, j * 128:(j + 1) * 128], ident,
                         [b_xin[sl], b_c], [b_pb[j // 4]])
                  for hh in range(2):
                      TT("dve", x_fm[:, hh * 4:(hh + 1) * 4, cs], yo32[:, hh * 4:(hh + 1) * 4, :],
                         pb[hh][:, :].rearrange("p (a b) -> p a b", b=128), ALU.add, [b_yo, b_pb[hh]], [b_x[t]])
              S.barrier()
              if dbg == "mix0":
                  S.dma("sp", dch, dbg_d, x_fm, [b for b in b_x], ())
                  S.barrier()
                  break

              def xb_of(c0, n):
                  return [b_x[t] for t in range(c0 // 128, (c0 + n + 127) // 128)]

              def prenorm_block(c0, n, vg, vs, dst, dstb, sqt, b_sqt, tmp, b_tmp, rs, b_rsb, pbi):
                  xs = x_fm[:, :, c0:c0 + n]
                  xb = xb_of(c0, n)
                  ACT(sqt[:, :, 0:n], xs, AF.Square, xb, [b_sqt])
                  for j in range(8):
                      MM(pb[pbi][:, 0:n], onesb, sqt[:, j, 0:n], j == 0, j == 7, [b_sqt, b_c], [b_pb[pbi]])
                  rstd_from_ss(pb[pbi][:, 0:n], n, 1.0 / D, rs[:, 0:n], [b_pb[pbi]], [b_rsb])
                  TT("dve", tmp[:, :, 0:n], xs, rs[:, 0:n].unsqueeze(1).to_broadcast([128, 8, n]), ALU.mult, xb + [b_rsb], [b_tmp])
                  TT("pool", tmp[:, :, 0:n], tmp[:, :, 0:n], vg.unsqueeze(2).to_broadcast([128, 8, n]), ALU.mult, [b_tmp, b_c], [b_tmp])
                  TT("pool", dst, tmp[:, :, 0:n], vs.unsqueeze(2).to_broadcast([128, 8, n]), ALU.add, [b_tmp, b_c], dstb)

              def post_block(c0, n, vgate, yo, b_yo_, sq, b_sq_, rs, b_rsb, pbi):
                  xb = xb_of(c0, n)
                  for f in range(8):
                      MM(pb[pbi][:, 0:n], onesb, sq[:, f, 0:n], f == 0, f == 7, [b_sq_, b_c], [b_pb[pbi]])
                  rstd_from_ss(pb[pbi][:, 0:n], n, 1.0 / D, rs[:, 0:n], [b_pb[pbi]], [b_rsb])
                  TT("dve", yo[:, :, 0:n], yo[:, :, 0:n], rs[:, 0:n].unsqueeze(1).to_broadcast([128, 8, n]), ALU.mult, [b_yo_, b_rsb], [b_yo_])
                  TT("pool", yo[:, :, 0:n], yo[:, :, 0:n], vgate.unsqueeze(2).to_broadcast([128, 8, n]), ALU.mult, [b_yo_, b_c], [b_yo_])
                  TT("dve", x_fm[:, :, c0:c0 + n], x_fm[:, :, c0:c0 + n], yo[:, :, 0:n], ALU.add, xb + [b_yo_], xb)

              def ffn(l, groups):
                  R0.reset(); R2.reset()
                  GT = 768
                  u2 = R0.alloc([8, GT], BF16); b_u2 = Buf()
                  yo = R0.alloc([8, GT], F32); b_yo_ = Buf()
                  hh_ = R2.alloc([22, GT], BF16); b_h = bufs(22)
                  sq = R2.alloc([8, GT], BF16); b_sq_ = Buf()
                  wi = [R2.alloc([8, 256], BF16) for _ in range(2)]; b_wi = bufs(2)
                  wo2 = [R2.alloc([22, 128], BF16) for _ in range(2)]; b_wo2 = bufs(2)
                  sg = [R2.alloc([512], F32) for _ in range(2)]; b_sg = bufs(2)
                  rs = R2.alloc([512], F32); b_rsb = Buf()
                  wic = [S.chan(), S.chan()]
                  woc = [S.chan(), S.chan()]
                  fwv = fwin_d[l].rearrange("(j p) n -> p j n", p=128)
                  fov = fwout_d[l].rearrange("(c p) n -> p c n", p=128)
                  for grp in groups:
                      offs = []
                      o_ = 0
                      for (c0, n, col) in grp:
                          offs.append(o_)
                          o_ += n
                      for (c0, n, col), off in zip(grp, offs):
                          prenorm_block(c0, n, Vv(l, col, 3), Vv(l, col, 4), u2[:, :, off:off + n], [b_u2],
                                        sq, b_sq_, yo, b_yo_, rs, b_rsb, 4)
                      cut('f_pre')
                      k = 0
                      for c in range(22):
                          if c == 1:
                              cut('f_h0')
                          sl = c % 2
                          S.dma(wq, wic[sl], wi[sl][:, :, 0:128], fwv[:, :, c * 128:(c + 1) * 128], (), [b_wi[sl]])
                          S.dma(wq, wic[sl], wi[sl][:, :, 128:256], fwv[:, :, FH + c * 128:FH + (c + 1) * 128], (), [b_wi[sl]])
                          for (c0, n, col), off in zip(grp, offs):
                              pa, pu = (0, 1) if k % 2 == 0 else (2, 3)
                              for j in range(8):
                                  MM(pb[pa][:, 0:n], wi[sl][:, j, 0:128], u2[:, j, off:off + n], j == 0, j == 7, [b_wi[sl], b_u2], [b_pb[pa]])
                              for j in range(8):
                                  MM(pb[pu][:, 0:n], wi[sl][:, j, 128:256], u2[:, j, off:off + n], j == 0, j == 7, [b_wi[sl], b_u2], [b_pb[pu]])
                              ACT(sg[k % 2][:, 0:n], pb[pa][:, 0:n], AF.Silu, [b_pb[pa]], [b_sg[k % 2]])
                              TT("dve", hh_[:, c, off:off + n], sg[k % 2][:, 0:n], pb[pu][:, 0:n], ALU.mult, [b_sg[k % 2], b_pb[pu]], [b_h[c]])
                              k += 1
                      cut('f_hid')
                      k = 0
                      for f in range(8):
                          if f == 1:
                              cut('f_o0')
                          sl = f % 2
                          S.dma(wq, woc[sl], wo2[sl], fov[:, :, f * 128:(f + 1) * 128], (), [b_wo2[sl]])
                          for (c0, n, col), off in zip(grp, offs):
                              pi = k % 2
                              for c in range(22):
                                  MM(pb[pi][:, 0:n], wo2[sl][:, c, :], hh_[:, c, off:off + n], c == 0, c == 21, [b_wo2[sl], b_h[c]], [b_pb[pi]])
                              ACT(sq[:, f, off:off + n], pb[pi][:, 0:n], AF.Square, [b_pb[pi]], [b_sq_])
                              CP("dve", yo[:, f, off:off + n], pb[pi][:, 0:n], [b_pb[pi]], [b_yo_])
                              k += 1
                      cut('f_out')
                      for (c0, n, col), off in zip(grp, offs):
                          post_block(c0, n, Vv(l, col, 5), yo[:, :, off:off + n], b_yo_, sq[:, :, off:off + n], b_sq_, rs, b_rsb, 4)
                          cut('f_post')
                  cut('f_all')
                  S.barrier()

              ffn(0, [[(0, 256, 2), (256, 512, s)], [(768, 512, s), (1280, 256, s)], [(1536, 512, s), (2048, 256, s)]])
              if dbg == "ffn0":
                  S.dma("sp", dch, dbg_d, x_fm, [b for b in b_x], ())
                  S.barrier()
                  break

              l = 1
              R0.reset(); R2.reset()
              u_st = R0.alloc([8, T], BF16); b_u1 = Buf()
              y1 = R2.alloc([8, TL], BF16); b_y1 = bufs(8)
              markA = R2.p
              sqt = R2.alloc([8, 512], BF16); b_sqt = Buf()
              tmpn = R2.alloc([8, 512], F32); b_tmpn = Buf()
              rsn = R2.alloc([512], F32); b_rsn = Buf()
              for (c0, n, col) in [(0, 256, 2)] + [(256 + 512 * i, 512, s) for i in range(4)]:
                  prenorm_block(c0, n, Vv(l, col, 0), Vv(l, col, 1), u_st[:, :, c0:c0 + n], [b_u1], sqt, b_sqt, tmpn, b_tmpn, rsn, b_rsn, 4)
              S.barrier()
              R2.p = markA
              cosT = R2.alloc([TL], F32, parts=64); sinT = R2.alloc([TL], F32, parts=64); b_tab = Buf()
              tch = S.chan()
              S.dma("sp", tch, cosT, cos_d, (), [b_tab])
              S.dma("sp", tch, sinT, sin_d, (), [b_tab])
              kT = R2.alloc([T], BF16, parts=64); b_kT = Buf()
              vv = R2.alloc([NT, 128], BF16); b_vv = Buf()
              qT = R2.alloc([TL], BF16, parts=64); b_qT = Buf()
              wk = R2.alloc([8, 2, 64], BF16); b_wk = Buf()
              wv2 = R2.alloc([8, 128], BF16); b_wv2 = Buf()
              wqb = [R2.alloc([8, 2, 64], BF16) for _ in range(2)]; b_wqb = bufs(2)
              ta1 = R2.alloc([512], F32, parts=64); b_ta1 = Buf()
              ta2 = R2.alloc([512], F32, parts=64); b_ta2 = Buf()
              sqa = R2.alloc([512], BF16, parts=64); b_sqa = Buf()
              Pc = [R2.alloc([512], BF16) for _ in range(4)]; b_Pc = bufs(4)
              pkc = [0]
              rden = R2.alloc([512], F32); b_rden = Buf()
              sm = R2.alloc([16], F32); b_sm = Buf()
              wkc = S.chan(); wvc = S.chan(); wqc = [S.chan(), S.chan()]
              wqv = wqkv_d.rearrange("(j p) n -> p j n", p=128)
              wsv = wqks_d.rearrange("(j p) n -> p j n", p=128)
              lat_blocks = [(256 + 512 * i, 512) for i in range(4)]

              def load_q_w(hq):
                  sl = hq % 2
                  S.dma(wq, wqc[sl], wqb[sl][:, :, 0, :], wqv[:, :, hq * 64:(hq + 1) * 64], (), [b_wqb[sl]])
                  S.dma(wq, wqc[sl], wqb[sl][:, :, 1, :], wsv[:, :, hq * 64:(hq + 1) * 64], (), [b_wqb[sl]])

              rp_cnt = [0]

              def rope_proj(wt, b_wt, c0, n, dstT, b_dst, scale):
                  ba = (rp_cnt[0] % 2) * 2
                  rp_cnt[0] += 1
                  for j in range(8):
                      MM(pb[ba][0:64, 0:n], wt[:, j, 0, :], u_st[:, j, c0:c0 + n], j == 0, j == 7, [b_wt, b_u1], [b_pb[ba]])
                  for j in range(8):
                      MM(pb[ba + 1][0:64, 0:n], wt[:, j, 1, :], u_st[:, j, c0:c0 + n], j == 0, j == 7, [b_wt, b_u1], [b_pb[ba + 1]])
                  lc = c0 - 256
                  TT("dve", ta1[:, 0:n], pb[ba][0:64, 0:n], cosT[:, lc:lc + n], ALU.mult, [b_pb[ba], b_tab], [b_ta1])
                  TT("dve", ta2[:, 0:n], pb[ba + 1][0:64, 0:n], sinT[:, lc:lc + n], ALU.mult, [b_pb[ba + 1], b_tab], [b_ta2])
                  TT("dve", ta1[:, 0:n], ta1[:, 0:n], ta2[:, 0:n], ALU.add, [b_ta1, b_ta2], [b_ta1])
                  ACT(dstT, ta1[:, 0:n], AF.Copy, [b_ta1], [b_dst], scale=scale)

              def sqmax(srcT, b_src, c0, n, acc_col, first):
                  ACT(sqa[:, 0:n], srcT, AF.Square, [b_src], [b_sqa])
                  MM(pb[5][:, 0:n], onesb[0:64, :], sqa[:, 0:n], True, True, [b_sqa, b_c], [b_pb[5]])
                  if first:
                      S.op("dve", lambda e: e.reduce_max(out=sm[:, acc_col:acc_col + 1], in_=pb[5][:, 0:n], axis=mybir.AxisListType.X), [b_pb[5]], [b_sm])
                  else:
                      S.op("dve", lambda e: e.reduce_max(out=sm[:, 2:3], in_=pb[5][:, 0:n], axis=mybir.AxisListType.X), [b_pb[5]], [b_sm])
                      TT("dve", sm[:, acc_col:acc_col + 1], sm[:, acc_col:acc_col + 1], sm[:, 2:3], ALU.max, [b_sm], [b_sm])

              load_q_w(0)
              for g in range(4):
                  S.dma(wq, wkc, wk[:, :, 0, :], wqv[:, :, 1024 + g * 64:1024 + (g + 1) * 64], (), [b_wk])
                  S.dma(wq, wkc, wk[:, :, 1, :], wsv[:, :, 1024 + g * 64:1024 + (g + 1) * 64], (), [b_wk])
                  S.dma(wq, wvc, wv2[:, :, 0:64], wqv[:, :, 1280 + g * 64:1280 + (g + 1) * 64], (), [b_wv2])
                  S.dma(wq, wvc, wv2[:, :, 64:128], wqv[:, :, 1280 + g * 64:1280 + (g + 1) * 64], (), [b_wv2])
                  for j in range(8):
                      MM(pb[0][0:64, 0:256], wk[:, j, 0, :], u_st[:, j, 0:256], j == 0, j == 7, [b_wk, b_u1], [b_pb[0]])
                  CP("act", kT[:, 0:256], pb[0][0:64, 0:256], [b_pb[0]], [b_kT])
                  sqmax(kT[:, 0:256], b_kT, 0, 256, 0, True)
                  for (c0, n) in lat_blocks:
                      rope_proj(wk, b_wk, c0, n, kT[:, c0:c0 + n], b_kT, 1.0)
                      sqmax(kT[:, c0:c0 + n], b_kT, c0, n, 0, False)
                  for t in range(NT):
                      pi = 3 + t % 2
                      for j in range(8):
                          MM(pb[pi][:, 0:128], u_st[:, j, t * 128:(t + 1) * 128], wv2[:, j, :], j == 0, j == 7, [b_wv2, b_u1], [b_pb[pi]])
                      CP("act", vv[:, t, :], pb[pi][:, 0:128], [b_pb[pi]], [b_vv])
                  for hq in range(4 * g, 4 * g + 4):
                      sl = hq % 2
                      if hq + 1 < 16:
                          load_q_w(hq + 1)
                      for bi, (c0, n) in enumerate(lat_blocks):
                          rope_proj(wqb[sl], b_wqb[sl], c0, n, qT[:, c0 - 256:c0 - 256 + n], b_qT, 0.125)
                          sqmax(qT[:, c0 - 256:c0 - 256 + n], b_qT, c0, n, 1, bi == 0)
                      TT("dve", sm[:, 3:4], sm[:, 0:1], sm[:, 1:2], ALU.mult, [b_sm], [b_sm])
                      ACT(sm[:, 3:4], sm[:, 3:4], AF.Ln, [b_sm], [b_sm])
                      ACT(sm[:, 3:4], sm[:, 3:4], AF.Exp, [b_sm], [b_sm], scale=0.5)
                      TT("dve", sm[:, 3:4], sm[:, 3:4], sinkB[:, hq:hq + 1], ALU.max, [b_sm, b_c], [b_sm])
                      TS("dve", sm[:, 4:5], sm[:, 3:4], -1.0, 0.0, ALU.mult, ALU.add, [b_sm], [b_sm])
                      ACT(sm[:, 5:6], sinkB[:, hq:hq + 1], AF.Exp, [b_sm, b_c], [b_sm], bias=sm[:, 4:5])
                      negM = sm[:, 4:5]
                      for Q in range(4):
                          n0 = Q * 4
                          qc = slice(Q * 512, (Q + 1) * 512)
                          jbs = [jb for jb in range(n0 - 1, n0 + 5) if 0 <= jb < 16]
                          tasks = [("c", cb) for cb in range(2)] + [("b", jb) for jb in jbs]
                          slots = {}

                          def emit_score(i):
                              kind, x_ = tasks[i]
                              bi = pkc[0] % 4
                              pkc[0] += 1
                              slots[i] = bi
                              ps_ = pb[bi]; bps = b_pb[bi]; P_ = Pc[bi]; bP = b_Pc[bi]
                              if kind == "c":
                                  MM(ps_[:, 0:512], kT[:, x_ * 128:(x_ + 1) * 128], qT[:, qc], True, True, [b_kT, b_qT], [bps])
                                  ACT(P_[:, 0:512], ps_[:, 0:512], AF.Exp, [bps, b_sm], [bP], bias=negM)
                              else:
                                  jb = x_
                                  qa = max(jb - 1, n0); qe = min(jb + 1, n0 + 3)
                                  w_ = (qe - qa + 1) * 128
                                  moff = (qa - (jb - 1)) * 128
                                  MM(ps_[:, 0:w_], kT[:, 256 + jb * 128:256 + (jb + 1) * 128], qT[:, qa * 128:(qe + 1) * 128], True, True, [b_kT, b_qT], [bps])
                                  ACT(P_[:, 0:w_], ps_[:, 0:w_], AF.Exp, [bps, b_sm], [bP], bias=negM)
                                  TT("dve", P_[:, 0:w_], P_[:, 0:w_], band[:, moff:moff + w_], ALU.mult, [bP, b_c], [bP])

                          def emit_pv(i):
                              kind, x_ = tasks[i]
                              bi = slots[i]
                              P_ = Pc[bi]; bP = b_Pc[bi]
                              if kind == "c":
                                  MM(pb[4][:, 0:512], vv[:, x_, :], P_[:, 0:512], i == 0, False, [b_vv, bP], [b_pb[4]])
                                  MM(pb[5][:, 0:512], onesb, P_[:, 0:512], i == 0, False, [b_c, bP], [b_pb[5]])
                              else:
                                  jb = x_
                                  qa = max(jb - 1, n0); qe = min(jb + 1, n0 + 3)
                                  last = i == len(tasks) - 1
                                  for nb in range(qa, qe + 1):
                                      oc = slice((nb - n0) * 128, (nb - n0 + 1) * 128)
                                      pc_ = slice((nb - qa) * 128, (nb - qa + 1) * 128)
                                      lst = last and nb == qe
                                      MM(pb[4][:, oc], vv[:, 2 + jb, :], P_[:, pc_], False, lst, [b_vv, bP], [b_pb[4]])
                                      MM(pb[5][:, oc], onesb, P_[:, pc_], False, lst, [b_c, bP], [b_pb[5]])

                          LA = 2
                          for i in range(len(tasks) + LA):
                              if i < len(tasks):
                                  emit_score(i)
                              if i - LA >= 0:
                                  emit_pv(i - LA)
                          ACT(rden[:, 0:512], pb[5][:, 0:512], AF.Ln, [b_pb[5], b_sm], [b_rden], bias=sm[:, 5:6])
                          ACT(rden[:, 0:512], rden[:, 0:512], AF.Exp, [b_rden], [b_rden], scale=-1.0)
                          hp = (hq % 2) * 64
                          TT("dve", y1[hp:hp + 64, hq // 2, qc], pb[4][hp:hp + 64, 0:512], rden[hp:hp + 64, 0:512], ALU.mult,
                             [b_pb[4], b_rden], [b_y1[hq // 2]])
              S.barrier()
              R0.reset(); R2.p = markA
              wo1 = R0.alloc([8, 1024], BF16); b_wo1 = Buf()
              yo1 = R2.alloc([8, 512], F32); b_yo1 = Buf()
              sq1 = R2.alloc([8, 512], BF16); b_sq1 = Buf()
              rs1 = R2.alloc([512], F32); b_rs1 = Buf()
              woc1 = S.chan()
              S.dma(wq, woc1, wo1, wo_d.rearrange("(j p) n -> p j n", p=128), (), [b_wo1])
              k = 0
              for (c0, n) in lat_blocks:
                  lc = c0 - 256
                  for f in range(8):
                      pi = k % 2; k += 1
                      for j in range(8):
                          MM(pb[pi][:, 0:n], wo1[:, j, f * 128:(f + 1) * 128], y1[:, j, lc:lc + n], j == 0, j == 7, [b_wo1, b_y1[j]], [b_pb[pi]])
                      ACT(sq1[:, f, 0:n], pb[pi][:, 0:n], AF.Square, [b_pb[pi]], [b_sq1])
                      CP("dve", yo1[:, f, 0:n], pb[pi][:, 0:n], [b_pb[pi]], [b_yo1])
                  post_block(c0, n, Vv(l, s, 2), yo1, b_yo1, sq1, b_sq1, rs1, b_rs1, 4)
              S.barrier()
              if dbg == "mix1":
                  S.dma("sp", dch, dbg_d, x_fm, [b for b in b_x], ())
                  S.barrier()
                  break
              ffn(1, [[(256, 512, s), (768, 256, s)], [(1024, 512, s), (1536, 256, s)], [(1792, 512, s)]])
              if dbg == "ffn1":
                  S.dma("sp", dch, dbg_d, x_fm, [b for b in b_x], ())
                  S.barrier()
                  break
              R0.reset(); R2.reset()
              ot = [R2.alloc([1024], F32) for _ in range(2)]; b_ot = bufs(2)
              for t in range(2, NT):
                  sl = t % 2
                  for j in range(8):
                      TR(pb[sl * 2 + j // 4][:, (j % 4) * 128:(j % 4 + 1) * 128], x_fm[:, j, t * 128:(t + 1) * 128], ident,
                         [b_x[t], b_c], [b_pb[sl * 2 + j // 4]])
                  CP("act", ot[sl][:, 0:512], pb[sl * 2][:, :], [b_pb[sl * 2]], [b_ot[sl]])
                  CP("dve", ot[sl][:, 512:1024], pb[sl * 2 + 1][:, :], [b_pb[sl * 2 + 1]], [b_ot[sl]])
                  S.dma("sp", och[sl], out_d[s, (t - 2) * 128:(t - 1) * 128, :], ot[sl], [b_ot[sl]], ())
              S.barrier()

        except StopBuild:
            pass
        S.barrier()
        for e in ("sp",):
            for c in S.chans:
                if c.cnt:
                    S.eng[e].wait_ge(c.sem, 16 * c.cnt)
    return nc


def host_prep(inputs, core, NS=2):
    f = np.float32
    b0 = core * NS
    x = np.ascontiguousarray(inputs["x"][b0:b0 + NS]).astype(f)
    ctx = np.ascontiguousarray(inputs["ctx"][b0:b0 + NS]).astype(f)
    c = inputs["c"][b0:b0 + NS]
    cols = [c[0], c[min(1, NS - 1)], inputs["c_ctx"]]
    cT = np.stack([np.asarray(v, f).reshape(8, 128).T for v in cols], axis=-1)
    ada_bT = np.asarray(inputs["ada_b"], f).reshape(2, 48, 128).transpose(2, 0, 1)
    ngT = np.asarray(inputs["norm_g"], f).reshape(2, 4, 8, 128).transpose(3, 0, 1, 2)
    lbT = np.asarray(inputs["rec_lb_logits"], f).reshape(2, 2, 4, 128).transpose(3, 0, 1, 2)
    wg2 = np.asarray(inputs["rec_w_g2"], f)[0]
    wg2p = np.zeros((32, 2, 256), f)
    wg2p[0:16, 0, :] = wg2[0]
    wg2p[16:32, 1, :] = wg2[1]
    bg2T = np.asarray(inputs["rec_b_g2"], f)[0].reshape(2, 4, 64).transpose(2, 0, 1)
    gnT = np.stack([np.asarray(inputs["rec_gn_a"], f)[0], np.asarray(inputs["rec_gn_b"], f)[0]], axis=-1)
    wqkv = np.asarray(inputs["att_w_qkv"], f)[0]
    qk = wqkv[:, :1280].reshape(1024, 640, 2)[:, :, ::-1].reshape(1024, 1280)
    sinkB = np.broadcast_to(np.asarray(inputs["att_sink"], f)[0][None, :], (128, 16))
    n_rows = TL // 64
    row = np.repeat(np.arange(n_rows), 64).astype(f)
    colp = np.tile(np.arange(64), n_rows).astype(f)
    inv = (np.float32(10000.0) ** (-np.arange(0, 32, 2, dtype=f) / np.float32(32))).astype(f)
    ang = np.concatenate([row[:, None] * inv, colp[:, None] * inv], axis=-1).astype(f)
    cosT = np.repeat(np.cos(ang).astype(f).T, 2, axis=0)
    sinv = np.sin(ang).astype(f).T
    sinT = np.empty((64, TL), f)
    sinT[0::2] = -sinv
    sinT[1::2] = sinv
    jj = np.arange(128)[:, None]; ii = np.arange(128)[None, :]
    same = (jj // 32) == (ii // 32)
    mask_intra = np.stack([(same & (jj <= ii)), (same & (jj >= ii))], axis=1).astype(f)
    mask_exp = np.zeros((128, 2, 4, 128), f)
    for cc in range(4):
        mask_exp[:, 0, cc, cc * 32:(cc + 1) * 32] = 1.0
        mask_exp[:, 1, 3 - cc, cc * 32:(cc + 1) * 32] = 1.0
    scanmask = np.ones((128, 512), f); scanmask[:, 0::32] = 0.0
    il = np.arange(384)[None, :]
    band = (np.abs(il - 128 - jj) <= 128).astype(f)
    return {
        "x": x, "ctx": ctx, "cT": np.ascontiguousarray(cT), "ada_w": np.asarray(inputs["ada_w"], f),
        "ada_bT": np.ascontiguousarray(ada_bT), "ngT": np.ascontiguousarray(ngT),
        "rec_w_in": np.asarray(inputs["rec_w_in"], f)[0], "rec_w_out": np.asarray(inputs["rec_w_out"], f)[0],
        "lbT": np.ascontiguousarray(lbT), "wg2p": wg2p, "bg2T": np.ascontiguousarray(bg2T), "gnT": np.ascontiguousarray(gnT),
        "att_w_qkv": wqkv, "att_w_qk_sw": np.ascontiguousarray(qk), "att_w_o": np.asarray(inputs["att_w_o"], f)[0],
        "sinkB": np.ascontiguousarray(sinkB), "cosT": np.ascontiguousarray(cosT), "sinT": sinT,
        "ffn_w_in": np.asarray(inputs["ffn_w_in"], f), "ffn_w_out": np.asarray(inputs["ffn_w_out"], f),
        "ident": np.eye(128, dtype=f), "mask_intra": mask_intra, "mask_exp": mask_exp, "scanmask": scanmask, "band": band,
    }


def kernel(**inputs):
    NS = 2
    nc = build(NS)
    in_maps = [host_prep(inputs, core, NS) for core in range(8)]
    res = run_bass_kernel_spmd(nc, in_maps, core_ids=list(range(8)))
    return np.concatenate([r["out"] for r in res.results], axis=0).astype(np.float32)
```

```python
from contextlib import ExitStack
import numpy as np
import concourse.bass as bass
import concourse.mybir as mybir
from concourse.bass_utils import run_bass_kernel_spmd

F32 = mybir.dt.float32
BF16 = mybir.dt.bfloat16
AF = mybir.ActivationFunctionType
ALU = mybir.AluOpType

D = 1024
T = 2304
NT = 18
TL = 2048
FH = 2816
EPS = 1e-6


class Buf:
    __slots__ = ("w", "rs", "excl")

    def __init__(self, excl=False):
        self.w = None
        self.rs = {}
        self.excl = excl


def bufs(n, excl=False):
    return [Buf(excl) for _ in range(n)]


class Chan:
    __slots__ = ("sem", "cnt")

    def __init__(self, sem):
        self.sem = sem
        self.cnt = 0


class Sched:
    ENGS = ("pe", "act", "dve", "pool", "sp")

    def __init__(self, nc, stack):
        self.nc = nc
        self.stack = stack
        self.eng = {"pe": nc.tensor, "act": nc.scalar, "dve": nc.vector, "pool": nc.gpsimd, "sp": nc.sync}
        self.esem = {e: stack.enter_context(nc.semaphore("es_" + e)) for e in self.ENGS}
        self.ecnt = {e: 0 for e in self.ENGS}
        self.seen = {e: {} for e in self.ENGS}
        self.chans = []

    def chan(self):
        c = Chan(self.stack.enter_context(self.nc.semaphore("ch%d" % len(self.chans))))
        self.chans.append(c)
        return c

    def _deps(self, eng, reads, writes, skip_sem=None):
        deps = {}
        for b in reads:
            if b.w is not None:
                s, v = b.w
                if v > deps.get(s, 0):
                    deps[s] = v
        for b in writes:
            if b.w is not None:
                s, v = b.w
                if v > deps.get(s, 0):
                    deps[s] = v
            for s, v in b.rs.items():
                if v > deps.get(s, 0):
                    deps[s] = v
        own = self.esem[eng]
        seen = self.seen[eng]
        e = self.eng[eng]
        for s, v in deps.items():
            if s is skip_sem:
                continue
            if s is own and eng == "pe":
                continue
            if seen.get(s, 0) >= v:
                continue
            seen[s] = v
            e.wait_ge(s, v)

    def _mark(self, tok, reads, writes):
        s, v = tok
        for b in writes:
            b.w = tok
            b.rs = {}
        for b in reads:
            if b.rs.get(s, 0) < v:
                b.rs[s] = v

    def op(self, eng, fn, reads=(), writes=()):
        if any(b.excl for b in reads):
            writes = list(writes) + [b for b in reads if b.excl]
            reads = [b for b in reads if not b.excl]
        self._deps(eng, reads, writes)
        self.ecnt[eng] += 1
        tok = (self.esem[eng], self.ecnt[eng])
        fn(self.eng[eng]).then_inc(self.esem[eng], 1)
        self._mark(tok, reads, writes)

    def dma(self, eng, chan, out, in_, reads=(), writes=()):
        self._deps(eng, reads, writes, skip_sem=chan.sem)
        chan.cnt += 1
        tok = (chan.sem, 16 * chan.cnt)
        self.eng[eng].dma_start(out=out, in_=in_).then_inc(chan.sem, 16)
        self._mark(tok, reads, writes)

    def barrier(self):
        toks = [(self.esem[f], self.ecnt[f]) for f in self.ENGS if self.ecnt[f] > 0]
        toks += [(c.sem, 16 * c.cnt) for c in self.chans if c.cnt > 0]
        for e in self.ENGS:
            seen = self.seen[e]
            for s, v in toks:
                if s is self.esem[e]:
                    continue
                if seen.get(s, 0) >= v:
                    continue
                seen[s] = v
                self.eng[e].wait_ge(s, v)


class StopBuild(Exception):
    pass


def build(NS=2, dbg=None, stop_after=None):
    import os
    CUT = os.environ.get('CUT', '')

    def cut(name):
        if CUT == name:
            raise StopBuild()
    nc = bass.Bass("TRN2", target_bir_lowering=False)

    def din(name, shape, dt=F32):
        return nc.dram_tensor(name, list(shape), dt, kind="ExternalInput").ap()

    x_d = din("x", [NS, TL, D])
    ctx_d = din("ctx", [NS, 256, D])
    cT_d = din("cT", [128, 8, 3])
    adaw_d = din("ada_w", [2, D, 6 * D])
    adab_d = din("ada_bT", [128, 2, 48])
    ng_d = din("ngT", [128, 2, 4, 8])
    rwin_d = din("rec_w_in", [D, 4128])
    rwout_d = din("rec_w_out", [D, D])
    lb_d = din("lbT", [128, 2, 2, 4])
    wg2_d = din("wg2p", [32, 2, 256])
    bg2_d = din("bg2T", [64, 2, 4])
    gn_d = din("gnT", [128, 2])
    wqkv_d = din("att_w_qkv", [D, 1536])
    wqks_d = din("att_w_qk_sw", [D, 1280])
    wo_d = din("att_w_o", [D, D])
    sink_d = din("sinkB", [128, 16])
    cos_d = din("cosT", [64, TL])
    sin_d = din("sinT", [64, TL])
    fwin_d = din("ffn_w_in", [2, D, 2 * FH])
    fwout_d = din("ffn_w_out", [2, FH, D])
    ident_d = din("ident", [128, 128])
    mintra_d = din("mask_intra", [128, 2, 128])
    mexp_d = din("mask_exp", [128, 2, 4, 128])
    smask_d = din("scanmask", [128, 512])
    band_d = din("band", [128, 384])
    out_d = nc.dram_tensor("out", [NS, TL, D], F32, kind="ExternalOutput").ap()
    dbg_d = None
    if dbg:
        dbg_d = nc.dram_tensor("dbg", [128, 8, T], F32, kind="ExternalOutput").ap()

    with ExitStack() as st:
        S = Sched(nc, st)
        AW = 52000
        big = st.enter_context(nc.sbuf_tensor("big", [128, AW], F32))
        pb = [st.enter_context(nc.psum_tensor("pb%d" % i, [128, 512], F32)) for i in range(6)]
        pq = [st.enter_context(nc.psum_tensor("pq%d" % i, [128, 1024], BF16)) for i in range(2)]
        b_pb = bufs(6, True)
        b_pq = bufs(2, True)

        def view(off, shape, dt, parts=128):
            n = 1
            for s_ in shape:
                n *= s_
            esz = 4 if dt is F32 else 2
            nb = n * esz
            assert off % 4 == 0 and nb % 4 == 0
            assert off + nb <= AW * 4, (off, nb)
            a = big[0:parts, off // 4:(off + nb) // 4]
            if dt is BF16:
                a = a.bitcast(BF16)
            if len(shape) == 2:
                a = a.rearrange("p (a b) -> p a b", b=shape[1])
            elif len(shape) == 3:
                a = a.rearrange("p (a b c) -> p a b c", b=shape[1], c=shape[2])
            return a

        class Bump:
            def __init__(self, lo, hi):
                self.lo, self.hi, self.p = lo, hi, lo

            def alloc(self, shape, dt, parts=128):
                n = 1
                for s_ in shape:
                    n *= s_
                nb = ((n * (4 if dt is F32 else 2)) + 3) // 4 * 4
                off = self.p
                self.p += nb
                assert self.p <= self.hi, ("region overflow", self.lo, self.hi, self.p)
                return view(off, shape, dt, parts)

            def reset(self):
                self.p = self.lo

        KB = 1024
        RC = Bump(0, 11 * KB)
        R0 = Bump(11 * KB, 47 * KB)
        R1 = Bump(47 * KB, 119 * KB)
        R2 = Bump(119 * KB, AW * 4)

        def ACT(out, in_, func, r, w, **kw):
            S.op("act", lambda e: e.activation(out=out, in_=in_, func=func, **kw), r, w)

        def TT(eng, out, a, b, op, r, w):
            S.op(eng, lambda e: e.tensor_tensor(out=out, in0=a, in1=b, op=op), r, w)

        def TS(eng, out, a, s1, s2, op0, op1, r, w):
            S.op(eng, lambda e: e.tensor_scalar(out=out, in0=a, scalar1=s1, scalar2=s2, op0=op0, op1=op1), r, w)

        def STT(eng, out, a, s, b, op0, op1, r, w):
            S.op(eng, lambda e: e.scalar_tensor_tensor(out=out, in0=a, scalar=s, in1=b, op0=op0, op1=op1), r, w)

        def CP(eng, out, in_, r, w):
            if eng == "act":
                ACT(out, in_, AF.Copy, r, w)
            else:
                S.op(eng, lambda e: e.tensor_copy(out=out, in_=in_), r, w)

        def MM(out, lhsT, rhs, start, stop, r, w):
            S.op("pe", lambda e: e.matmul(out, lhsT=lhsT, rhs=rhs, start=start, stop=stop), r, w)

        def TR(out, in_, idn, r, w):
            S.op("pe", lambda e: e.transpose(out, in_, idn), r, w)

        def MS(eng, ap, val, w):
            S.op(eng, lambda e: e.memset(ap, val), (), w)

        cch = S.chan()
        ident = RC.alloc([128], F32); b_c = Buf()
        identb = RC.alloc([128], BF16)
        onesb = RC.alloc([128], BF16)
        mintra = RC.alloc([2, 128], F32)
        mexp = RC.alloc([2, 4, 128], BF16)
        mexp32 = R2.alloc([2, 4, 128], F32)
        smask = RC.alloc([512], F32)
        band = RC.alloc([384], BF16)
        band32 = R2.alloc([384], F32)
        epsc = RC.alloc([1], F32)
        ngT = RC.alloc([2, 4, 8], F32)
        adab = RC.alloc([2, 48], F32)
        lbl = RC.alloc([2, 2, 4], F32)
        lbv = RC.alloc([2, 4], F32)
        omlb = RC.alloc([2, 4], F32)
        wg2 = RC.alloc([2, 256], BF16, parts=32)
        wg2f = R2.alloc([2, 256], F32, parts=32)
        bg2 = RC.alloc([2, 4], F32, parts=64)
        gnv = RC.alloc([2], F32)
        sinkB = RC.alloc([16], F32)
        cT = RC.alloc([8, 3], F32)
        sT = RC.alloc([8, 3], F32)
        V = RC.alloc([2 * 3 * 6, 8], F32)
        modT = R2.alloc([2, 48, 3], F32)
        for dst, src in ((ident, ident_d), (mintra, mintra_d), (mexp32, mexp_d), (smask, smask_d), (band32, band_d),
                         (ngT, ng_d), (adab, adab_d), (lbl, lb_d), (gnv, gn_d), (sinkB, sink_d), (cT, cT_d)):
            S.dma("sp", cch, dst, src, (), [b_c])
        S.dma("sp", cch, wg2f, wg2_d, (), [b_c])
        S.dma("sp", cch, bg2, bg2_d, (), [b_c])
        CP("dve", identb, ident, [b_c], [b_c])
        CP("dve", mexp, mexp32, [b_c], [b_c])
        CP("dve", band, band32, [b_c], [b_c])
        CP("dve", wg2, wg2f, [b_c], [b_c])
        MS("dve", onesb, 1.0, [b_c])
        MS("dve", epsc, EPS, [b_c])
        TT("dve", lbv, lbl[:, 0], lbl[:, 1], ALU.subtract, [b_c], [b_c])
        ACT(lbv, lbv, AF.Sigmoid, [b_c], [b_c])
        TS("dve", omlb, lbv, -1.0, 1.0, ALU.mult, ALU.add, [b_c], [b_c])
        ACT(sT, cT, AF.Silu, [b_c], [b_c])

        wch = [S.chan(), S.chan()]
        awb = [R1.alloc([8, 512], F32), R1.alloc([8, 512], F32)]
        b_aw = bufs(2)
        b_mod = Buf()
        it = 0
        for l in range(2):
            awv = adaw_d[l].rearrange("(j p) n -> p j n", p=128)
            for g in range(12):
                sl = it % 2
                S.dma("sp", wch[sl], awb[sl], awv[:, :, g * 512:(g + 1) * 512], (), [b_aw[sl]])
                pbt = pb[it % 2]
                for mm in range(4):
                    for j in range(8):
                        MM(pbt[:, mm * 3:mm * 3 + 3], awb[sl][:, j, mm * 128:(mm + 1) * 128], sT[:, j, :],
                           j == 0, j == 7, [b_aw[sl], b_c], [b_pb[it % 2]])
                for mm in range(4):
                    m = g * 4 + mm
                    TS("dve", modT[:, l, m, :], pbt[:, mm * 3:mm * 3 + 3], adab[:, l, m:m + 1], 0.0, ALU.add, ALU.add,
                       [b_pb[it % 2], b_c], [b_mod])
                it += 1
        def Vv(l, col, kind):
            i = (l * 3 + col) * 6 + kind
            return V[:, i, :]
        for l in range(2):
            for col in range(3):
                def mk(kind):
                    return modT[:, l, kind * 8:(kind + 1) * 8, col]
                STT("dve", Vv(l, col, 0), mk(1), 1.0, ngT[:, l, 0, :], ALU.add, ALU.mult, [b_mod, b_c], [b_c])
                CP("dve", Vv(l, col, 1), mk(0), [b_mod], [b_c])
                TT("dve", Vv(l, col, 2), mk(2), ngT[:, l, 1, :], ALU.mult, [b_mod, b_c], [b_c])
                STT("dve", Vv(l, col, 3), mk(4), 1.0, ngT[:, l, 2, :], ALU.add, ALU.mult, [b_mod, b_c], [b_c])
                CP("dve", Vv(l, col, 4), mk(3), [b_mod], [b_c])
                TT("dve", Vv(l, col, 5), mk(5), ngT[:, l, 3, :], ALU.mult, [b_mod, b_c], [b_c])
        S.barrier()

        xch = [S.chan(), S.chan()]
        och = [S.chan(), S.chan()]
        dch = S.chan()
        wq = "pool"

        try:
          cut('p0')
          for s in range(NS):
              R0.reset(); R1.reset(); R2.reset()

              def colof(t):
                  return 2 if t < 2 else s

              def xsrc(t):
                  return ctx_d[s, t * 128:(t + 1) * 128, :] if t < 2 else x_d[s, (t - 2) * 128:(t - 1) * 128, :]

              def rstd_from_ss(ps_ap, n, scale, dst, r, w):
                  ACT(dst, ps_ap, AF.Ln, r + [b_c], w, scale=scale, bias=epsc[:, 0:1])
                  ACT(dst, dst, AF.Exp, w, w, scale=-0.5)

              l = 0
              y_st = R0.alloc([8, T], BF16); b_y = bufs(8)
              u_st = R1.alloc([8, T], BF16); b_u = bufs(NT)
              qd = [R1.alloc([T], BF16) for _ in range(2)]; b_qd = [bufs(NT), bufs(NT)]
              ki = [R1.alloc([T], BF16) for _ in range(2)]; b_ki = [bufs(NT), bufs(NT)]
              keT = [R1.alloc([NT, 128], BF16) for _ in range(2)]; b_ke = [bufs(NT), bufs(NT)]
              vT = R1.alloc([T], BF16); b_vT = bufs(NT)
              gate = R1.alloc([T], BF16); b_gate = bufs(NT)
              markR2 = R2.p
              xin = [R2.alloc([1024], F32) for _ in range(2)]; b_xin = bufs(2)
              xt32 = R2.alloc([8, 128], F32); b_xt32 = Buf()
              sqb = R2.alloc([8, 128], BF16); b_sqb = Buf()
              rs_t = R2.alloc([128], F32); b_rs = Buf()
              def load_xT(t, k):
                  sl = k % 2
                  S.dma("sp", xch[sl], xin[sl], xsrc(t), (), [b_xin[sl]])
                  for j in range(8):
                      TR(pb[j // 4][:, (j % 4) * 128:(j % 4 + 1) * 128], xin[sl][:, j * 128:(j + 1) * 128], ident,
                         [b_xin[sl], b_c], [b_pb[j // 4]])

              for t in range(NT):
                  load_xT(t, t)
                  cut('p1a')
                  col = colof(t)
                  for hh in range(2):
                      pv = pb[hh][:, :].rearrange("p (a b) -> p a b", b=128)
                      ACT(sqb[:, hh * 4:(hh + 1) * 4, :], pv, AF.Square, [b_pb[hh]], [b_sqb])
                      CP("dve", xt32[:, hh * 4:(hh + 1) * 4, :], pv, [b_pb[hh]], [b_xt32])
                  cut('p1b')
                  for j in range(8):
                      MM(pb[2][:, 0:128], onesb, sqb[:, j, :], j == 0, j == 7, [b_sqb, b_c], [b_pb[2]])
                  rstd_from_ss(pb[2][:, 0:128], 128, 1.0 / D, rs_t, [b_pb[2]], [b_rs])
                  cut('p1c')
                  TT("dve", xt32, xt32, rs_t.unsqueeze(1).to_broadcast([128, 8, 128]), ALU.mult, [b_xt32, b_rs], [b_xt32])
                  cut('p1d')
                  TT("pool", xt32, xt32, Vv(l, col, 0).unsqueeze(2).to_broadcast([128, 8, 128]), ALU.mult, [b_xt32, b_c], [b_xt32])
                  TT("pool", u_st[:, :, t * 128:(t + 1) * 128], xt32, Vv(l, col, 1).unsqueeze(2).to_broadcast([128, 8, 128]),
                     ALU.add, [b_xt32, b_c], [b_u[t]])

              S.barrier()
              R2.p = markR2
              wb = [R2.alloc([8, 5, 128], BF16) for _ in range(2)]; b_wb = bufs(2)
              wlr = R2.alloc([8, 32], BF16); b_wlr = Buf()
              o_sb = R2.alloc([T], F32); b_o = bufs(NT)
              lrT = R2.alloc([T], BF16, parts=32); b_lr = bufs(NT)
              dtmp = [R2.alloc([16], F32) for _ in range(2)]; b_dt = bufs(2)
              nt_sq = R2.alloc([512], BF16); b_ntsq = Buf()
              nt_r = R2.alloc([512], F32); b_ntr = Buf()
              dcat = [R2.alloc([NT, 5], F32) for _ in range(2)]; b_dc = [bufs(NT), bufs(NT)]
              for d in range(2):
                  MS("pool", dcat[d], 0.0, b_dc[d])
              qt = [R2.alloc([T], BF16) for _ in range(2)]; b_qt = [bufs(NT), bufs(NT)]
              d4 = [R2.alloc([NT], F32) for _ in range(2)]; b_d4 = [bufs(NT), bufs(NT)]
              Dc = [R2.alloc([16], F32) for _ in range(2)]; b_Dc = bufs(2)
              markU = R2.p
              t_qs = R2.alloc([512], F32); b_tqs = Buf()
              t_s = [R2.alloc([512], F32) for _ in range(2)]; b_ts = bufs(2)
              t_g = [R2.alloc([512], F32) for _ in range(2)]; b_tg = bufs(2)
              t_e = [R2.alloc([512], F32) for _ in range(2)]; b_te = bufs(2)
              t_ki = [R2.alloc([512], F32) for _ in range(2)]; b_tki = bufs(2)
              t_ke = [R2.alloc([512], BF16) for _ in range(2)]; b_tke = bufs(2)
              R2.p = markU
              vxm = [R2.alloc([4, 128], BF16) for _ in range(2)]; b_vxm = bufs(2)
              Vx = [[R2.alloc([5, 128], BF16) for _ in range(3)] for _ in range(2)]; b_Vx = [bufs(3), bufs(3)]
              Am = [[R2.alloc([128], BF16) for _ in range(3)] for _ in range(2)]; b_Am = [bufs(3), bufs(3)]
              U32 = [[R2.alloc([4, 128], F32) for _ in range(2)] for _ in range(2)]; b_U32 = [bufs(2), bufs(2)]
              Lb = [[R2.alloc([3, 128], BF16) for _ in range(2)] for _ in range(2)]; b_Lb = [bufs(2), bufs(2)]
              S32 = [R2.alloc([128], F32) for _ in range(2)]; b_S32 = bufs(2)
              Sbf = [[R2.alloc([128], BF16) for _ in range(2)] for _ in range(2)]; b_Sbf = [bufs(2), bufs(2)]

              cut('p1')
              rwv = rwin_d.rearrange("(j p) n -> p j n", p=128)
              wc = [S.chan(), S.chan()]
              wlc = S.chan()
              S.dma(wq, wlc, wlr, rwv[:, :, 3584:3616], (), [b_wlr])

              def load_head_w(hi):
                  sl = hi % 2
                  if hi < 4:
                      for g in range(5):
                          S.dma(wq, wc[sl], wb[sl][:, :, g, :], rwv[:, :, g * 512 + hi * 128: g * 512 + (hi + 1) * 128], (), [b_wb[sl]])
                  else:
                      h = hi - 4
                      S.dma(wq, wc[sl], wb[sl][:, :, 0, 0:64], rwv[:, :, 2560 + h * 64:2560 + (h + 1) * 64], (), [b_wb[sl]])
                      S.dma(wq, wc[sl], wb[sl][:, :, 1, 0:64], rwv[:, :, 2816 + h * 64:2816 + (h + 1) * 64], (), [b_wb[sl]])
                      S.dma(wq, wc[sl], wb[sl][:, :, 3, :], rwv[:, :, 3072 + h * 128:3072 + (h + 1) * 128], (), [b_wb[sl]])
                      S.dma(wq, wc[sl], wb[sl][:, :, 4, :], rwv[:, :, 3616 + h * 128:3616 + (h + 1) * 128], (), [b_wb[sl]])

              blocks = [(i * 512, min(512, T - i * 512)) for i in range(5)]
              order = [list(range(NT)), [1, 0] + list(range(NT - 1, 1, -1))]
              load_head_w(0)
              for hi in range(8):
                  isA = hi < 4
                  h = hi if isA else hi - 4
                  K = 128 if isA else 64
                  sc = 1.0 if isA else 1.0 / 16.0
                  qscale = (128.0 ** -0.5) if isA else (64.0 ** -0.5)
                  sl = hi % 2
                  if hi + 1 < 8:
                      load_head_w(hi + 1)
                  w = wb[sl]
                  for (c0, n) in blocks:
                      ta, tb = c0 // 128, (c0 + n) // 128
                      nch = n // 32
                      tl = list(range(ta, tb))
                      ub = [b_u[t] for t in tl]

                      def proj(g, M, pbi):
                          for j in range(8):
                              MM(pb[pbi][0:M, 0:n], w[:, j, g, 0:M], u_st[:, j, c0:c0 + n], j == 0, j == 7,
                                 ub + [b_wb[sl]], [b_pb[pbi]])
                      if isA:
                          proj(0, 128, 0); proj(1, 128, 1); proj(2, 128, 2); proj(3, 128, 3); proj(4, 128, 4)
                          ACT(t_qs[:, 0:n], pb[0][:, 0:n], AF.Silu, [b_pb[0]], [b_tqs])
                          qsrc = t_qs; qb = [b_tqs]
                          ksrc = []; kb = []
                          ACT(gate[:, c0:c0 + n], pb[4][:, 0:n], AF.Silu, [b_pb[4]], [b_gate[t] for t in tl])
                          for d in range(2):
                              ACT(t_s[d][:, 0:n], pb[1 + d][:, 0:n], AF.Sigmoid, [b_pb[1 + d]], [b_ts[d]])
                              TS("dve", t_s[d][:, 0:n], t_s[d][:, 0:n], omlb[:, d, h:h + 1], lbv[:, d, h:h + 1], ALU.mult, ALU.add,
                                 [b_ts[d], b_c], [b_ts[d]])
                          for d in range(2):
                              ACT(t_g[d][:, 0:n], t_s[d][:, 0:n], AF.Ln, [b_ts[d]], [b_tg[d]])
                              TS("pool", t_s[d][:, 0:n], t_s[d][:, 0:n], -1.0, 1.0, ALU.mult, ALU.add, [b_ts[d]], [b_ts[d]])
                              ksrc.append(t_s[d]); kb.append([b_ts[d]])
                      else:
                          proj(0, 64, 0); proj(1, 64, 1); proj(3, 128, 3); proj(4, 128, 4)
                          if h == 0:
                              for j in range(8):
                                  MM(pb[5][0:32, 0:n], wlr[:, j, :], u_st[:, j, c0:c0 + n], j == 0, j == 7, ub + [b_wlr], [b_pb[5]])
                              CP("act", lrT[:, c0:c0 + n], pb[5][0:32, 0:n], [b_pb[5]], [b_lr[t] for t in tl])
                          qsrc = pb[0]; qb = [b_pb[0]]
                          ksrc = [pb[1], pb[1]]; kb = [[b_pb[1]], [b_pb[1]]]
                          ACT(gate[:, c0:c0 + n], pb[4][:, 0:n], AF.Silu, [b_pb[4]], [b_gate[t] for t in tl])
                          for d in range(2):
                              MM(pb[5][0:64, 0:n], wg2[:, d, h * 64:(h + 1) * 64], lrT[:, c0:c0 + n], True, True,
                                 [b_lr[t] for t in tl] + [b_c], [b_pb[5]])
                              ACT(t_g[d][0:64, 0:n], pb[5][0:64, 0:n], AF.Sigmoid, [b_pb[5], b_c], [b_tg[d]], bias=bg2[:, d, h:h + 1])
                          for d in range(2):
                              ACT(t_g[d][0:64, 0:n], t_g[d][0:64, 0:n], AF.Ln, [b_tg[d]], [b_tg[d]])
                      CP("act", vT[:, c0:c0 + n], pb[3][:, 0:n], [b_pb[3]], [b_vT[t] for t in tl])
                      def dir_ops(d):
                          g_ = t_g[d][0:K, 0:n]
                          gv = g_.rearrange("p (a b) -> p a b", b=32)
                          S.op("dve", lambda e, g_=g_, d=d: e.tensor_tensor_scan(out=t_e[d][0:K, 0:n], data0=smask[0:K, 0:n], data1=g_,
                                                                            initial=0.0, op0=ALU.mult, op1=ALU.add),
                               [b_tg[d], b_c], [b_te[d]])
                          yield
                          pv = t_e[d][0:K, 0:n].rearrange("p (a b) -> p a b", b=32)
                          if d == 0:
                              CP("pool", g_, t_e[d][0:K, 0:n], [b_te[d]], [b_tg[d]])
                              yield
                              tot = gv[:, :, 31:32]
                          else:
                              TT("dve", g_, g_, t_e[d][0:K, 0:n], ALU.subtract, [b_tg[d], b_te[d]], [b_tg[d]])
                              yield
                              TT("dve", gv, gv, pv[:, :, 31:32].to_broadcast([K, nch, 32]), ALU.add, [b_tg[d], b_te[d]], [b_tg[d]])
                              yield
                              tot = gv[:, :, 0:1]
                          dd = dtmp[d][0:K, 0:nch]
                          ACT(dd.unsqueeze(2), tot, AF.Exp, [b_tg[d]], [b_dt[d]], scale=sc)
                          yield
                          ddv = dd.rearrange("p (t c) -> p t c", c=4)
                          if d == 0:
                              CP("pool", dcat[d][0:K, ta:tb, 1:5], ddv, [b_dt[d]], [b_dc[d][t] for t in tl])
                              yield
                          else:
                              for c in range(4):
                                  CP("pool", dcat[d][0:K, ta:tb, 4 - c], ddv[:, :, c], [b_dt[d]], [b_dc[d][t] for t in tl])
                                  yield
                          Dcv = Dc[d][0:K, 0:nch].rearrange("p (t c) -> p t c", c=4)
                          po_ = [0, 1, 2, 3] if d == 0 else [3, 2, 1, 0]
                          MS("dve", Dcv[:, :, po_[0]], 1.0, [b_Dc[d]])
                          yield
                          CP("dve", Dcv[:, :, po_[1]], ddv[:, :, po_[0]], [b_dt[d]], [b_Dc[d]])
                          yield
                          TT("dve", Dcv[:, :, po_[2]], Dcv[:, :, po_[1]], ddv[:, :, po_[1]], ALU.mult, [b_dt[d], b_Dc[d]], [b_Dc[d]])
                          yield
                          TT("dve", Dcv[:, :, po_[3]], Dcv[:, :, po_[2]], ddv[:, :, po_[2]], ALU.mult, [b_dt[d], b_Dc[d]], [b_Dc[d]])
                          yield
                          TT("dve", d4[d][0:K, ta:tb], Dcv[:, :, po_[3]], ddv[:, :, po_[3]], ALU.mult, [b_dt[d], b_Dc[d]], [b_d4[d][t] for t in tl])
                          yield
                          ACT(t_e[d][0:K, 0:n], g_, AF.Exp, [b_tg[d]], [b_te[d]], scale=sc)
                          yield
                          STT("dve", qd[d][0:K, c0:c0 + n], qsrc[0:K, 0:n], qscale, t_e[d][0:K, 0:n], ALU.mult, ALU.mult,
                              qb + [b_te[d]], [b_qd[d][t] for t in tl])
                          yield
                          TT("dve", t_ki[d][0:K, 0:n].rearrange("p (a b) -> p a b", b=32), t_e[d][0:K, 0:n].rearrange("p (a b) -> p a b", b=32),
                             Dc[d][0:K, 0:nch].unsqueeze(2).to_broadcast([K, nch, 32]), ALU.mult, [b_te[d], b_Dc[d]], [b_tki[d]])
                          yield
                          STT("dve", qt[d][0:K, c0:c0 + n], qsrc[0:K, 0:n], qscale, t_ki[d][0:K, 0:n], ALU.mult, ALU.mult,
                              qb + [b_tki[d]], [b_qt[d][t] for t in tl])
                          yield
                          ACT(t_e[d][0:K, 0:n], g_, AF.Exp, [b_tg[d]], [b_te[d]], scale=-sc)
                          yield
                          TT("dve", t_ki[d][0:K, 0:n], ksrc[d][0:K, 0:n], t_e[d][0:K, 0:n], ALU.mult, kb[d] + [b_te[d]], [b_tki[d]])
                          yield
                          CP("pool", ki[d][0:K, c0:c0 + n], t_ki[d][0:K, 0:n], [b_tki[d]], [b_ki[d][t] for t in tl])
                          yield
                          TT("pool", t_ke[d][0:K, 0:n].rearrange("p (a b) -> p a b", b=32),
                             t_ki[d][0:K, 0:n].rearrange("p (a b) -> p a b", b=32),
                             dd.unsqueeze(2).to_broadcast([K, nch, 32]), ALU.mult,
                             [b_tki[d], b_dt[d]], [b_tke[d]])
                          yield
                          for ti, t in enumerate(tl):
                              TR(pq[0][:, (d * 4 + ti) * 128:(d * 4 + ti) * 128 + K], t_ke[d][0:K, ti * 128:(ti + 1) * 128], identb[0:K, 0:K],
                                 [b_tke[d], b_c], [b_pq[0]])
                              yield
                          nt_ = len(tl)
                          CP("act", keT[d][:, ta:tb, 0:K],
                             pq[0][:, d * 512:d * 512 + nt_ * 128].rearrange("p (a b) -> p a b", b=128)[:, :, 0:K],
                             [b_pq[0]], [b_ke[d][t] for t in tl])
                          yield
                      gens_ = [dir_ops(0), dir_ops(1)]
                      while gens_:
                          for g__ in list(gens_):
                              try:
                                  next(g__)
                              except StopIteration:
                                  gens_.remove(g__)
                  cut('h%dp1' % hi)
                  S.barrier()
                  for d in range(2):
                      MS("dve", S32[d], 0.0, [b_S32[d]])
                      MS("dve", Sbf[d][0], 0.0, [b_Sbf[d][0]])
                  visited = set()

                  def prepA(k):
                      ts_ = [order[d][k] for d in range(2)]
                      b3 = k % 3
                      for d in range(2):
                          t = ts_[d]
                          TT("pool", vxm[d], mexp[:, d], vT[:, t * 128:(t + 1) * 128].unsqueeze(1).to_broadcast([128, 4, 128]), ALU.mult,
                             [b_vT[t], b_c], [b_vxm[d]])
                      for d in range(2):
                          t = ts_[d]
                          cs = slice(t * 128, (t + 1) * 128)
                          for c in range(4):
                              TR(pq[d][:, c * 128:(c + 1) * 128], vxm[d][:, c, :], identb, [b_vxm[d], b_c], [b_pq[d]])
                          TR(pq[d][:, 512:640], vT[:, cs], identb, [b_vT[t], b_c], [b_pq[d]])
                          MM(pb[2 + d][:, 0:128], ki[d][0:K, cs], qd[d][0:K, cs], True, True,
                             [b_ki[d][t], b_qd[d][t]], [b_pb[2 + d]])
                      for d in range(2):
                          CP("act", Vx[d][b3], pq[d][:, 0:640].rearrange("p (a b) -> p a b", b=128), [b_pq[d]], [b_Vx[d][b3]])
                          TT("dve", Am[d][b3], pb[2 + d][:, 0:128], mintra[:, d, :], ALU.mult, [b_pb[2 + d], b_c], [b_Am[d][b3]])

                  def prepB(k):
                      ts_ = [order[d][k] for d in range(2)]
                      b3 = k % 3
                      bf_ = k % 2
                      for d in range(2):
                          t = ts_[d]
                          MM(pb[d][0:K, :], keT[d][:, t, 0:K], Vx[d][b3][:, 0:4, :].rearrange("p a b -> p (a b)"), True, True,
                             [b_ke[d][t], b_Vx[d][b3]], [b_pb[d]])
                      for d in range(2):
                          CP("act", U32[d][bf_][0:K].rearrange("p a b -> p (a b)"), pb[d][0:K, :], [b_pb[d]], [b_U32[d][bf_]])
                      for s_ in (1, 2, 3):
                          for d in range(2):
                              t = ts_[d]
                              U_ = U32[d][bf_]
                              STT("dve", U_[0:K, s_, :], U_[0:K, s_ - 1, :], dcat[d][0:K, t, s_ + 1:s_ + 2], U_[0:K, s_, :], ALU.mult, ALU.add,
                                  [b_U32[d][bf_], b_dc[d][t]], [b_U32[d][bf_]])
                      for d in range(2):
                          CP("act", Lb[d][bf_][0:K].rearrange("p a b -> p (a b)"), U32[d][bf_][0:K, 0:3, :].rearrange("p a b -> p (a b)"),
                             [b_U32[d][bf_]], [b_Lb[d][bf_]])

                  def chain(k):
                      ts_ = [order[d][k] for d in range(2)]
                      b3 = k % 3
                      bf_ = k % 2
                      cur = k % 2
                      for d in range(2):
                          t = ts_[d]
                          STT("dve", S32[d][0:K], S32[d][0:K], d4[d][0:K, t:t + 1], U32[d][bf_][0:K, 3, :], ALU.mult, ALU.add,
                              [b_S32[d], b_d4[d][t], b_U32[d][bf_]], [b_S32[d]])
                      if k + 1 < NT:
                          for d in range(2):
                              CP("act", Sbf[d][1 - cur][0:K], S32[d][0:K], [b_S32[d]], [b_Sbf[d][1 - cur]])
                      for d in range(2):
                          t = ts_[d]
                          cs = slice(t * 128, (t + 1) * 128)
                          po = pb[4 + d][:, 0:128]
                          MM(po, Vx[d][b3][:, 4, :], Am[d][b3], True, False, [b_Vx[d][b3], b_Am[d][b3]], [b_pb[4 + d]])
                          for s_ in (1, 2, 3):
                              c = s_ if d == 0 else 3 - s_
                              MM(po[:, c * 32:(c + 1) * 32], Lb[d][bf_][0:K, s_ - 1, :], qd[d][0:K, t * 128 + c * 32:t * 128 + (c + 1) * 32],
                                 False, False, [b_Lb[d][bf_], b_qd[d][t]], [b_pb[4 + d]])
                          MM(po, Sbf[d][cur][0:K], qt[d][0:K, cs], False, True, [b_Sbf[d][cur], b_qt[d][t]], [b_pb[4 + d]])
                      for d in range(2):
                          t = ts_[d]
                          cs = slice(t * 128, (t + 1) * 128)
                          po = pb[4 + d][:, 0:128]
                          if t not in visited:
                              CP("dve", o_sb[:, cs], po, [b_pb[4 + d]], [b_o[t]])
                              visited.add(t)
                          else:
                              TT("dve", o_sb[:, cs], o_sb[:, cs], po, ALU.add, [b_o[t], b_pb[4 + d]], [b_o[t]])

                  prepA(0); prepA(1); prepB(0)
                  for k in range(NT):
                      if k + 2 < NT:
                          prepA(k + 2)
                      if k + 1 < NT:
                          prepB(k + 1)
                      chain(k)
                  cut('h%dp2' % hi)
                  S.barrier()
                  for (c0, n) in blocks:
                      tl = list(range(c0 // 128, (c0 + n) // 128))
                      ob = [b_o[t] for t in tl]
                      ACT(nt_sq[:, 0:n], o_sb[:, c0:c0 + n], AF.Square, ob, [b_ntsq])
                      MM(pb[4][:, 0:n], onesb, nt_sq[:, 0:n], True, True, [b_ntsq, b_c], [b_pb[4]])
                      rstd_from_ss(pb[4][:, 0:n], n, 1.0 / 128.0, nt_r[:, 0:n], [b_pb[4]], [b_ntr])
                      TT("dve", nt_r[:, 0:n], nt_r[:, 0:n], o_sb[:, c0:c0 + n], ALU.mult, [b_ntr] + ob, [b_ntr])
                      STT("dve", y_st[:, hi, c0:c0 + n], nt_r[:, 0:n], gnv[:, (0 if isA else 1):(1 if isA else 2)], gate[:, c0:c0 + n],
                          ALU.mult, ALU.mult, [b_ntr, b_c] + [b_gate[t] for t in tl], [b_y[hi]])

              cut('heads')
              S.barrier()
              R1.reset(); R2.reset()
              x_fm = R1.alloc([8, T], F32); b_x = bufs(NT)
              wo_sb = R2.alloc([8, 1024], BF16); b_wo = Buf()
              yo32 = R2.alloc([8, 128], F32); b_yo = Buf()
              sq2 = R2.alloc([8, 128], BF16); b_sq2 = Buf()
              rs2 = R2.alloc([128], F32); b_rs2 = Buf()
              xin = [R2.alloc([1024], F32) for _ in range(2)]; b_xin = bufs(2)
              wch2 = S.chan()
              S.dma(wq, wch2, wo_sb, rwout_d.rearrange("(j p) n -> p j n", p=128), (), [b_wo])
              for t in range(NT):
                  col = colof(t)
                  cs = slice(t * 128, (t + 1) * 128)
                  for f in range(8):
                      pbt = pb[2 + f % 2]
                      for j in range(8):
                          MM(pbt[:, 0:128], wo_sb[:, j, f * 128:(f + 1) * 128], y_st[:, j, cs], j == 0, j == 7,
                             [b_wo, b_y[j]], [b_pb[2 + f % 2]])
                      ACT(sq2[:, f, :], pbt[:, 0:128], AF.Square, [b_pb[2 + f % 2]], [b_sq2])
                      CP("dve", yo32[:, f, :], pbt[:, 0:128], [b_pb[2 + f % 2]], [b_yo])
                  for f in range(8):
                      MM(pb[4][:, 0:128], onesb, sq2[:, f, :], f == 0, f == 7, [b_sq2, b_c], [b_pb[4]])
                  rstd_from_ss(pb[4][:, 0:128], 128, 1.0 / D, rs2, [b_pb[4]], [b_rs2])
                  TT("dve", yo32, yo32, rs2.unsqueeze(1).to_broadcast([128, 8, 128]), ALU.mult, [b_yo, b_rs2], [b_yo])
                  TT("pool", yo32, yo32, Vv(l, col, 2).unsqueeze(2).to_broadcast([128, 8, 128]), ALU.mult, [b_yo, b_c], [b_yo])
                  sl = t % 2
                  S.dma("sp", xch[sl], xin[sl], xsrc(t), (), [b_xin[sl]])
                  for j in range(8):
                      TR(pb[j // 4][:, (j % 4) * 128:(j % 4 + 1) * 128], xin[sl][:, j * 128:(j + 1) * 128], ident,
                         [b_xin[sl], b_c], [b_pb[j // 4]])
                  for hh in range(2):
                      TT("dve", x_fm[:, hh * 4:(hh + 1) * 4, cs], yo32[:, hh * 4:(hh + 1) * 4, :],
                         pb[hh][:, :].rearrange("p (a b) -> p a b", b=128), ALU.add, [b_yo, b_pb[hh]], [b_x[t]])
              S.barrier()
              if dbg == "mix0":
                  S.dma("sp", dch, dbg_d, x_fm, [b for b in b_x], ())
                  S.barrier()
                  break

              def xb_of(c0, n):
                  return [b_x[t] for t in range(c0 // 128, (c0 + n + 127) // 128)]

              def prenorm_block(c0, n, vg, vs, dst, dstb, sqt, b_sqt, tmp, b_tmp, rs, b_rsb, pbi):
                  xs = x_fm[:, :, c0:c0 + n]
                  xb = xb_of(c0, n)
                  ACT(sqt[:, :, 0:n], xs, AF.Square, xb, [b_sqt])
                  for j in range(8):
                      MM(pb[pbi][:, 0:n], onesb, sqt[:, j, 0:n], j == 0, j == 7, [b_sqt, b_c], [b_pb[pbi]])
                  rstd_from_ss(pb[pbi][:, 0:n], n, 1.0 / D, rs[:, 0:n], [b_pb[pbi]], [b_rsb])
                  TT("dve", tmp[:, :, 0:n], xs, rs[:, 0:n].unsqueeze(1).to_broadcast([128, 8, n]), ALU.mult, xb + [b_rsb], [b_tmp])
                  TT("pool", tmp[:, :, 0:n], tmp[:, :, 0:n], vg.unsqueeze(2).to_broadcast([128, 8, n]), ALU.mult, [b_tmp, b_c], [b_tmp])
                  TT("pool", dst, tmp[:, :, 0:n], vs.unsqueeze(2).to_broadcast([128, 8, n]), ALU.add, [b_tmp, b_c], dstb)

              def post_block(c0, n, vgate, yo, b_yo_, sq, b_sq_, rs, b_rsb, pbi):
                  xb = xb_of(c0, n)
                  for f in range(8):
                      MM(pb[pbi][:, 0:n], onesb, sq[:, f, 0:n], f == 0, f == 7, [b_sq_, b_c], [b_pb[pbi]])
                  rstd_from_ss(pb[pbi][:, 0:n], n, 1.0 / D, rs[:, 0:n], [b_pb[pbi]], [b_rsb])
                  TT("dve", yo[:, :, 0:n], yo[:, :, 0:n], rs[:, 0:n].unsqueeze(1).to_broadcast([128, 8, n]), ALU.mult, [b_yo_, b_rsb], [b_yo_])
                  TT("pool", yo[:, :, 0:n], yo[:, :, 0:n], vgate.unsqueeze(2).to_broadcast([128, 8, n]), ALU.mult, [b_yo_, b_c], [b_yo_])
                  TT("dve", x_fm[:, :, c0:c0 + n], x_fm[:, :, c0:c0 + n], yo[:, :, 0:n], ALU.add, xb + [b_yo_], xb)

              def ffn(l, groups):
                  R0.reset(); R2.reset()
                  GT = 768
                  u2 = R0.alloc([8, GT], BF16); b_u2 = Buf()
                  yo = R0.alloc([8, GT], F32); b_yo_ = Buf()
                  hh_ = R2.alloc([22, GT], BF16); b_h = bufs(22)
                  sq = R2.alloc([8, GT], BF16); b_sq_ = Buf()
                  wi = [R2.alloc([8, 256], BF16) for _ in range(2)]; b_wi = bufs(2)
                  wo2 = [R2.alloc([22, 128], BF16) for _ in range(2)]; b_wo2 = bufs(2)
                  sg = [R2.alloc([512], F32) for _ in range(2)]; b_sg = bufs(2)
                  rs = R2.alloc([512], F32); b_rsb = Buf()
                  wic = [S.chan(), S.chan()]
                  woc = [S.chan(), S.chan()]
                  fwv = fwin_d[l].rearrange("(j p) n -> p j n", p=128)
                  fov = fwout_d[l].rearrange("(c p) n -> p c n", p=128)
                  for grp in groups:
                      offs = []
                      o_ = 0
                      for (c0, n, col) in grp:
                          offs.append(o_)
                          o_ += n
                      for (c0, n, col), off in zip(grp, offs):
                          prenorm_block(c0, n, Vv(l, col, 3), Vv(l, col, 4), u2[:, :, off:off + n], [b_u2],
                                        sq, b_sq_, yo, b_yo_, rs, b_rsb, 4)
                      cut('f_pre')
                      k = 0
                      for c in range(22):
                          if c == 1:
                              cut('f_h0')
                          sl = c % 2
                          S.dma(wq, wic[sl], wi[sl][:, :, 0:128], fwv[:, :, c * 128:(c + 1) * 128], (), [b_wi[sl]])
                          S.dma(wq, wic[sl], wi[sl][:, :, 128:256], fwv[:, :, FH + c * 128:FH + (c + 1) * 128], (), [b_wi[sl]])
                          for (c0, n, col), off in zip(grp, offs):
                              pa, pu = (0, 1) if k % 2 == 0 else (2, 3)
                              for j in range(8):
                                  MM(pb[pa][:, 0:n], wi[sl][:, j, 0:128], u2[:, j, off:off + n], j == 0, j == 7, [b_wi[sl], b_u2], [b_pb[pa]])
                              for j in range(8):
                                  MM(pb[pu][:, 0:n], wi[sl][:, j, 128:256], u2[:, j, off:off + n], j == 0, j == 7, [b_wi[sl], b_u2], [b_pb[pu]])
                              ACT(sg[k % 2][:, 0:n], pb[pa][:, 0:n], AF.Silu, [b_pb[pa]], [b_sg[k % 2]])
                              TT("dve", hh_[:, c, off:off + n], sg[k % 2][:, 0:n], pb[pu][:, 0:n], ALU.mult, [b_sg[k % 2], b_pb[pu]], [b_h[c]])
                              k += 1
                      cut('f_hid')
                      k = 0
                      for f in range(8):
                          if f == 1:
                              cut('f_o0')
                          sl = f % 2
                          S.dma(wq, woc[sl], wo2[sl], fov[:, :, f * 128:(f + 1) * 128], (), [b_wo2[sl]])
                          for (c0, n, col), off in zip(grp, offs):
                              pi = k % 2
                              for c in range(22):
                                  MM(pb[pi][:, 0:n], wo2[sl][:, c, :], hh_[:, c, off:off + n], c == 0, c == 21, [b_wo2[sl], b_h[c]], [b_pb[pi]])
                              ACT(sq[:, f, off:off + n], pb[pi][:, 0:n], AF.Square, [b_pb[pi]], [b_sq_])
                              CP("dve", yo[:, f, off:off + n], pb[pi][:, 0:n], [b_pb[pi]], [b_yo_])
                              k += 1
                      cut('f_out')
                      for (c0, n, col), off in zip(grp, offs):
                          post_block(c0, n, Vv(l, col, 5), yo[:, :, off:off + n], b_yo_, sq[:, :, off:off + n], b_sq_, rs, b_rsb, 4)
                          cut('f_post')
                  cut('f_all')
                  S.barrier()

              ffn(0, [[(0, 256, 2), (256, 512, s)], [(768, 512, s), (1280, 256, s)], [(1536, 512, s), (2048, 256, s)]])
              if dbg == "ffn0":
                  S.dma("sp", dch, dbg_d, x_fm, [b for b in b_x], ())
                  S.barrier()
                  break

              l = 1
              R0.reset(); R2.reset()
              u_st = R0.alloc([8, T], BF16); b_u1 = Buf()
              y1 = R2.alloc([8, TL], BF16); b_y1 = bufs(8)
              markA = R2.p
              sqt = R2.alloc([8, 512], BF16); b_sqt = Buf()
              tmpn = R2.alloc([8, 512], F32); b_tmpn = Buf()
              rsn = R2.alloc([512], F32); b_rsn = Buf()
              for (c0, n, col) in [(0, 256, 2)] + [(256 + 512 * i, 512, s) for i in range(4)]:
                  prenorm_block(c0, n, Vv(l, col, 0), Vv(l, col, 1), u_st[:, :, c0:c0 + n], [b_u1], sqt, b_sqt, tmpn, b_tmpn, rsn, b_rsn, 4)
              S.barrier()
              R2.p = markA
              cosT = R2.alloc([TL], F32, parts=64); sinT = R2.alloc([TL], F32, parts=64); b_tab = Buf()
              tch = S.chan()
              S.dma("sp", tch, cosT, cos_d, (), [b_tab])
              S.dma("sp", tch, sinT, sin_d, (), [b_tab])
              kT = R2.alloc([T], BF16, parts=64); b_kT = Buf()
              vv = R2.alloc([NT, 128], BF16); b_vv = Buf()
              qTs = [R2.alloc([TL], BF16, parts=64) for _ in range(2)]; b_qTs = bufs(2)
              wk = R2.alloc([8, 2, 64], BF16); b_wk = Buf()
              wv2 = R2.alloc([8, 128], BF16); b_wv2 = Buf()
              wqb = [R2.alloc([8, 2, 64], BF16) for _ in range(2)]; b_wqb = bufs(2)
              ta1 = R2.alloc([512], F32, parts=64); b_ta1 = Buf()
              ta2 = R2.alloc([512], F32, parts=64); b_ta2 = Buf()
              sqa = R2.alloc([512], BF16, parts=64); b_sqa = Buf()
              Pc = [R2.alloc([512], BF16) for _ in range(4)]; b_Pc = bufs(4)
              pkc = [0]
              rden = R2.alloc([512], F32); b_rden = Buf()
              sms = [R2.alloc([8], F32) for _ in range(2)]; b_sms = bufs(2)
              kmx = R2.alloc([4], F32); b_kmx = Buf()
              sm_unused = None
              wkc = S.chan(); wvc = S.chan(); wqc = [S.chan(), S.chan()]
              wqv = wqkv_d.rearrange("(j p) n -> p j n", p=128)
              wsv = wqks_d.rearrange("(j p) n -> p j n", p=128)
              lat_blocks = [(256 + 512 * i, 512) for i in range(4)]

              def load_q_w(hq):
                  sl = hq % 2
                  S.dma(wq, wqc[sl], wqb[sl][:, :, 0, :], wqv[:, :, hq * 64:(hq + 1) * 64], (), [b_wqb[sl]])
                  S.dma(wq, wqc[sl], wqb[sl][:, :, 1, :], wsv[:, :, hq * 64:(hq + 1) * 64], (), [b_wqb[sl]])

              rp_cnt = [0]

              def rope_proj(wt, b_wt, c0, n, dstT, b_dst, scale):
                  ba = (rp_cnt[0] % 2) * 2
                  rp_cnt[0] += 1
                  for j in range(8):
                      MM(pb[ba][0:64, 0:n], wt[:, j, 0, :], u_st[:, j, c0:c0 + n], j == 0, j == 7, [b_wt, b_u1], [b_pb[ba]])
                  for j in range(8):
                      MM(pb[ba + 1][0:64, 0:n], wt[:, j, 1, :], u_st[:, j, c0:c0 + n], j == 0, j == 7, [b_wt, b_u1], [b_pb[ba + 1]])
                  lc = c0 - 256
                  TT("dve", ta1[:, 0:n], pb[ba][0:64, 0:n], cosT[:, lc:lc + n], ALU.mult, [b_pb[ba], b_tab], [b_ta1])
                  TT("dve", ta2[:, 0:n], pb[ba + 1][0:64, 0:n], sinT[:, lc:lc + n], ALU.mult, [b_pb[ba + 1], b_tab], [b_ta2])
                  TT("dve", ta1[:, 0:n], ta1[:, 0:n], ta2[:, 0:n], ALU.add, [b_ta1, b_ta2], [b_ta1])
                  ACT(dstT, ta1[:, 0:n], AF.Copy, [b_ta1], [b_dst], scale=scale)

              def sqmax(srcT, b_src, n, smt, b_smt, acc_col, first):
                  ACT(sqa[:, 0:n], srcT, AF.Square, [b_src], [b_sqa])
                  MM(pb[5][:, 0:n], onesb[0:64, :], sqa[:, 0:n], True, True, [b_sqa, b_c], [b_pb[5]])
                  if first:
                      S.op("dve", lambda e: e.reduce_max(out=smt[:, acc_col:acc_col + 1], in_=pb[5][:, 0:n], axis=mybir.AxisListType.X), [b_pb[5]], [b_smt])
                  else:
                      S.op("dve", lambda e: e.reduce_max(out=smt[:, 2:3], in_=pb[5][:, 0:n], axis=mybir.AxisListType.X), [b_pb[5]], [b_smt])
                      TT("dve", smt[:, acc_col:acc_col + 1], smt[:, acc_col:acc_col + 1], smt[:, 2:3], ALU.max, [b_smt], [b_smt])

              load_q_w(0)
              for g in range(4):
                  S.dma(wq, wkc, wk[:, :, 0, :], wqv[:, :, 1024 + g * 64:1024 + (g + 1) * 64], (), [b_wk])
                  S.dma(wq, wkc, wk[:, :, 1, :], wsv[:, :, 1024 + g * 64:1024 + (g + 1) * 64], (), [b_wk])
                  S.dma(wq, wvc, wv2[:, :, 0:64], wqv[:, :, 1280 + g * 64:1280 + (g + 1) * 64], (), [b_wv2])
                  S.dma(wq, wvc, wv2[:, :, 64:128], wqv[:, :, 1280 + g * 64:1280 + (g + 1) * 64], (), [b_wv2])
                  for j in range(8):
                      MM(pb[0][0:64, 0:256], wk[:, j, 0, :], u_st[:, j, 0:256], j == 0, j == 7, [b_wk, b_u1], [b_pb[0]])
                  CP("act", kT[:, 0:256], pb[0][0:64, 0:256], [b_pb[0]], [b_kT])
                  sqmax(kT[:, 0:256], b_kT, 256, kmx, b_kmx, 0, True)
                  for (c0, n) in lat_blocks:
                      rope_proj(wk, b_wk, c0, n, kT[:, c0:c0 + n], b_kT, 1.0)
                      sqmax(kT[:, c0:c0 + n], b_kT, n, kmx, b_kmx, 0, False)
                  for t in range(NT):
                      pi = 3 + t % 2
                      for j in range(8):
                          MM(pb[pi][:, 0:128], u_st[:, j, t * 128:(t + 1) * 128], wv2[:, j, :], j == 0, j == 7, [b_wv2, b_u1], [b_pb[pi]])
                      CP("act", vv[:, t, :], pb[pi][:, 0:128], [b_pb[pi]], [b_vv])
                  def prepQ(hq):
                      par = hq % 2
                      sl = hq % 2
                      if hq + 1 < 16:
                          load_q_w(hq + 1)
                      qT_ = qTs[par]; bq_ = b_qTs[par]; sm_ = sms[par]; bsm_ = b_sms[par]
                      prev = None
                      for bi, (c0, n) in enumerate(lat_blocks):
                          rope_proj(wqb[sl], b_wqb[sl], c0, n, qT_[:, c0 - 256:c0 - 256 + n], bq_, 0.125)
                          if prev is not None:
                              sqmax(qT_[:, prev[0] - 256:prev[0] - 256 + prev[1]], bq_, prev[1], sm_, bsm_, 1, prev[2])
                          prev = (c0, n, bi == 0)
                      sqmax(qT_[:, prev[0] - 256:prev[0] - 256 + prev[1]], bq_, prev[1], sm_, bsm_, 1, prev[2])
                      TT("dve", sm_[:, 3:4], kmx[:, 0:1], sm_[:, 1:2], ALU.mult, [bsm_, b_kmx], [bsm_])
                      ACT(sm_[:, 3:4], sm_[:, 3:4], AF.Ln, [bsm_], [bsm_])
                      ACT(sm_[:, 3:4], sm_[:, 3:4], AF.Exp, [bsm_], [bsm_], scale=0.5)
                      TT("dve", sm_[:, 3:4], sm_[:, 3:4], sinkB[:, hq:hq + 1], ALU.max, [bsm_, b_c], [bsm_])
                      TS("dve", sm_[:, 4:5], sm_[:, 3:4], -1.0, 0.0, ALU.mult, ALU.add, [bsm_], [bsm_])
                      ACT(sm_[:, 5:6], sinkB[:, hq:hq + 1], AF.Exp, [bsm_, b_c], [bsm_], bias=sm_[:, 4:5])

                  heads_ = list(range(4 * g, 4 * g + 4))
                  prepQ(heads_[0])
                  for hi_, hq in enumerate(heads_):
                      if hi_ + 1 < 4:
                          prepQ(heads_[hi_ + 1])
                      qT = qTs[hq % 2]; b_qT = b_qTs[hq % 2]; sm = sms[hq % 2]; b_sm = b_sms[hq % 2]
                      negM = sm[:, 4:5]
                      for Q in range(4):
                          n0 = Q * 4
                          qc = slice(Q * 512, (Q + 1) * 512)
                          jbs = [jb for jb in range(n0 - 1, n0 + 5) if 0 <= jb < 16]
                          tasks = [("c", cb) for cb in range(2)] + [("b", jb) for jb in jbs]
                          slots = {}

                          def emit_score(i):
                              kind, x_ = tasks[i]
                              bi = pkc[0] % 4
                              pkc[0] += 1
                              slots[i] = bi
                              ps_ = pb[bi]; bps = b_pb[bi]; P_ = Pc[bi]; bP = b_Pc[bi]
                              if kind == "c":
                                  MM(ps_[:, 0:512], kT[:, x_ * 128:(x_ + 1) * 128], qT[:, qc], True, True, [b_kT, b_qT], [bps])
                                  ACT(P_[:, 0:512], ps_[:, 0:512], AF.Exp, [bps, b_sm], [bP], bias=negM)
                              else:
                                  jb = x_
                                  qa = max(jb - 1, n0); qe = min(jb + 1, n0 + 3)
                                  w_ = (qe - qa + 1) * 128
                                  moff = (qa - (jb - 1)) * 128
                                  MM(ps_[:, 0:w_], kT[:, 256 + jb * 128:256 + (jb + 1) * 128], qT[:, qa * 128:(qe + 1) * 128], True, True, [b_kT, b_qT], [bps])
                                  ACT(P_[:, 0:w_], ps_[:, 0:w_], AF.Exp, [bps, b_sm], [bP], bias=negM)
                                  TT("dve", P_[:, 0:w_], P_[:, 0:w_], band[:, moff:moff + w_], ALU.mult, [bP, b_c], [bP])

                          def emit_pv(i):
                              kind, x_ = tasks[i]
                              bi = slots[i]
                              P_ = Pc[bi]; bP = b_Pc[bi]
                              if kind == "c":
                                  MM(pb[4][:, 0:512], vv[:, x_, :], P_[:, 0:512], i == 0, False, [b_vv, bP], [b_pb[4]])
                                  MM(pb[5][:, 0:512], onesb, P_[:, 0:512], i == 0, False, [b_c, bP], [b_pb[5]])
                              else:
                                  jb = x_
                                  qa = max(jb - 1, n0); qe = min(jb + 1, n0 + 3)
                                  last = i == len(tasks) - 1
                                  for nb in range(qa, qe + 1):
                                      oc = slice((nb - n0) * 128, (nb - n0 + 1) * 128)
                                      pc_ = slice((nb - qa) * 128, (nb - qa + 1) * 128)
                                      lst = last and nb == qe
                                      MM(pb[4][:, oc], vv[:, 2 + jb, :], P_[:, pc_], False, lst, [b_vv, bP], [b_pb[4]])
                                      MM(pb[5][:, oc], onesb, P_[:, pc_], False, lst, [b_c, bP], [b_pb[5]])

                          LA = 2
                          for i in range(len(tasks) + LA):
                              if i < len(tasks):
                                  emit_score(i)
                              if i - LA >= 0:
                                  emit_pv(i - LA)
                          ACT(rden[:, 0:512], pb[5][:, 0:512], AF.Ln, [b_pb[5], b_sm], [b_rden], bias=sm[:, 5:6])
                          ACT(rden[:, 0:512], rden[:, 0:512], AF.Exp, [b_rden], [b_rden], scale=-1.0)
                          hp = (hq % 2) * 64
                          TT("dve", y1[hp:hp + 64, hq // 2, qc], pb[4][hp:hp + 64, 0:512], rden[hp:hp + 64, 0:512], ALU.mult,
                             [b_pb[4], b_rden], [b_y1[hq // 2]])
              S.barrier()
              R0.reset(); R2.p = markA
              wo1 = R0.alloc([8, 1024], BF16); b_wo1 = Buf()
              yo1 = R2.alloc([8, 512], F32); b_yo1 = Buf()
              sq1 = R2.alloc([8, 512], BF16); b_sq1 = Buf()
              rs1 = R2.alloc([512], F32); b_rs1 = Buf()
              woc1 = S.chan()
              S.dma(wq, woc1, wo1, wo_d.rearrange("(j p) n -> p j n", p=128), (), [b_wo1])
              k = 0
              for (c0, n) in lat_blocks:
                  lc = c0 - 256
                  for f in range(8):
                      pi = k % 2; k += 1
                      for j in range(8):
                          MM(pb[pi][:, 0:n], wo1[:, j, f * 128:(f + 1) * 128], y1[:, j, lc:lc + n], j == 0, j == 7, [b_wo1, b_y1[j]], [b_pb[pi]])
                      ACT(sq1[:, f, 0:n], pb[pi][:, 0:n], AF.Square, [b_pb[pi]], [b_sq1])
                      CP("dve", yo1[:, f, 0:n], pb[pi][:, 0:n], [b_pb[pi]], [b_yo1])
                  post_block(c0, n, Vv(l, s, 2), yo1, b_yo1, sq1, b_sq1, rs1, b_rs1, 4)
              S.barrier()
              if dbg == "mix1":
                  S.dma("sp", dch, dbg_d, x_fm, [b for b in b_x], ())
                  S.barrier()
                  break
              ffn(1, [[(256, 512, s), (768, 256, s)], [(1024, 512, s), (1536, 256, s)], [(1792, 512, s)]])
              if dbg == "ffn1":
                  S.dma("sp", dch, dbg_d, x_fm, [b for b in b_x], ())
                  S.barrier()
                  break
              R0.reset(); R2.reset()
              ot = [R2.alloc([1024], F32) for _ in range(2)]; b_ot = bufs(2)
              for t in range(2, NT):
                  sl = t % 2
                  for j in range(8):
                      TR(pb[sl * 2 + j // 4][:, (j % 4) * 128:(j % 4 + 1) * 128], x_fm[:, j, t * 128:(t + 1) * 128], ident,
                         [b_x[t], b_c], [b_pb[sl * 2 + j // 4]])
                  CP("act", ot[sl][:, 0:512], pb[sl * 2][:, :], [b_pb[sl * 2]], [b_ot[sl]])
                  CP("dve", ot[sl][:, 512:1024], pb[sl * 2 + 1][:, :], [b_pb[sl * 2 + 1]], [b_ot[sl]])
                  S.dma("sp", och[sl], out_d[s, (t - 2) * 128:(t - 1) * 128, :], ot[sl], [b_ot[sl]], ())
              S.barrier()

        except StopBuild:
            pass
        S.barrier()
        for e in ("sp",):
            for c in S.chans:
                if c.cnt:
                    S.eng[e].wait_ge(c.sem, 16 * c.cnt)
    return nc


def host_prep(inputs, core, NS=2):
    f = np.float32
    b0 = core * NS
    x = np.ascontiguousarray(inputs["x"][b0:b0 + NS]).astype(f)
    ctx = np.ascontiguousarray(inputs["ctx"][b0:b0 + NS]).astype(f)
    c = inputs["c"][b0:b0 + NS]
    cols = [c[0], c[min(1, NS - 1)], inputs["c_ctx"]]
    cT = np.stack([np.asarray(v, f).reshape(8, 128).T for v in cols], axis=-1)
    ada_bT = np.asarray(inputs["ada_b"], f).reshape(2, 48, 128).transpose(2, 0, 1)
    ngT = np.asarray(inputs["norm_g"], f).reshape(2, 4, 8, 128).transpose(3, 0, 1, 2)
    lbT = np.asarray(inputs["rec_lb_logits"], f).reshape(2, 2, 4, 128).transpose(3, 0, 1, 2)
    wg2 = np.asarray(inputs["rec_w_g2"], f)[0]
    wg2p = np.zeros((32, 2, 256), f)
    wg2p[0:16, 0, :] = wg2[0]
    wg2p[16:32, 1, :] = wg2[1]
    bg2T = np.asarray(inputs["rec_b_g2"], f)[0].reshape(2, 4, 64).transpose(2, 0, 1)
    gnT = np.stack([np.asarray(inputs["rec_gn_a"], f)[0], np.asarray(inputs["rec_gn_b"], f)[0]], axis=-1)
    wqkv = np.asarray(inputs["att_w_qkv"], f)[0]
    qk = wqkv[:, :1280].reshape(1024, 640, 2)[:, :, ::-1].reshape(1024, 1280)
    sinkB = np.broadcast_to(np.asarray(inputs["att_sink"], f)[0][None, :], (128, 16))
    n_rows = TL // 64
    row = np.repeat(np.arange(n_rows), 64).astype(f)
    colp = np.tile(np.arange(64), n_rows).astype(f)
    inv = (np.float32(10000.0) ** (-np.arange(0, 32, 2, dtype=f) / np.float32(32))).astype(f)
    ang = np.concatenate([row[:, None] * inv, colp[:, None] * inv], axis=-1).astype(f)
    cosT = np.repeat(np.cos(ang).astype(f).T, 2, axis=0)
    sinv = np.sin(ang).astype(f).T
    sinT = np.empty((64, TL), f)
    sinT[0::2] = -sinv
    sinT[1::2] = sinv
    jj = np.arange(128)[:, None]; ii = np.arange(128)[None, :]
    same = (jj // 32) == (ii // 32)
    mask_intra = np.stack([(same & (jj <= ii)), (same & (jj >= ii))], axis=1).astype(f)
    mask_exp = np.zeros((128, 2, 4, 128), f)
    for cc in range(4):
        mask_exp[:, 0, cc, cc * 32:(cc + 1) * 32] = 1.0
        mask_exp[:, 1, 3 - cc, cc * 32:(cc + 1) * 32] = 1.0
    scanmask = np.ones((128, 512), f); scanmask[:, 0::32] = 0.0
    il = np.arange(384)[None, :]
    band = (np.abs(il - 128 - jj) <= 128).astype(f)
    return {
        "x": x, "ctx": ctx, "cT": np.ascontiguousarray(cT), "ada_w": np.asarray(inputs["ada_w"], f),
        "ada_bT": np.ascontiguousarray(ada_bT), "ngT": np.ascontiguousarray(ngT),
        "rec_w_in": np.asarray(inputs["rec_w_in"], f)[0], "rec_w_out": np.asarray(inputs["rec_w_out"], f)[0],
        "lbT": np.ascontiguousarray(lbT), "wg2p": wg2p, "bg2T": np.ascontiguousarray(bg2T), "gnT": np.ascontiguousarray(gnT),
        "att_w_qkv": wqkv, "att_w_qk_sw": np.ascontiguousarray(qk), "att_w_o": np.asarray(inputs["att_w_o"], f)[0],
        "sinkB": np.ascontiguousarray(sinkB), "cosT": np.ascontiguousarray(cosT), "sinT": sinT,
        "ffn_w_in": np.asarray(inputs["ffn_w_in"], f), "ffn_w_out": np.asarray(inputs["ffn_w_out"], f),
        "ident": np.eye(128, dtype=f), "mask_intra": mask_intra, "mask_exp": mask_exp, "scanmask": scanmask, "band": band,
    }


def kernel(**inputs):
    NS = 2
    nc = build(NS)
    in_maps = [host_prep(inputs, core, NS) for core in range(8)]
    res = run_bass_kernel_spmd(nc, in_maps, core_ids=list(range(8)))
    return np.concatenate([r["out"] for r in res.results], axis=0).astype(np.float32)
```
